# Optimizing a Trainium2 kernel written in Bass

```python
import jax, jax.numpy as jnp
from jax import lax
import numpy as np

D_MODEL = 2048
BATCH = 4
SEQ = 8192
DEPTH = 1
DEC_BATCH = 2
DEC_SEQ = 8192
PAST_LEN = 128

N_FOURIER_GROUPS = 4
FOURIER_GROUP_DIM = 256
FOURIER_DIM = N_FOURIER_GROUPS * FOURIER_GROUP_DIM
N_HEADS = 8
QK_NOPE_DIM = 128
QK_ROPE_DIM = 64
V_HEAD_DIM = 128
Q_LORA_RANK = 512
KV_LORA_RANK = 512
MLA_DIM = N_HEADS * V_HEAD_DIM
MIX_DIM = FOURIER_DIM + MLA_DIM
IN_DIM = FOURIER_DIM + Q_LORA_RANK + KV_LORA_RANK + QK_ROPE_DIM
ROPE_THETA = 10000.0
Q_BLOCK = 128
N_MEM = 256
N_CROSS_HEADS = 4
CROSS_HEAD_DIM = D_MODEL // N_CROSS_HEADS
D_FF = 5632
CONV_WIDTH = 3
EPS = 1e-6

kernel_name = "hybrid_fourier_mla_encoder"


def rms_norm(x, g):
    x32 = x.astype(jnp.float32)
    y = x32 * lax.rsqrt(jnp.mean(x32 * x32, axis=-1, keepdims=True) + EPS)
    return (y * g.astype(jnp.float32)).astype(x.dtype)


def rope_tables(seq):
    inv = ROPE_THETA ** (-jnp.arange(0, QK_ROPE_DIM, 2, dtype=jnp.float32) / QK_ROPE_DIM)
    ang = jnp.arange(seq, dtype=jnp.float32)[:, None] * inv[None, :]
    return jnp.cos(ang), jnp.sin(ang)


def apply_rope(x, cos, sin):
    half = QK_ROPE_DIM // 2
    x1, x2 = x[..., :half], x[..., half:]
    c = cos.astype(x.dtype)
    s = sin.astype(x.dtype)
    return jnp.concatenate([x1 * c - x2 * s, x1 * s + x2 * c], axis=-1)


def fourier_mix(u):
    b, s, _ = u.shape
    ug = u.reshape(b, s, N_FOURIER_GROUPS, FOURIER_GROUP_DIM).astype(jnp.float32)
    f = jnp.fft.fft2(ug, axes=(1, 3), norm="ortho").real
    return f.reshape(b, s, FOURIER_DIM).astype(u.dtype)


def mla(c_q, c_kv, k_rope, q_norm_g, w_uq, kv_norm_g, w_ukv):
    b, s, _ = c_q.shape
    q = (rms_norm(c_q, q_norm_g) @ w_uq).reshape(b, s, N_HEADS, QK_NOPE_DIM + QK_ROPE_DIM)
    q_nope, q_rope = q[..., :QK_NOPE_DIM], q[..., QK_NOPE_DIM:]
    kv = (rms_norm(c_kv, kv_norm_g) @ w_ukv).reshape(b, s, N_HEADS, QK_NOPE_DIM + V_HEAD_DIM)
    k_nope, v = kv[..., :QK_NOPE_DIM], kv[..., QK_NOPE_DIM:]
    cos, sin = rope_tables(s)
    q_rope = apply_rope(q_rope, cos[:, None, :], sin[:, None, :])
    k_rope = apply_rope(k_rope, cos, sin)
    scale = (QK_NOPE_DIM + QK_ROPE_DIM) ** -0.5
    nb = s // Q_BLOCK
    qn_b = q_nope.reshape(b, nb, Q_BLOCK, N_HEADS, QK_NOPE_DIM).transpose(1, 0, 2, 3, 4)
    qr_b = q_rope.reshape(b, nb, Q_BLOCK, N_HEADS, QK_ROPE_DIM).transpose(1, 0, 2, 3, 4)

    def block(args):
        qn, qr = args
        sc = (jnp.einsum('bqhd,bkhd->bhqk', qn, k_nope)
              + jnp.einsum('bqhr,bkr->bhqk', qr, k_rope))
        p = jax.nn.softmax(sc.astype(jnp.float32) * scale, axis=-1).astype(v.dtype)
        return jnp.einsum('bhqk,bkhd->bqhd', p, v)

    o = lax.map(block, (qn_b, qr_b))
    return o.transpose(1, 0, 2, 3, 4).reshape(b, s, MLA_DIM)


def cross_attention(h, mem_n, w_cq, w_ckv, w_co):
    b, s, _ = h.shape
    m = mem_n.shape[1]
    q = (h @ w_cq).reshape(b, s, N_CROSS_HEADS, CROSS_HEAD_DIM)
    kv = (mem_n @ w_ckv).reshape(b, m, 2, N_CROSS_HEADS, CROSS_HEAD_DIM)
    k, v = kv[:, :, 0], kv[:, :, 1]
    sc = jnp.einsum('bqhd,bkhd->bhqk', q, k).astype(jnp.float32) * (CROSS_HEAD_DIM ** -0.5)
    p = jax.nn.softmax(sc, axis=-1).astype(v.dtype)
    o = jnp.einsum('bhqk,bkhd->bqhd', p, v).reshape(b, s, D_MODEL)
    return o @ w_co


def conv_ffn(h, w_gate, w_up, conv_w, conv_b, w_down):
    g = h @ w_gate
    u = h @ w_up
    gp = jnp.pad(g, ((0, 0), (1, 1), (0, 0)))
    g = gp[:, :-2] * conv_w[0] + gp[:, 1:-1] * conv_w[1] + gp[:, 2:] * conv_w[2] + conv_b
    return (jax.nn.silu(g) * u) @ w_down


def encoder_trunk(x, mem, norm_mix_g, w_in, q_norm_g, w_uq, kv_norm_g, w_ukv,
                  fourier_out_g, mla_out_g, w_out, norm_cross_g, norm_mem_g,
                  w_cq, w_ckv, w_co, norm_ffn_g, w_gate, w_up, conv_w, conv_b,
                  w_down, final_norm_g):
    c1 = FOURIER_DIM
    c2 = c1 + Q_LORA_RANK
    c3 = c2 + KV_LORA_RANK
    for l in range(DEPTH):
        h = rms_norm(x, norm_mix_g[l])
        z = h @ w_in[l]
        f_in, c_q, c_kv, k_r = z[..., :c1], z[..., c1:c2], z[..., c2:c3], z[..., c3:]
        f_out = rms_norm(fourier_mix(f_in), fourier_out_g[l])
        a_out = rms_norm(mla(c_q, c_kv, k_r, q_norm_g[l], w_uq[l], kv_norm_g[l], w_ukv[l]),
                         mla_out_g[l])
        x = x + jnp.concatenate([f_out, a_out], axis=-1) @ w_out[l]
        x = x + cross_attention(rms_norm(x, norm_cross_g[l]), rms_norm(mem, norm_mem_g[l]),
                                w_cq[l], w_ckv[l], w_co[l])
        x = x + conv_ffn(rms_norm(x, norm_ffn_g[l]), w_gate[l], w_up[l], conv_w[l],
                         conv_b[l], w_down[l])
    return rms_norm(x, final_norm_g)


def setup_inputs(seed: int = 0) -> dict:
    key = jax.random.key(seed)
    ks = jax.random.split(key, 32)
    f32 = jnp.float32

    def w(k, shape, fan_in):
        return jax.random.normal(k, shape, f32) * (fan_in ** -0.5)

    def gain(k, shape):
        return 1.0 + 0.02 * jax.random.normal(k, shape, f32)

    L = DEPTH
    return {
        "x_prompt": jax.random.normal(ks[0], (BATCH, SEQ, D_MODEL), f32),
        "x_sample": jax.random.normal(ks[1], (DEC_BATCH, DEC_SEQ, D_MODEL), f32),
        "mem_prompt": jax.random.normal(ks[2], (BATCH, N_MEM, D_MODEL), f32),
        "mem_sample": jax.random.normal(ks[3], (DEC_BATCH, N_MEM, D_MODEL), f32),
        "norm_mix_g": gain(ks[4], (L, D_MODEL)),
        "w_in": w(ks[5], (L, D_MODEL, IN_DIM), D_MODEL),
        "q_norm_g": gain(ks[6], (L, Q_LORA_RANK)),
        "w_uq": w(ks[7], (L, Q_LORA_RANK, N_HEADS * (QK_NOPE_DIM + QK_ROPE_DIM)), Q_LORA_RANK),
        "kv_norm_g": gain(ks[8], (L, KV_LORA_RANK)),
        "w_ukv": w(ks[9], (L, KV_LORA_RANK, N_HEADS * (QK_NOPE_DIM + V_HEAD_DIM)), KV_LORA_RANK),
        "fourier_out_g": gain(ks[10], (L, FOURIER_DIM)),
        "mla_out_g": gain(ks[11], (L, MLA_DIM)),
        "w_out": w(ks[12], (L, MIX_DIM, D_MODEL), MIX_DIM),
        "norm_cross_g": gain(ks[13], (L, D_MODEL)),
        "norm_mem_g": gain(ks[14], (L, D_MODEL)),
        "w_cq": w(ks[15], (L, D_MODEL, D_MODEL), D_MODEL),
        "w_ckv": w(ks[16], (L, D_MODEL, 2 * D_MODEL), D_MODEL),
        "w_co": w(ks[17], (L, D_MODEL, D_MODEL), D_MODEL),
        "norm_ffn_g": gain(ks[18], (L, D_MODEL)),
        "w_gate": w(ks[19], (L, D_MODEL, D_FF), D_MODEL),
        "w_up": w(ks[20], (L, D_MODEL, D_FF), D_MODEL),
        "conv_w": w(ks[21], (L, CONV_WIDTH, D_FF), CONV_WIDTH),
        "conv_b": 0.01 * jax.random.normal(ks[22], (L, D_FF), f32),
        "w_down": w(ks[23], (L, D_FF, D_MODEL), D_FF),
        "final_norm_g": gain(ks[24], (D_MODEL,)),
    }


def reference(x_prompt, x_sample, mem_prompt, mem_sample, norm_mix_g, w_in, q_norm_g, w_uq,
              kv_norm_g, w_ukv, fourier_out_g, mla_out_g, w_out, norm_cross_g, norm_mem_g,
              w_cq, w_ckv, w_co, norm_ffn_g, w_gate, w_up, conv_w, conv_b, w_down,
              final_norm_g):
    y_prompt = encoder_trunk(x_prompt, mem_prompt, norm_mix_g, w_in, q_norm_g, w_uq, kv_norm_g,
                             w_ukv, fourier_out_g, mla_out_g, w_out, norm_cross_g, norm_mem_g,
                             w_cq, w_ckv, w_co, norm_ffn_g, w_gate, w_up, conv_w, conv_b,
                             w_down, final_norm_g)
    y_sample = encoder_trunk(x_sample, mem_sample, norm_mix_g, w_in, q_norm_g, w_uq, kv_norm_g,
                             w_ukv, fourier_out_g, mla_out_g, w_out, norm_cross_g, norm_mem_g,
                             w_cq, w_ckv, w_co, norm_ffn_g, w_gate, w_up, conv_w, conv_b,
                             w_down, final_norm_g)
    return (y_prompt, y_sample)
```

```python
import math
from contextlib import ExitStack

import numpy as np
import concourse.bass as bass
import concourse.mybir as mybir
from concourse.bass_utils import run_bass_kernel_spmd

F32 = mybir.dt.float32
BF16 = mybir.dt.bfloat16
AF = mybir.ActivationFunctionType
ALU = mybir.AluOpType

D = 2048
NF = 1024
NG = 4
GC = 256
QL = 512
KVL = 512
NH = 8
DN = 128
DR = 64
DV = 128
NMEM = 256
NCH = 4
CHD = 512
DFF = 5632
FC = DFF // 128
EPS = 1e-6
KC = D // 128
TT = 512


class Sem:
    __slots__ = ("h", "total", "is_dma", "name")

    def __init__(self, h, is_dma, name):
        self.h = h
        self.total = 0
        self.is_dma = is_dma
        self.name = name


class Res:
    __slots__ = ("w", "r")

    def __init__(self):
        self.w = None
        self.r = {}


class Stream:
    def __init__(self, name):
        self.name = name
        self.sem = None
        self.ops = []
        self.known = {}


class Tracker:
    def __init__(self, nc, es):
        self.nc = nc
        self.es = es
        self.sems = []
        self.streams = {}
        for name in ("pe", "act", "dve", "pool", "sp"):
            st = Stream(name)
            if name != "sp":
                st.sem = self.new_sem("c_" + name, False)
            self.streams[name] = st
        self.n_ins = 0

    def new_sem(self, name, is_dma=True):
        h = self.es.enter_context(self.nc.semaphore(name))
        s = Sem(h, is_dma, name)
        self.sems.append(s)
        return s

    def _wait(self, st, ev):
        sem, val = ev
        if sem.is_dma:
            val = sem.total
        if st.known.get(sem, 0) >= val:
            return
        st.known[sem] = val
        st.ops.append(("w", sem.h, val))

    def op(self, stname, fn, reads=(), writes=(), dma=None):
        st = self.streams[stname]
        for r in reads:
            if r.w is not None:
                if not (stname == "pe" and r.w[0] is st.sem):
                    self._wait(st, r.w)
        for w in writes:
            if w.w is not None:
                if not (stname == "pe" and w.w[0] is st.sem):
                    self._wait(st, w.w)
            for ev in w.r.values():
                if not (stname == "pe" and ev[0] is st.sem):
                    self._wait(st, ev)
        if dma is not None:
            dma.total += 16
            ev = (dma, dma.total)
            st.ops.append(("i", fn, dma.h, 16))
        else:
            st.sem.total += 1
            ev = (st.sem, st.sem.total)
            st.ops.append(("i", fn, st.sem.h, 1))
        key = ev[0]
        for r in reads:
            r.r[key] = ev
        for w in writes:
            w.w = ev
            w.r = {}
        self.n_ins += 1

    def barrier(self):
        for st in self.streams.values():
            for sem in self.sems:
                if sem.total > 0:
                    self._wait(st, (sem, sem.total))

    def ins(self, stname, method, *args, reads=(), writes=(), dma=None, **kw):
        self.op(stname, (method, args, kw), reads, writes, dma)

    def replay(self, eng, stname):
        for o in self.streams[stname].ops:
            if o[0] == "w":
                eng.wait_ge(o[1], o[2])
            else:
                m, a, kw = o[1]
                getattr(eng, m)(*a, **kw).then_inc(o[2], o[3])


def host_constants(S):
    N1 = 128
    N2 = S // 128
    c = {}
    c["c_ident"] = np.eye(128, dtype=np.float32)
    j = np.arange(GC)
    ang = 2.0 * np.pi * np.outer(j, j) / GC
    c["c_cdft"] = (np.concatenate([np.cos(ang), -np.sin(ang)], axis=1) / math.sqrt(GC)).astype(np.float32)
    s1 = np.arange(N1)[:, None, None].astype(np.float64)
    s2 = np.arange(N2)[None, :, None].astype(np.float64)
    k1 = np.arange(N1)[None, None, :].astype(np.float64)
    th = 2.0 * np.pi * (s1 * k1 / N1 + s2 * k1 / S)
    far = np.cos(th) / math.sqrt(N1)
    fai = -np.sin(th) / math.sqrt(N1)
    c["c_fa"] = np.stack([far, fai, -fai], axis=2).astype(np.float32)
    s2b = np.arange(N2)[:, None].astype(np.float64)
    k2 = np.arange(N2)[None, :].astype(np.float64)
    ph = 2.0 * np.pi * s2b * k2 / N2
    c["c_fb"] = (np.concatenate([np.cos(ph), np.sin(ph)], axis=0) / math.sqrt(N2)).astype(np.float32)
    inv = 10000.0 ** (-np.arange(0, DR, 2, dtype=np.float32) / DR)
    a = np.arange(S, dtype=np.float32)[None, :] * inv[:, None].astype(np.float32)
    c["c_cos"] = np.concatenate([np.cos(a), np.cos(a)], axis=0).astype(np.float32)
    c["c_sin"] = np.concatenate([np.sin(a), np.sin(a)], axis=0).astype(np.float32)
    return c


WEIGHT_SPECS = [
    ("norm_mix_g", [D]), ("w_in", [D, 2112]), ("q_norm_g", [QL]), ("w_uq", [QL, 1536]),
    ("kv_norm_g", [KVL]), ("w_ukv", [KVL, 2048]), ("fourier_out_g", [NF]), ("mla_out_g", [1024]),
    ("w_out", [D, D]), ("norm_cross_g", [D]), ("norm_mem_g", [D]), ("w_cq", [D, D]),
    ("w_ckv", [D, 2 * D]), ("w_co", [D, D]), ("norm_ffn_g", [D]), ("w_gate", [D, DFF]),
    ("w_up", [D, DFF]), ("conv_w", [3, DFF]), ("conv_b", [DFF]), ("w_down", [DFF, D]),
    ("final_norm_g", [D]),
]


def build(S, dbg=False, phases=99):
    N2 = S // 128
    NT = S // TT
    nc = bass.Bass("TRN2", target_bir_lowering=False)
    I = {}
    I["x"] = nc.dram_tensor("x", [S, D], F32, kind="ExternalInput").ap()
    I["mem"] = nc.dram_tensor("mem", [NMEM, D], F32, kind="ExternalInput").ap()
    for name, shp in WEIGHT_SPECS:
        I[name] = nc.dram_tensor(name, shp, F32, kind="ExternalInput").ap()
    cshapes = {"c_ident": [128, 128], "c_cdft": [GC, 2 * GC], "c_fa": [128, N2, 3, 128],
               "c_fb": [2 * N2, N2], "c_cos": [DR, S], "c_sin": [DR, S]}
    for name, shp in cshapes.items():
        I[name] = nc.dram_tensor(name, shp, F32, kind="ExternalInput").ap()
    y_out = nc.dram_tensor("y", [S, D], F32, kind="ExternalOutput").ap()

    skind = "ExternalOutput" if dbg else "Internal"

    def scratch(name, shape, dt):
        return nc.dram_tensor(name, shape, dt, kind=skind).ap()

    WIN = scratch("s_win", [D, 2176], BF16)
    WUQ = scratch("s_wuq", [QL, 2048], BF16)
    WUKV = scratch("s_wukv", [KVL, 2048], BF16)
    WOUT = scratch("s_wout", [D, D], BF16)
    WCQ = scratch("s_wcq", [D, D], BF16)
    WCKV = scratch("s_wckv", [D, 2 * D], BF16)
    WCO = scratch("s_wco", [D, D], BF16)
    WG = scratch("s_wg", [D, DFF], BF16)
    WU = scratch("s_wu", [D, DFF], BF16)
    WD = scratch("s_wd", [DFF, D], BF16)
    VS = scratch("s_vs", [S, NG, 2, GC], BF16)
    YS = scratch("s_ys", [NG, 2, N2, 128, GC], BF16)
    FS = scratch("s_fs", [S, NF], F32)
    QN = scratch("s_qn", [NH, DN, S], BF16)
    QR = scratch("s_qr", [NH, DR, S], BF16)
    KN = scratch("s_kn", [NH, DN, S], BF16)
    KR = scratch("s_kr", [DR, S], BF16)
    VV = scratch("s_vv", [NH, 128, S // 128, DV], BF16)
    OT = scratch("s_ot", [NH * DV, S], F32)
    X2 = scratch("s_x2", [S, D], F32)

    es = ExitStack()
    with es:
        T = Tracker(nc, es)
        op = T.op
        ins = T.ins

        def sb(stack, name, shape, dt):
            return stack.enter_context(nc.sbuf_tensor(name, shape, dt))

        PS = [es.enter_context(nc.psum_tensor("psb%d" % i, [128, 512], F32)) for i in range(8)]
        PSR = [Res() for _ in range(8)]
        ps_i = [0]

        def next_ps():
            i = ps_i[0] % 8
            ps_i[0] += 1
            return PS[i], PSR[i]

        ident_f = sb(es, "ident_f", [128, 128], F32)
        ident_b = sb(es, "ident_b", [128, 128], BF16)
        ones_b = sb(es, "ones_b", [128, 128], BF16)
        o512_b = sb(es, "o512_b", [128, 128], BF16)
        o1024_b = sb(es, "o1024_b", [128, 128], BF16)
        gvec = sb(es, "gvec", [128, 88], F32)
        cwv = sb(es, "cwv", [128, 4, FC], F32)
        gfin = sb(es, "gfin", [128, D], F32)
        R_const = Res()
        s_const = T.new_sem("d_const")

        G_OFF = {"norm_mix_g": 0, "q_norm_g": 16, "kv_norm_g": 20, "fourier_out_g": 24, "mla_out_g": 32,
                 "norm_cross_g": 40, "norm_mem_g": 56, "norm_ffn_g": 72}

        with ExitStack() as ph:
            va = sb(ph, "va", [88, 128], F32)
            vb = sb(ph, "vb", [88, 128], F32)
            vc = sb(ph, "vc", [88, 128], F32)
            R_v = Res()
            ins("sp", "dma_start", out=ident_f[:], in_=I["c_ident"], writes=[R_const], dma=s_const)
            for name, off in G_OFF.items():
                n = I[name].shape[0] // 128
                src = I[name].rearrange("(k p) -> k p", p=128)
                ins("sp", "dma_start", out=va[off:off + n, :], in_=src,
                   writes=[R_v], dma=s_const)
            for j in range(3):
                dst = (vb, vb, vc)[j]
                o = (0, 44, 0)[j]
                src = I["conv_w"][j].rearrange("(k p) -> k p", p=128)
                ins("sp", "dma_start", out=dst[o:o + 44, :], in_=src,
                   writes=[R_v], dma=s_const)
            src = I["conv_b"].rearrange("(k p) -> k p", p=128)
            ins("sp", "dma_start", out=vc[44:88, :], in_=src, writes=[R_v], dma=s_const)
            ins("sp", "dma_start", out=gfin[:], in_=I["final_norm_g"].partition_broadcast(128),
               writes=[R_const], dma=s_const)
            ins("dve", "tensor_copy", out=ident_b[:], in_=ident_f[:], reads=[R_const], writes=[R_const])
            ins("dve", "memset", ones_b[:], 1.0, writes=[R_const])
            ins("dve", "memset", o512_b[:], 1.0 / 512, writes=[R_const])
            ins("dve", "memset", o1024_b[:], 1.0 / 1024, writes=[R_const])
            for src_t, dst_ap in ((va, gvec[:, :]), (vb, cwv[:, 0:2, :]), (vc, cwv[:, 2:4, :])):
                pt, pr = next_ps()
                ins("pe", "transpose", out=pt[:, 0:88], in_=src_t[:, :], identity=ident_f[0:88, 0:88],
                   reads=[R_v, R_const], writes=[pr])
                if dst_ap.ndim == 3:
                    srcv = pt[:, 0:88].rearrange("p (a b) -> p a b", a=2)
                else:
                    srcv = pt[:, 0:88]
                ins("dve", "tensor_copy", out=dst_ap, in_=srcv,
                   reads=[pr], writes=[R_const])
            T.barrier()

        deferred = []
        with ExitStack() as ph:
            NSL = 3
            stf = [sb(ph, "stf%d" % i, [128, 2176], F32) for i in range(NSL)]
            stb = [sb(ph, "stb%d" % i, [128, 2176], BF16) for i in range(NSL)]
            Rf = [Res() for _ in range(NSL)]
            Rb = [Res() for _ in range(NSL)]
            s_ld = [T.new_sem("d_p0l%d" % i) for i in range(NSL)]
            s_st = [T.new_sem("d_p0s%d" % i) for i in range(NSL)]
            cnt = [0]

            def conv_block(src_ap, dst_ap, g_ap, pieces, kc_i):
                i = cnt[0] % NSL
                ceng = ("act", "dve")[cnt[0] % 2]
                cnt[0] += 1
                ws = src_ap.shape[1]
                wd = dst_ap.shape[1]
                ins("sp", "dma_start", out=stf[i][:, 0:ws], in_=src_ap, writes=[Rf[i]], dma=s_ld[i])
                for (dv, sv, sign) in pieces:
                    o_ap = dv(stb[i])
                    i_ap = sv(stf[i])
                    if sign < 0 or ceng == "dve":
                        if g_ap is None:
                            ins("dve", "tensor_scalar",
                                out=o_ap, in0=i_ap, scalar1=float(sign), scalar2=None, op0=ALU.mult,
                                reads=[Rf[i], R_const], writes=[Rb[i]])
                        else:
                            ins("dve", "tensor_scalar",
                                out=o_ap, in0=i_ap, scalar1=g_ap, scalar2=float(sign), op0=ALU.mult, op1=ALU.mult,
                                reads=[Rf[i], R_const], writes=[Rb[i]])
                    else:
                        if g_ap is None:
                            ins("act", "copy", out=o_ap, in_=i_ap,
                               reads=[Rf[i], R_const], writes=[Rb[i]])
                        else:
                            ins("act", "mul", out=o_ap, in_=i_ap, mul=g_ap,
                               reads=[Rf[i], R_const], writes=[Rb[i]])
                ins("pool", "dma_start", out=dst_ap, in_=stb[i][:, 0:wd], reads=[Rb[i]], dma=s_st[i])

            def simple(src, dst, gname, goff2=0):
                din, dout = src.shape
                for kc in range(din // 128):
                    g_ap = None
                    if gname is not None:
                        col = G_OFF[gname] + kc - goff2
                        g_ap = gvec[:, col:col + 1]
                    for c0 in range(0, dout, 2048):
                        w = min(2048, dout - c0)
                        deferred.append((src[kc * 128:(kc + 1) * 128, c0:c0 + w], dst[kc * 128:(kc + 1) * 128, c0:c0 + w], g_ap, w))

            for kc in range(KC):
                g_ap = gvec[:, kc:kc + 1]
                conv_block(I["w_in"][kc * 128:(kc + 1) * 128, :], WIN[kc * 128:(kc + 1) * 128, :], g_ap,
                           [(lambda t: t[:, 0:2112], lambda t: t[:, 0:2112], 1),
                            (lambda t: t[:, 2112:2144], lambda t: t[:, 2080:2112], -1),
                            (lambda t: t[:, 2144:2176], lambda t: t[:, 2048:2080], 1)], kc)
            for kc in range(QL // 128):
                g_ap = gvec[:, 16 + kc:17 + kc]

                def sv(t, a, b):
                    return t[:, 0:1536].rearrange("p (h c) -> p h c", c=192)[:, :, a:b]

                conv_block(I["w_uq"][kc * 128:(kc + 1) * 128, :], WUQ[kc * 128:(kc + 1) * 128, :], g_ap,
                           [(lambda t: t[:, 0:1024].rearrange("p (h c) -> p h c", c=128), lambda t: sv(t, 0, 128), 1),
                            (lambda t: t[:, 1024:1536].rearrange("p (h c) -> p h c", c=64), lambda t: sv(t, 128, 192), 1),
                            (lambda t: t[:, 1536:2048].rearrange("p (h c) -> p h c", c=64)[:, :, 0:32], lambda t: sv(t, 160, 192), -1),
                            (lambda t: t[:, 1536:2048].rearrange("p (h c) -> p h c", c=64)[:, :, 32:64], lambda t: sv(t, 128, 160), 1)],
                           kc)
            for kc in range(KVL // 128):
                g_ap = gvec[:, 20 + kc:21 + kc]

                def sv2(t, a, b):
                    return t[:, 0:2048].rearrange("p (h c) -> p h c", c=256)[:, :, a:b]

                conv_block(I["w_ukv"][kc * 128:(kc + 1) * 128, :], WUKV[kc * 128:(kc + 1) * 128, :], g_ap,
                           [(lambda t: t[:, 0:1024].rearrange("p (h c) -> p h c", c=128), lambda t: sv2(t, 0, 128), 1),
                            (lambda t: t[:, 1024:2048].rearrange("p (h c) -> p h c", c=128), lambda t: sv2(t, 128, 256), 1)],
                           kc)
            if phases >= 4:
                simple(I["w_out"], WOUT, "fourier_out_g")
                simple(I["w_cq"], WCQ, "norm_cross_g")
                simple(I["w_ckv"], WCKV, "norm_mem_g")
                simple(I["w_co"], WCO, None)
            if phases >= 6:
                simple(I["w_gate"], WG, "norm_ffn_g")
                simple(I["w_up"], WU, "norm_ffn_g")
                simple(I["w_down"], WD, None)
            T.barrier()


        def load_w(stack_slot, Rslot, ssem, dram_ap, nkc, wcols):
            flat = stack_slot[:].rearrange("p a b -> p (a b)")
            dst = flat[:, 0:nkc * wcols].rearrange("p (k c) -> p k c", k=nkc)
            src = dram_ap.rearrange("(k p) c -> p k c", p=128)
            ins("sp", "dma_start", out=dst, in_=src, writes=[Rslot], dma=ssem)
            return dst


        def make_deferred_emitter(stack, tag, ceng):
            NS2 = 2
            dstf = [sb(stack, "dstf%s%d" % (tag, i), [128, 2048], F32) for i in range(NS2)]
            dstb = [sb(stack, "dstb%s%d" % (tag, i), [128, 2048], BF16) for i in range(NS2)]
            dRf = [Res() for _ in range(NS2)]
            dRb = [Res() for _ in range(NS2)]
            d_ld = [T.new_sem("d_dl%s%d" % (tag, i)) for i in range(NS2)]
            d_st = [T.new_sem("d_ds%s%d" % (tag, i)) for i in range(NS2)]
            cnt = [0]

            def emit(n):
                for _ in range(n):
                    if not deferred:
                        return
                    src_ap, dst_ap, g_ap, w = deferred.pop(0)
                    i = cnt[0] % NS2
                    cnt[0] += 1
                    ins("sp", "dma_start", out=dstf[i][:, 0:w], in_=src_ap, writes=[dRf[i]], dma=d_ld[i])
                    sc1 = g_ap if g_ap is not None else 1.0
                    if ceng == "act":
                        if g_ap is None:
                            ins("act", "copy", out=dstb[i][:, 0:w], in_=dstf[i][:, 0:w], reads=[dRf[i]], writes=[dRb[i]])
                        else:
                            ins("act", "mul", out=dstb[i][:, 0:w], in_=dstf[i][:, 0:w], mul=g_ap,
                                reads=[dRf[i], R_const], writes=[dRb[i]])
                    else:
                        ins(ceng, "tensor_scalar", out=dstb[i][:, 0:w], in0=dstf[i][:, 0:w], scalar1=sc1, scalar2=1.0,
                            op0=ALU.mult, op1=ALU.mult, reads=[dRf[i], R_const], writes=[dRb[i]])
                    ins("pool", "dma_start", out=dst_ap, in_=dstb[i][:, 0:w], reads=[dRb[i]], dma=d_st[i])
            return emit

        class WStream:
            def __init__(self, slots, Rs, sems, specs):
                self.slots, self.Rs, self.sems, self.specs = slots, Rs, sems, specs
                self.nxt = 0
                self.views = {}

            def get(self, k):
                n = len(self.slots)
                while self.nxt <= min(k + n - 1, len(self.specs) - 1):
                    j = self.nxt
                    ap, nkc, wc = self.specs[j]
                    self.views[j] = load_w(self.slots[j % n], self.Rs[j % n], self.sems[j % n], ap, nkc, wc)
                    self.nxt += 1
                return self.views[k], self.Rs[k % n]

        stat = sb(es, "stat", [128, 64], F32)
        Rstat = [Res() for _ in range(16)]
        stat_i = [0]

        def next_stat():
            i = stat_i[0] % 16
            stat_i[0] += 1
            return stat[:, 4 * i:4 * i + 4], Rstat[i]

        def tok_norm_a(x_ap, Rx, xs_t, Rxs, junk_t, width=D):
            st_ap, Rs = next_stat()
            ins("act", "activation", out=junk_t[:, 0:width], in_=x_ap, func=AF.Square,
                scale=float(width) ** -0.5, accum_out=st_ap[:, 0:1], reads=[Rx], writes=[Rs])
            ins("act", "activation", out=st_ap[:, 1:2], in_=st_ap[:, 0:1], func=AF.Sqrt, bias=eps_t[:, 0:1],
                reads=[Rs, R_const], writes=[Rs])
            ins("dve", "reciprocal", out=st_ap[:, 2:3], in_=st_ap[:, 1:2], reads=[Rs], writes=[Rs])
            ins("dve", "tensor_scalar", out=xs_t[:, 0:width], in0=x_ap, scalar1=st_ap[:, 2:3], scalar2=None,
                op0=ALU.mult, reads=[Rx, Rs], writes=[Rxs])

        def tok_norm_b(xs_t, Rxs, hT_t, RhT, j, width=D):
            nk = width // 128
            for g0 in range(0, nk, 8):
                gn = min(8, nk - g0)
                pt, pr = next_ps()
                ptb = pt.bitcast(BF16)
                for k in range(gn):
                    ins("pe", "transpose", out=ptb[:, k * 128:(k + 1) * 128],
                        in_=xs_t[:, (g0 + k) * 128:(g0 + k + 1) * 128], identity=ident_b[:, :],
                        reads=[Rxs, R_const], writes=[pr])
                src = ptb[:, 0:gn * 128].rearrange("p (k c) -> p k c", k=gn)
                dst = hT_t[:, g0:g0 + gn, j * 128:(j + 1) * 128]
                if (g0 // 8) % 2 == 0:
                    ins("act", "copy", out=dst, in_=src, reads=[pr], writes=[RhT])
                else:
                    ins("dve", "tensor_copy", out=dst, in_=src, reads=[pr], writes=[RhT])

        def tok_norm_T(x_ap, Rx, xs_t, Rxs, junk_t, hT_t, RhT, j, width=D):
            tok_norm_a(x_ap, Rx, xs_t, Rxs, junk_t, width)
            tok_norm_b(xs_t, Rxs, hT_t, RhT, j, width)

        def norm_pipe(n, get_x, xs_l, Rxs_l, junk_t, hT_t, RhT, width=D, pre_load=None, done_a=0):
            for j in range(n + 1):
                if j < n and j >= done_a:
                    if pre_load is not None:
                        pre_load(j)
                    x_ap, Rx = get_x(j)
                    tok_norm_a(x_ap, Rx, xs_l[j % 2], Rxs_l[j % 2], junk_t, width)
                if j >= 1:
                    tok_norm_b(xs_l[(j - 1) % 2], Rxs_l[(j - 1) % 2], hT_t, RhT, j - 1, width)

        eps_t = sb(es, "eps_t", [128, 1], F32)
        ins("dve", "memset", eps_t[:], EPS, writes=[R_const])

        def fm_rstd(src_chunks, Rsrc, sq_t, Rsq, ones_t, rstd_t, Rrstd, n):
            nchunks = len(src_chunks)
            for c, a in enumerate(src_chunks):
                ins("act", "activation", out=sq_t[:, c, 0:n], in_=a, func=AF.Square,
                   reads=[Rsrc], writes=[Rsq])
            pt, pr = next_ps()
            for c in range(nchunks):
                ins("pe", "matmul", pt[:, 0:n], ones_t[:, :], sq_t[:, c, 0:n], start=(c == 0),
                                                 stop=(c == nchunks - 1), reads=[Rsq, R_const], writes=[pr])
            ins("act", "activation", out=rstd_t[:, 0:n], in_=pt[:, 0:n], func=AF.Sqrt, bias=eps_t[:, 0:1],
               reads=[pr, R_const], writes=[Rrstd])
            ins("dve", "reciprocal", out=rstd_t[:, 0:n], in_=rstd_t[:, 0:n], reads=[Rrstd], writes=[Rrstd])

        with ExitStack() as ph:
            wsl = [sb(ph, "w1_%d" % i, [128, 16, 512], BF16) for i in range(3)]
            Rw = [Res() for _ in range(3)]
            s_w = [T.new_sem("d_w1_%d" % i) for i in range(3)]
            wspecs = []
            for it in range(NT):
                for c in range(4):
                    wspecs.append((WIN[:, c * 512:(c + 1) * 512], 16, 512))
                wspecs.append((WIN[:, 2048:2176], 16, 128))
                wspecs.append((WUQ[:, :], 4, 2048))
                wspecs.append((WUKV[:, :], 4, 2048))
            WS = WStream(wsl, Rw, s_w, wspecs)

            cdft_f = sb(ph, "cdft_f", [128, 2, 512], F32)
            cdft_b = sb(ph, "cdft_b", [128, 2, 512], BF16)
            R_cd = Res()
            ins("sp", "dma_start", out=cdft_f[:], in_=I["c_cdft"].rearrange("(k p) c -> p k c", p=128),
               writes=[R_cd], dma=s_const)
            ins("dve", "tensor_copy", out=cdft_b[:], in_=cdft_f[:], reads=[R_cd], writes=[R_cd])
            xt = [sb(ph, "xt%d" % i, [128, D], F32) for i in range(2)]
            Rxt = [Res() for _ in range(2)]
            s_xt = [T.new_sem("d_xt%d" % i) for i in range(2)]
            xs = [sb(ph, "xs%d" % i, [128, D], BF16) for i in range(2)]
            Rxs = [Res() for _ in range(2)]
            junk = sb(ph, "junk", [128, D], BF16)
            hT = sb(ph, "hT", [128, KC, TT], BF16)
            RhT = Res()
            fT = sb(ph, "fT", [128, 8, TT], BF16)
            RfT = Res()
            vst = [sb(ph, "vst%d" % i, [128, 2048], BF16) for i in range(2)]
            Rvst = [Res() for _ in range(2)]
            s_vst = [T.new_sem("d_vst%d" % i) for i in range(2)]
            cset = []
            for nm in ("q", "kv"):
                cset.append(dict(
                    c=sb(ph, "c_%s" % nm, [128, 4, TT], F32), Rc=Res(),
                    sq=sb(ph, "sq_%s" % nm, [128, 4, TT], BF16), Rsq=Res(),
                    rstd=sb(ph, "rstd_%s" % nm, [128, TT], F32), Rrstd=Res(),
                    cn=sb(ph, "cn_%s" % nm, [128, 4, TT], BF16), Rcn=Res()))
            qk_o = sb(ph, "qk_o", [128, NH, TT], BF16)
            Rqk_o = Res()
            s_qk_o = T.new_sem("d_qk_o")
            qr_o = sb(ph, "qr_o", [64, NH, TT], BF16)
            Rqr_o = Res()
            s_qr_o = T.new_sem("d_qr_o")
            v_o = sb(ph, "v_o", [128, NH, 4, DV], BF16)
            Rv_o = Res()
            s_v_o = T.new_sem("d_v_o")
            kr_o = sb(ph, "kr_o", [64, TT], BF16)
            Rkr_o = Res()
            s_kr_o = T.new_sem("d_kr_o")
            cs_t = sb(ph, "cs_t", [64, 2, TT], F32)
            Rcs = Res()
            s_cs = T.new_sem("d_cs")
            rt = sb(ph, "rt", [64, 2, TT], F32)
            Rrt = Res()

            x_tiled = I["x"].rearrange("(n p) d -> n p d", p=128)
            xload_done = {}

            def load_x(g):
                if g in xload_done or g >= S // 128:
                    return
                xload_done[g] = True
                ins("sp", "dma_start", out=xt[g % 2][:, :], in_=x_tiled[g], writes=[Rxt[g % 2]], dma=s_xt[g % 2])

            def rope_combine(p_plain, pr_plain, p_rot, pr_rot, out_ap, Rout):
                ins("dve", "tensor_tensor", out=rt[:, 0, :], in0=p_plain[0:64, :], in1=cs_t[:, 0, :], op=ALU.mult,
                   reads=[pr_plain, Rcs], writes=[Rrt])
                ins("dve", "tensor_tensor", out=rt[:, 1, :], in0=p_rot[0:64, :], in1=cs_t[:, 1, :], op=ALU.mult,
                   reads=[pr_rot, Rcs], writes=[Rrt])
                ins("dve", "tensor_tensor", out=out_ap, in0=rt[:, 0, :], in1=rt[:, 1, :], op=ALU.add,
                   reads=[Rrt], writes=[Rout])

            evq = [0]

            def evac(dst, src, pr, Rdst):
                evq[0] += 1
                if evq[0] % 2 == 0:
                    ins("act", "copy", out=dst, in_=src, reads=[pr], writes=[Rdst])
                else:
                    ins("dve", "tensor_copy", out=dst, in_=src, reads=[pr], writes=[Rdst])

            for it in range(NT):
                t0 = it * TT
                ins("sp", "dma_start", out=cs_t[:, 0, :], in_=I["c_cos"][:, t0:t0 + TT], writes=[Rcs], dma=s_cs)
                ins("sp", "dma_start", out=cs_t[:, 1, :], in_=I["c_sin"][:, t0:t0 + TT], writes=[Rcs], dma=s_cs)
                if it == 0:
                    load_x(0)
                    load_x(1)
                norm_pipe(4, lambda j, it=it: (xt[(it * 4 + j) % 2][:, :], Rxt[(it * 4 + j) % 2]), xs, Rxs, junk, hT, RhT,
                          pre_load=lambda j, it=it: load_x(it * 4 + j), done_a=(0 if it == 0 else 2))
                wb = it * 7
                for m in range(8):
                    wv, wr = WS.get(wb + m // 4)
                    pt, pr = next_ps()
                    for kc in range(KC):
                        ins("pe", "matmul",
                            pt[:, :], wv[:, kc, (m % 4) * 128:(m % 4 + 1) * 128], hT[:, kc, :], start=(kc == 0), stop=(kc == KC - 1),
                            reads=[wr, RhT], writes=[pr])
                    evac(fT[:, m, :], pt[:, :], pr, RfT)
                for j in range(4):
                    g = it * 4 + j
                    vs_t, vs_r, vs_s = vst[g % 2], Rvst[g % 2], s_vst[g % 2]
                    for gi in range(NG):
                        pt, pr = next_ps()
                        for kc in range(2):
                            ins("pe", "matmul",
                                pt[:, :], fT[:, 2 * gi + kc, j * 128:(j + 1) * 128], cdft_b[:, kc, :], start=(kc == 0), stop=(kc == 1),
                                reads=[RfT, R_cd], writes=[pr])
                        evac(vs_t[:, gi * 512:(gi + 1) * 512], pt[:, :], pr, vs_r)
                    dst = VS[g * 128:(g + 1) * 128].rearrange("t g r c -> t (g r c)")
                    ins("pool", "dma_start", out=dst, in_=vs_t[:, :], reads=[vs_r], dma=vs_s)
                if it + 1 < NT:
                    for j in range(2):
                        g = (it + 1) * 4 + j
                        load_x(g)
                        tok_norm_a(xt[g % 2][:, :], Rxt[g % 2], xs[g % 2], Rxs[g % 2], junk)
                for ci in range(2):
                    cs_ = cset[ci]
                    wv, wr = WS.get(wb + 2 + ci)
                    for c in range(4):
                        pt, pr = next_ps()
                        for kc in range(KC):
                            ins("pe", "matmul",
                                pt[:, :], wv[:, kc, c * 128:(c + 1) * 128], hT[:, kc, :], start=(kc == 0), stop=(kc == KC - 1),
                                reads=[wr, RhT], writes=[pr])
                        ins("dve", "tensor_copy", out=cs_["c"][:, c, :], in_=pt[:, :],
                           reads=[pr], writes=[cs_["Rc"]])
                    fm_rstd([cs_["c"][:, c, :] for c in range(4)], cs_["Rc"], cs_["sq"], cs_["Rsq"], o512_b,
                            cs_["rstd"], cs_["Rrstd"], TT)
                    for c in range(4):
                        ins("dve", "tensor_tensor", out=cs_["cn"][:, c, :], in0=cs_["c"][:, c, :],
                                                                          in1=cs_["rstd"][:, :], op=ALU.mult,
                           reads=[cs_["Rc"], cs_["Rrstd"]], writes=[cs_["Rcn"]])
                wk, r_ = WS.get(wb + 4)
                pp = []
                for half in range(2):
                    pt, pr = next_ps()
                    for kc in range(KC):
                        ins("pe", "matmul",
                            pt[0:64, :], wk[:, kc, half * 64:(half + 1) * 64], hT[:, kc, :], start=(kc == 0), stop=(kc == KC - 1),
                            reads=[r_, RhT], writes=[pr])
                    pp.append((pt, pr))
                rope_combine(pp[0][0], pp[0][1], pp[1][0], pp[1][1], kr_o[:, :], Rkr_o)
                ins("pool", "dma_start", out=KR[:, t0:t0 + TT], in_=kr_o[:, :], reads=[Rkr_o], dma=s_kr_o)
                wq, r_ = WS.get(wb + 5)
                cn = cset[0]["cn"]
                Rcn = cset[0]["Rcn"]
                for h in range(NH):
                    pt, pr = next_ps()
                    for kc in range(4):
                        ins("pe", "matmul",
                            pt[:, :], wq[:, kc, h * 128:(h + 1) * 128], cn[:, kc, :], start=(kc == 0), stop=(kc == 3),
                            reads=[r_, Rcn], writes=[pr])
                    evac(qk_o[:, h, :], pt[:, :], pr, Rqk_o)
                ins("pool", "dma_start", out=QN[:, :, t0:t0 + TT].rearrange("h d t -> d h t"), in_=qk_o[:, :, :],
                   reads=[Rqk_o], dma=s_qk_o)
                for h in range(NH):
                    pp = []
                    for half in range(2):
                        pt, pr = next_ps()
                        c0 = 1024 + half * 512 + h * 64
                        for kc in range(4):
                            ins("pe", "matmul",
                                pt[0:64, :], wq[:, kc, c0:c0 + 64], cn[:, kc, :], start=(kc == 0), stop=(kc == 3),
                                reads=[r_, Rcn], writes=[pr])
                        pp.append((pt, pr))
                    rope_combine(pp[0][0], pp[0][1], pp[1][0], pp[1][1], qr_o[:, h, :], Rqr_o)
                ins("pool", "dma_start", out=QR[:, :, t0:t0 + TT].rearrange("h d t -> d h t"), in_=qr_o[:, :, :],
                   reads=[Rqr_o], dma=s_qr_o)
                wkv, r_ = WS.get(wb + 6)
                cn = cset[1]["cn"]
                Rcn = cset[1]["Rcn"]
                for h in range(NH):
                    pt, pr = next_ps()
                    for kc in range(4):
                        ins("pe", "matmul",
                            pt[:, :], wkv[:, kc, h * 128:(h + 1) * 128], cn[:, kc, :], start=(kc == 0), stop=(kc == 3),
                            reads=[r_, Rcn], writes=[pr])
                    evac(qk_o[:, h, :], pt[:, :], pr, Rqk_o)
                ins("pool", "dma_start", out=KN[:, :, t0:t0 + TT].rearrange("h d t -> d h t"), in_=qk_o[:, :, :],
                   reads=[Rqk_o], dma=s_qk_o)
                for j in range(4):
                    for n in range(2):
                        pt, pr = next_ps()
                        for kc in range(4):
                            ins("pe", "matmul",
                                pt[:, :], cn[:, kc, j * 128:(j + 1) * 128], wkv[:, kc, 1024 + n * 512:1024 + (n + 1) * 512],
                                start=(kc == 0), stop=(kc == 3), reads=[r_, Rcn], writes=[pr])
                        evac(v_o[:, 4 * n:4 * n + 4, j, :], pt[:, :].rearrange("p (h d) -> p h d", h=4), pr, Rv_o)
                ins("pool", "dma_start", out=VV[:, :, 4 * it:4 * it + 4, :].rearrange("h p j d -> p h j d"),
                                                        in_=v_o[:, :, :, :], reads=[Rv_o], dma=s_v_o)
            T.barrier()


        if phases >= 2:
          with ExitStack() as ph:
            At = sb(ph, "At", [128, N2, 512], BF16)
            fa_b = sb(ph, "fa_b", [128, N2, 3, 128], BF16)
            fb_b = sb(ph, "fb_b", [2 * N2, N2], BF16)
            fstg = sb(ph, "fstg", [128, 8, 3, 128], F32)
            fb_f = sb(ph, "fb_f", [2 * N2, N2], F32)
            R_fa, R_fstg, R_fb = Res(), Res(), Res()
            s_f = T.new_sem("d_fft_c")
            SB = min(8, N2)
            for b0 in range(0, N2, SB):
                ins("sp", "dma_start", out=fstg[:, 0:SB], in_=I["c_fa"][:, b0:b0 + SB], writes=[R_fstg], dma=s_f)
                ins("act", "copy", out=fa_b[:, b0:b0 + SB], in_=fstg[:, 0:SB], reads=[R_fstg], writes=[R_fa])
            s_fbf = T.new_sem("d_fft_fb")
            ins("sp", "dma_start", out=fb_f[:, :], in_=I["c_fb"], writes=[R_fb], dma=s_fbf)
            ins("dve", "tensor_copy", out=fb_b[:, :], in_=fb_f[:, :], reads=[R_fb], writes=[R_fb])
            if dbg:
                d_fa = nc.dram_tensor("d_fa", [128, N2, 3, 128], BF16, kind="ExternalOutput").ap()
                s_dbg = T.new_sem("d_dbg")
                ins("sp", "dma_start", out=d_fa, in_=fa_b[:, :, :, :], reads=[R_fa], dma=s_dbg)
            NP = 4 if N2 >= 4 else 1
            PW = N2 // NP
            R_A = [Res() for _ in range(NP)]
            s_A = [T.new_sem("d_A%d" % i) for i in range(NP)]
            ystg = [sb(ph, "ystg%d" % i, [128, SB, 512], BF16) for i in range(2)]
            R_ystg = [Res() for _ in range(2)]
            s_ystg = [T.new_sem("d_ystg%d" % i) for i in range(2)]
            NB1 = N2 // SB
            R_YS = [[Res() for _ in range(2 * NB1)] for _ in range(NG)]
            KB = 16
            y2 = [sb(ph, "y2_%d" % i, [2 * N2, KB, 256], BF16) for i in range(2)]
            R_y2 = [Res() for _ in range(2)]
            s_y2 = [T.new_sem("d_y2_%d" % i) for i in range(2)]
            fo = [sb(ph, "fo%d" % i, [N2, KB, 256], F32) for i in range(2)]
            R_fo = [Res() for _ in range(2)]
            s_fo = [T.new_sem("d_fo%d" % i) for i in range(2)]
            VSv = VS.rearrange("(s1 s2) g r c -> s1 s2 g (r c)", s2=N2)
            FSv = FS.rearrange("(k2 k1) f -> k2 k1 f", k1=128)
            cnt1 = [0]
            cnt2 = [0]

            def stage1(g):
                for p in range(NP):
                    ins("sp", "dma_start", out=At[:, p * PW:(p + 1) * PW, :], in_=VSv[:, p * PW:(p + 1) * PW, g, :],
                        writes=[R_A[p]], dma=s_A[p])
                for b in range(NB1):
                    i = cnt1[0] % 2
                    cnt1[0] += 1
                    for sl in range(SB):
                        s2 = b * SB + sl
                        ra = R_A[s2 // PW]
                        pt, pr = next_ps()
                        ar = At[:, s2, 0:256]
                        ai = At[:, s2, 256:512]
                        ins("pe", "matmul", pt[:, 0:256], fa_b[:, s2, 0, :], ar, start=True, stop=False, reads=[R_fa, ra], writes=[pr])
                        ins("pe", "matmul", pt[:, 0:256], fa_b[:, s2, 2, :], ai, start=False, stop=True, reads=[R_fa, ra], writes=[pr])
                        ins("pe", "matmul", pt[:, 256:512], fa_b[:, s2, 1, :], ar, start=True, stop=False, reads=[R_fa, ra], writes=[pr])
                        ins("pe", "matmul", pt[:, 256:512], fa_b[:, s2, 0, :], ai, start=False, stop=True, reads=[R_fa, ra], writes=[pr])
                        if sl % 2 == 0:
                            ins("act", "copy", out=ystg[i][:, sl, :], in_=pt[:, :], reads=[pr], writes=[R_ystg[i]])
                        else:
                            ins("dve", "tensor_copy", out=ystg[i][:, sl, :], in_=pt[:, :], reads=[pr], writes=[R_ystg[i]])
                    for r_ in range(2):
                        dst = YS[g, r_].rearrange("s k c -> k s c")[:, b * SB:(b + 1) * SB, :]
                        ins("pool", "dma_start", out=dst, in_=ystg[i][:, :, r_ * 256:(r_ + 1) * 256],
                            reads=[R_ystg[i]], writes=[R_YS[g][2 * b + r_]], dma=s_ystg[i])

            def stage2(g):
                Y2v = YS[g].rearrange("r s k c -> (r s) k c")
                for kb in range(128 // KB):
                    i = cnt2[0] % 2
                    cnt2[0] += 1
                    ins("sp", "dma_start", out=y2[i][:, :, :], in_=Y2v[:, kb * KB:(kb + 1) * KB, :], reads=R_YS[g],
                        writes=[R_y2[i]], dma=s_y2[i])
                    for kk in range(0, KB, 2):
                        pt, pr = next_ps()
                        ins("pe", "matmul", pt[0:N2, :], fb_b[:, :], y2[i][:, kk:kk + 2, :].rearrange("p a c -> p (a c)"),
                            start=True, stop=True, reads=[R_fb, R_y2[i]], writes=[pr])
                        dsto = fo[i][:, kk:kk + 2, :].rearrange("p a c -> p (a c)")
                        if (kk // 2) % 2 == 0:
                            ins("act", "copy", out=dsto, in_=pt[0:N2, :], reads=[pr], writes=[R_fo[i]])
                        else:
                            ins("dve", "tensor_copy", out=dsto, in_=pt[0:N2, :], reads=[pr], writes=[R_fo[i]])
                    ins("pool", "dma_start", out=FSv[:, kb * KB:(kb + 1) * KB, g * 256:(g + 1) * 256], in_=fo[i][:, :, :],
                        reads=[R_fo[i]], dma=s_fo[i])

            stage1(0)
            for g in range(1, NG):
                stage1(g)
                stage2(g - 1)
            stage2(NG - 1)
            T.barrier()


        if phases >= 3:
          with ExitStack() as ph:
            NKC = S // 128
            NQT = S // TT
            kr_sb = sb(ph, "kr_sb", [128, S], BF16)
            R_kr = Res()
            s_kr = T.new_sem("d_kr")
            ins("dve", "memset", kr_sb[64:128, :], 0.0, writes=[R_kr])
            ins("sp", "dma_start", out=kr_sb[0:64, :], in_=KR[:, :], writes=[R_kr], dma=s_kr)
            emit_def = make_deferred_emitter(ph, "a", "act")
            kn_sb = [sb(ph, "kn_sb%d" % i, [128, S], BF16) for i in range(2)]
            v_sb = [sb(ph, "v_sb%d" % i, [128, NKC, DV], BF16) for i in range(2)]
            R_kn = [Res() for _ in range(2)]
            R_v = [Res() for _ in range(2)]
            s_kn = [T.new_sem("d_kn%d" % i) for i in range(2)]
            s_v = [T.new_sem("d_v%d" % i) for i in range(2)]
            qn_sb = [sb(ph, "qn_sb%d" % i, [128, TT], BF16) for i in range(2)]
            qr_sb = [sb(ph, "qr_sb%d" % i, [128, TT], BF16) for i in range(2)]
            R_q = [Res() for _ in range(2)]
            for i in range(2):
                ins("dve", "memset", qr_sb[i][64:128, :], 0.0, writes=[R_q[i]])
            accL = [sb(ph, "accL%d" % i, [128, TT], F32) for i in range(4)]
            R_accL = [Res() for _ in range(4)]
            accb = sb(ph, "accb", [128, TT], BF16)
            R_accb = Res()
            accP = [sb(ph, "accP%d" % i, [128, TT], F32) for i in range(2)]
            R_accP = [Res() for _ in range(2)]
            s_q = [T.new_sem("d_q%d" % i) for i in range(2)]
            NPT = 8
            pT = [sb(ph, "pT%d" % i, [128, TT], BF16) for i in range(NPT)]
            R_pT = [Res() for _ in range(NPT)]
            rl = sb(ph, "rl", [128, TT], F32)
            R_rl = Res()
            ot_sb = [sb(ph, "ot_sb%d" % i, [128, TT], F32) for i in range(2)]
            R_ot = [Res() for _ in range(2)]
            s_ot = [T.new_sem("d_ot%d" % i) for i in range(2)]
            sm_scale = float(DN + DR) ** -0.5
            LOOK = 2
            POOL_SET = ()

            def load_head(h):
                i = h % 2
                ins("sp", "dma_start", out=kn_sb[i][:, :], in_=KN[h], writes=[R_kn[i]], dma=s_kn[i])
                ins("sp", "dma_start", out=v_sb[i][:, :, :], in_=VV[h], writes=[R_v[i]], dma=s_v[i])

            def load_q(n):
                h, qt = divmod(n, NQT)
                i = n % 2
                ins("sp", "dma_start", out=qn_sb[i][:, :], in_=QN[h, :, qt * TT:(qt + 1) * TT], writes=[R_q[i]], dma=s_q[i])
                ins("sp", "dma_start", out=qr_sb[i][0:64, :], in_=QR[h, :, qt * TT:(qt + 1) * TT], writes=[R_q[i]], dma=s_q[i])

            load_head(0)
            load_q(0)
            pcnt = [0]
            for h in range(NH):
                hi = h % 2
                for qt in range(NQT):
                    n = h * NQT + qt
                    qi = n % 2
                    if n + 1 < NH * NQT:
                        load_q(n + 1)
                    if qt == 1 and h + 1 < NH:
                        load_head(h + 1)
                    ob, obr = PS[4 + qi], PSR[4 + qi]
                    lb, lbr = PS[6 + qi], PSR[6 + qi]
                    pend = []
                    st_acc = {"p": False, "d": 0, "pe": False}

                    def s_tile(kc):
                        sbk, sbr = PS[pcnt[0] % 4], PSR[pcnt[0] % 4]
                        pi = pcnt[0] % NPT
                        pcnt[0] += 1
                        ins("pe", "matmul", sbk[:, :], kn_sb[hi][:, kc * 128:(kc + 1) * 128], qn_sb[qi][:, :], start=True, stop=False,
                            reads=[R_kn[hi], R_q[qi]], writes=[sbr])
                        ins("pe", "matmul", sbk[:, :], kr_sb[:, kc * 128:(kc + 1) * 128], qr_sb[qi][:, :], start=False, stop=True,
                            reads=[R_kr, R_q[qi]], writes=[sbr])
                        ins("act", "activation", out=pT[pi][:, :], in_=sbk[:, :], func=AF.Exp, scale=sm_scale,
                            reads=[sbr], writes=[R_pT[pi]])
                        return pi

                    def pv_tile(kc, pi):
                        ins("pe", "matmul", ob[:, :], v_sb[hi][:, kc, :], pT[pi][:, :], start=(kc == 0), stop=(kc == NKC - 1),
                            reads=[R_v[hi], R_pT[pi]], writes=[obr])
                        if kc % 8 == 7:
                            ins("pe", "matmul", lb[:, :], ones_b[:, :], pT[pi][:, :], start=(not st_acc["pe"]), stop=False,
                                reads=[R_const, R_pT[pi]], writes=[lbr])
                            st_acc["pe"] = True
                        elif (kc % 12) in POOL_SET:
                            if not st_acc["p"]:
                                st_acc["p"] = True
                                ins("pool", "tensor_copy", out=accP[qi][:, :], in_=pT[pi][:, :], reads=[R_pT[pi]], writes=[R_accP[qi]])
                            else:
                                ins("pool", "tensor_tensor", out=accP[qi][:, :], in0=accP[qi][:, :], in1=pT[pi][:, :], op=ALU.add,
                                    reads=[R_pT[pi], R_accP[qi]], writes=[R_accP[qi]])
                        else:
                            ai = 2 * qi + st_acc["d"] % 2
                            if st_acc["d"] < 2:
                                ins("dve", "tensor_copy", out=accL[ai][:, :], in_=pT[pi][:, :], reads=[R_pT[pi]], writes=[R_accL[ai]])
                            else:
                                ins("dve", "tensor_tensor", out=accL[ai][:, :], in0=accL[ai][:, :], in1=pT[pi][:, :], op=ALU.add,
                                    reads=[R_pT[pi], R_accL[ai]], writes=[R_accL[ai]])
                            st_acc["d"] += 1
                        if kc in (NKC // 4, (3 * NKC) // 4):
                            emit_def(1)

                    for kc in range(NKC + LOOK):
                        if kc < NKC:
                            pend.append((kc, s_tile(kc)))
                        if kc >= LOOK:
                            k0, p0 = pend.pop(0)
                            pv_tile(k0, p0)
                    ins("dve", "tensor_tensor", out=accL[2 * qi][:, :], in0=accL[2 * qi][:, :], in1=accL[2 * qi + 1][:, :], op=ALU.add,
                        reads=[R_accL[2 * qi], R_accL[2 * qi + 1]], writes=[R_accL[2 * qi]])
                    if st_acc["p"]:
                        ins("dve", "tensor_tensor", out=accb[:, :], in0=accL[2 * qi][:, :], in1=accP[qi][:, :], op=ALU.add,
                            reads=[R_accL[2 * qi], R_accP[qi]], writes=[R_accb])
                    else:
                        ins("dve", "tensor_copy", out=accb[:, :], in_=accL[2 * qi][:, :], reads=[R_accL[2 * qi]], writes=[R_accb])
                    ins("pe", "matmul", lb[:, :], ones_b[:, :], accb[:, :], start=(not st_acc["pe"]), stop=True,
                        reads=[R_const, R_accb], writes=[lbr])
                    ins("dve", "reciprocal", out=rl[:, :], in_=lb[:, :], reads=[lbr], writes=[R_rl])
                    ins("dve", "tensor_tensor", out=ot_sb[qi][:, :], in0=ob[:, :], in1=rl[:, :], op=ALU.mult,
                        reads=[obr, R_rl], writes=[R_ot[qi]])
                    ins("pool", "dma_start", out=OT[h * DV:(h + 1) * DV, qt * TT:(qt + 1) * TT], in_=ot_sb[qi][:, :],
                        reads=[R_ot[qi]], dma=s_ot[qi])
            emit_def(10 ** 6)
            T.barrier()


        if phases >= 4:
          with ExitStack() as ph:
            wsl = [sb(ph, "w4_%d" % i, [128, 16, 512], BF16) for i in range(3)]
            Rw = [Res() for _ in range(3)]
            s_w = [T.new_sem("d_w4_%d" % i) for i in range(3)]
            ckT = sb(ph, "ckT", [128, KC, NMEM], BF16)
            cv = sb(ph, "cv", [128, 2, D], BF16)
            R_ck, R_cv = Res(), Res()
            xs = [sb(ph, "xs4_%d" % i, [128, D], BF16) for i in range(2)]
            Rxs = [Res() for _ in range(2)]
            junk = sb(ph, "junk4", [128, D], BF16)
            evq = [0]

            def evac(dst, src, pr, Rdst):
                evq[0] += 1
                if evq[0] % 2 == 0:
                    ins("act", "copy", out=dst, in_=src, reads=[pr], writes=[Rdst])
                else:
                    ins("dve", "tensor_copy", out=dst, in_=src, reads=[pr], writes=[Rdst])

            with ExitStack() as pre:
                memx = [sb(pre, "memx%d" % i, [128, D], F32) for i in range(2)]
                Rmx = [Res() for _ in range(2)]
                s_mx = [T.new_sem("d_mx%d" % i) for i in range(2)]
                memT = sb(pre, "memT", [128, KC, NMEM], BF16)
                RmT = Res()
                wspecs = [(WCKV[:, c * 512:(c + 1) * 512], 16, 512) for c in range(8)]
                WS = WStream(wsl, Rw, s_w, wspecs)
                for j in range(2):
                    ins("sp", "dma_start", out=memx[j][:, :], in_=I["mem"][j * 128:(j + 1) * 128, :], writes=[Rmx[j]], dma=s_mx[j])
                    tok_norm_T(memx[j][:, :], Rmx[j], xs[j], Rxs[j], junk, memT, RmT, j)
                for m in range(KC):
                    wv, wr = WS.get(m // 4)
                    pt, pr = next_ps()
                    for kc in range(KC):
                        ins("pe", "matmul", pt[:, 0:NMEM], wv[:, kc, (m % 4) * 128:(m % 4 + 1) * 128], memT[:, kc, :],
                            start=(kc == 0), stop=(kc == KC - 1), reads=[wr, RmT], writes=[pr])
                    evac(ckT[:, m, :], pt[:, 0:NMEM], pr, R_ck)
                for n in range(4):
                    wv, wr = WS.get(4 + n)
                    for kk in range(2):
                        pt, pr = next_ps()
                        for kc in range(KC):
                            ins("pe", "matmul", pt[:, :], memT[:, kc, kk * 128:(kk + 1) * 128], wv[:, kc, :],
                                start=(kc == 0), stop=(kc == KC - 1), reads=[wr, RmT], writes=[pr])
                        evac(cv[:, kk, n * 512:(n + 1) * 512], pt[:, :], pr, R_cv)
                T.barrier()

            xt = sb(ph, "xt4", [128, 4, D], F32)
            Rxt = [Res() for _ in range(4)]
            s_xt = [T.new_sem("d_xt4_%d" % i) for i in range(4)]
            s_xo = [T.new_sem("d_xo4_%d" % i) for i in range(4)]
            otl = sb(ph, "otl", [128, NH, TT], F32)
            R_otl = Res()
            s_otl = T.new_sem("d_otl")
            fsl = [sb(ph, "fsl%d" % i, [128, NF], F32) for i in range(2)]
            R_fsl = [Res() for _ in range(2)]
            s_fsl = [T.new_sem("d_fsl%d" % i) for i in range(2)]
            sq = sb(ph, "sq4", [128, NH, TT], BF16)
            R_sq = Res()
            rstd = sb(ph, "rstd4", [128, TT], F32)
            R_rstd = Res()
            bufA = sb(ph, "bufA", [128, KC, TT], BF16)
            bufB = sb(ph, "bufB", [128, KC, TT], BF16)
            R_A4, R_B4 = Res(), Res()
            pTc = [sb(ph, "pTc%d" % i, [128, 2, TT], BF16) for i in range(2)]
            R_pTc = [Res() for _ in range(2)]
            rlc = sb(ph, "rlc", [128, TT], F32)
            R_rlc = Res()
            wspecs = []
            for it in range(NT):
                for W_ in (WOUT, WCQ, WCO):
                    for c in range(4):
                        wspecs.append((W_[:, c * 512:(c + 1) * 512], 16, 512))
            WS = WStream(wsl, Rw, s_w, wspecs)
            x_t4 = I["x"].rearrange("(n p) d -> n p d", p=128)
            x2_t4 = X2.rearrange("(n p) d -> n p d", p=128)
            fs_t4 = FS.rearrange("(n p) d -> n p d", p=128)
            OTv = OT.rearrange("(h d) t -> d h t", d=128)
            c_scale = float(CHD) ** -0.5
            hcnt = [0]

            pf_done = set()

            def load_otl(i):
                if ("o", i) in pf_done:
                    return
                pf_done.add(("o", i))
                ins("sp", "dma_start", out=otl[:, :, :], in_=OTv[:, :, i * TT:(i + 1) * TT], writes=[R_otl], dma=s_otl)

            def load_fs(g):
                if ("f", g) in pf_done:
                    return
                pf_done.add(("f", g))
                ins("sp", "dma_start", out=fsl[g % 2][:, :], in_=fs_t4[g], writes=[R_fsl[g % 2]], dma=s_fsl[g % 2])

            def proj_residual(actT, R_act, wbase):
                for n in range(4):
                    wv, wr = WS.get(wbase + n)
                    for j in range(4):
                        pt, pr = next_ps()
                        for kc in range(KC):
                            ins("pe", "matmul", pt[:, :], actT[:, kc, j * 128:(j + 1) * 128], wv[:, kc, :],
                                start=(kc == 0), stop=(kc == KC - 1), reads=[wr, R_act], writes=[pr])
                        ins("dve", "tensor_tensor", out=xt[:, j, n * 512:(n + 1) * 512], in0=pt[:, :],
                            in1=xt[:, j, n * 512:(n + 1) * 512], op=ALU.add, reads=[pr, Rxt[j]], writes=[Rxt[j]])

            for it in range(NT):
                t0 = it * TT
                wb = it * 12
                load_otl(it)
                ins("act", "activation", out=sq[:, :, :], in_=otl[:, :, :], func=AF.Square, reads=[R_otl], writes=[R_sq])
                pt, pr = next_ps()
                for h in range(NH):
                    ins("pe", "matmul", pt[:, :], o1024_b[:, :], sq[:, h, :], start=(h == 0), stop=(h == NH - 1),
                        reads=[R_sq, R_const], writes=[pr])
                ins("act", "activation", out=rstd[:, :], in_=pt[:, :], func=AF.Sqrt, bias=eps_t[:, 0:1],
                    reads=[pr, R_const], writes=[R_rstd])
                ins("dve", "reciprocal", out=rstd[:, :], in_=rstd[:, :], reads=[R_rstd], writes=[R_rstd])
                for h in range(NH):
                    ins("dve", "tensor_tensor", out=bufA[:, 8 + h, :], in0=otl[:, h, :], in1=rstd[:, :], op=ALU.mult,
                        reads=[R_otl, R_rstd], writes=[R_A4])
                norm_pipe(4, lambda j, it=it: (fsl[(it * 4 + j) % 2][:, :], R_fsl[(it * 4 + j) % 2]), xs, Rxs, junk, bufA, R_A4,
                          width=NF, pre_load=lambda j, it=it: load_fs(it * 4 + j))
                for j in range(4):
                    ins("sp", "dma_start", out=xt[:, j, :], in_=x_t4[it * 4 + j], writes=[Rxt[j]], dma=s_xt[j])
                proj_residual(bufA, R_A4, wb)
                if it + 1 < NT:
                    load_otl(it + 1)
                    load_fs((it + 1) * 4)
                    load_fs((it + 1) * 4 + 1)
                norm_pipe(4, lambda j: (xt[:, j, :], Rxt[j]), xs, Rxs, junk, bufB, R_B4)
                for m in range(KC):
                    wv, wr = WS.get(wb + 4 + m // 4)
                    pt, pr = next_ps()
                    for kc in range(KC):
                        ins("pe", "matmul", pt[:, :], wv[:, kc, (m % 4) * 128:(m % 4 + 1) * 128], bufB[:, kc, :],
                            start=(kc == 0), stop=(kc == KC - 1), reads=[wr, R_B4], writes=[pr])
                    evac(bufA[:, m, :], pt[:, :], pr, R_A4)
                for hc in range(NCH):
                    pi = hcnt[0] % 2
                    hcnt[0] += 1
                    for kk in range(2):
                        pt, pr = next_ps()
                        for dc in range(4):
                            ins("pe", "matmul", pt[:, :], ckT[:, hc * 4 + dc, kk * 128:(kk + 1) * 128], bufA[:, hc * 4 + dc, :],
                                start=(dc == 0), stop=(dc == 3), reads=[R_ck, R_A4], writes=[pr])
                        ins("act", "activation", out=pTc[pi][:, kk, :], in_=pt[:, :], func=AF.Exp, scale=c_scale,
                            reads=[pr], writes=[R_pTc[pi]])
                    pt, pr = next_ps()
                    for kk in range(2):
                        ins("pe", "matmul", pt[:, :], ones_b[:, :], pTc[pi][:, kk, :], start=(kk == 0), stop=(kk == 1),
                            reads=[R_const, R_pTc[pi]], writes=[pr])
                    ins("dve", "reciprocal", out=rlc[:, :], in_=pt[:, :], reads=[pr], writes=[R_rlc])
                    for dvc in range(4):
                        pt, pr = next_ps()
                        c0 = hc * CHD + dvc * 128
                        for kk in range(2):
                            ins("pe", "matmul", pt[:, :], cv[:, kk, c0:c0 + 128], pTc[pi][:, kk, :], start=(kk == 0), stop=(kk == 1),
                                reads=[R_cv, R_pTc[pi]], writes=[pr])
                        ins("dve", "tensor_tensor", out=bufB[:, hc * 4 + dvc, :], in0=pt[:, :], in1=rlc[:, :], op=ALU.mult,
                            reads=[pr, R_rlc], writes=[R_B4])
                proj_residual(bufB, R_B4, wb + 8)
                for j in range(4):
                    ins("pool", "dma_start", out=x2_t4[it * 4 + j], in_=xt[:, j, :], reads=[Rxt[j]], dma=s_xo[j])
            T.barrier()


        if phases >= 6:
          with ExitStack() as ph:
            wsl = [sb(ph, "w7_%d" % i, [128, 16, 512], BF16) for i in range(3)]
            Rw = [Res() for _ in range(3)]
            s_w = [T.new_sem("d_w7_%d" % i) for i in range(3)]
            gbh = sb(ph, "gbh", [128, FC, 32], F32)
            R_gbh = Res()
            xs = [sb(ph, "xs7_%d" % i, [128, D], BF16) for i in range(2)]
            Rxs = [Res() for _ in range(2)]
            junk = sb(ph, "junk7", [128, D], BF16)
            x2_t = X2.rearrange("(n p) d -> n p d", p=128)
            y_t = y_out.rearrange("(n p) d -> n p d", p=128)
            with ExitStack() as pre:
                xh = sb(pre, "xh", [32, D], F32)
                R_xh = Res()
                s_xh = T.new_sem("d_xh")
                hTh = sb(pre, "hTh", [128, KC, 32], BF16)
                R_hTh = Res()
                ins("dve", "memset", xh[:, :], 0.0, writes=[R_xh])
                if NT > 1:
                    X2v = X2.rearrange("(n t) d -> n t d", t=TT)
                    ins("sp", "dma_start", out=xh[0:NT - 1, :], in_=X2v[0:NT - 1, TT - 1, :], writes=[R_xh], dma=s_xh)
                    ins("sp", "dma_start", out=xh[16:16 + NT - 1, :], in_=X2v[1:NT, 0, :], writes=[R_xh], dma=s_xh)
                st_ap, Rs = next_stat()
                ins("act", "activation", out=junk[0:32, :], in_=xh[:, :], func=AF.Square, scale=float(D) ** -0.5,
                    accum_out=st_ap[0:32, 0:1], reads=[R_xh], writes=[Rs])
                ins("act", "activation", out=st_ap[0:32, 1:2], in_=st_ap[0:32, 0:1], func=AF.Sqrt, bias=eps_t[0:32, 0:1],
                    reads=[Rs, R_const], writes=[Rs])
                ins("dve", "reciprocal", out=st_ap[0:32, 2:3], in_=st_ap[0:32, 1:2], reads=[Rs], writes=[Rs])
                ins("dve", "tensor_scalar", out=xs[0][0:32, :], in0=xh[:, :], scalar1=st_ap[0:32, 2:3], scalar2=None, op0=ALU.mult,
                    reads=[R_xh, Rs], writes=[Rxs[0]])
                pt, pr = next_ps()
                ptb = pt.bitcast(BF16)
                for k in range(KC):
                    ins("pe", "transpose", out=ptb[:, k * 32:(k + 1) * 32], in_=xs[0][0:32, k * 128:(k + 1) * 128],
                        identity=ident_b[0:32, 0:32], reads=[Rxs[0], R_const], writes=[pr])
                ins("dve", "tensor_copy", out=hTh[:, :, :], in_=ptb[:, 0:KC * 32].rearrange("p (k c) -> p k c", k=KC),
                    reads=[pr], writes=[R_hTh])
                wspecs = [(WG[:, c * 512:(c + 1) * 512], 16, 512) for c in range(FC // 4)]
                WS = WStream(wsl, Rw, s_w, wspecs)
                for fc in range(FC):
                    wv, wr = WS.get(fc // 4)
                    pt, pr = next_ps()
                    for kc in range(KC):
                        ins("pe", "matmul", pt[:, 0:32], wv[:, kc, (fc % 4) * 128:(fc % 4 + 1) * 128], hTh[:, kc, :],
                            start=(kc == 0), stop=(kc == KC - 1), reads=[wr, R_hTh], writes=[pr])
                    ins("dve", "tensor_copy", out=gbh[:, fc, :], in_=pt[:, 0:32], reads=[pr], writes=[R_gbh])
                T.barrier()

            NXB = 7
            xb = [sb(ph, "xb7_%d" % i, [128, D], F32) for i in range(NXB)]
            Rxb = [Res() for _ in range(NXB)]
            s_xt = [T.new_sem("d_xt7_%d" % i) for i in range(NXB)]
            s_xo = [T.new_sem("d_xo7_%d" % i) for i in range(NXB)]
            x7_done = set()

            def load_x7(g):
                if g in x7_done or g >= 4 * NT:
                    return
                x7_done.add(g)
                ins("sp", "dma_start", out=xb[g % NXB][:, :], in_=x2_t[g], writes=[Rxb[g % NXB]], dma=s_xt[g % NXB])
            hT = sb(ph, "hT7", [128, KC, TT], BF16)
            R_hT = Res()
            aT = sb(ph, "aT", [128, FC, TT], BF16)
            R_aT = Res()
            NACC = 2
            acc = [sb(ph, "acc%d" % i, [128, TT], F32) for i in range(NACC)]
            R_acc = [Res() for _ in range(NACC)]
            sg = [sb(ph, "sg%d" % i, [128, TT], F32) for i in range(4)]
            R_sg = [Res() for _ in range(4)]
            hl = sb(ph, "hl", [128, 2, FC], F32)
            R_hl = Res()
            NQ = FC // 4
            wspecs = []
            for it in range(NT):
                for q in range(NQ):
                    wspecs.append((WG[:, q * 512:(q + 1) * 512], 16, 512))
                    wspecs.append((WU[:, q * 512:(q + 1) * 512], 16, 512))
                for n in range(4):
                    for kq in range(4):
                        wspecs.append((WD[kq * 1408:(kq + 1) * 1408, n * 512:(n + 1) * 512], 11, 512))
            WS = WStream(wsl, Rw, s_w, wspecs)
            WPT = 2 * NQ + 16
            fcnt = [0]
            for it in range(NT):
                wb = it * WPT
                if it == 0:
                    load_x7(0)
                    load_x7(1)
                norm_pipe(4, lambda j, it=it: (xb[(it * 4 + j) % NXB][:, :], Rxb[(it * 4 + j) % NXB]), xs, Rxs, junk, hT, R_hT,
                          pre_load=lambda j, it=it: load_x7(it * 4 + j), done_a=(0 if it == 0 else 2))
                if it > 0:
                    ins("dve", "tensor_tensor", out=hl[:, 0, :], in0=cwv[:, 0, :], in1=gbh[:, :, it - 1], op=ALU.mult,
                        reads=[R_const, R_gbh], writes=[R_hl])
                if it < NT - 1:
                    ins("dve", "tensor_tensor", out=hl[:, 1, :], in0=cwv[:, 2, :], in1=gbh[:, :, 16 + it], op=ALU.mult,
                        reads=[R_const, R_gbh], writes=[R_hl])
                for q in range(NQ):
                    wg, wgr = WS.get(wb + 2 * q)
                    for c in range(4):
                        fc = 4 * q + c
                        ai = fcnt[0] % NACC
                        fcnt[0] += 1
                        pg, pgr = next_ps()
                        for kc in range(KC):
                            ins("pe", "matmul", pg[:, :], wg[:, kc, c * 128:(c + 1) * 128], hT[:, kc, :],
                                start=(kc == 0), stop=(kc == KC - 1), reads=[wgr, R_hT], writes=[pgr])
                        a_ = acc[ai]
                        Ra = R_acc[ai]
                        ins("act", "activation", out=a_[:, :], in_=pg[:, :], func=AF.Identity, scale=cwv[:, 1, fc:fc + 1],
                            bias=cwv[:, 3, fc:fc + 1], reads=[pgr, R_const], writes=[Ra])
                        ins("dve", "scalar_tensor_tensor", out=a_[:, 1:TT], in0=pg[:, 0:TT - 1], scalar=cwv[:, 0, fc:fc + 1],
                            in1=a_[:, 1:TT], op0=ALU.mult, op1=ALU.add, reads=[pgr, Ra, R_const], writes=[Ra])
                        ins("dve", "scalar_tensor_tensor", out=a_[:, 0:TT - 1], in0=pg[:, 1:TT], scalar=cwv[:, 2, fc:fc + 1],
                            in1=a_[:, 0:TT - 1], op0=ALU.mult, op1=ALU.add, reads=[pgr, Ra, R_const], writes=[Ra])
                        if it > 0:
                            ins("pool", "tensor_tensor", out=a_[:, 0:1], in0=a_[:, 0:1], in1=hl[:, 0, fc:fc + 1], op=ALU.add,
                                reads=[Ra, R_hl], writes=[Ra])
                        if it < NT - 1:
                            ins("pool", "tensor_tensor", out=a_[:, TT - 1:TT], in0=a_[:, TT - 1:TT], in1=hl[:, 1, fc:fc + 1], op=ALU.add,
                                reads=[Ra, R_hl], writes=[Ra])
                        ins("act", "activation", out=sg[c][:, :], in_=a_[:, :], func=AF.Silu, reads=[Ra], writes=[R_sg[c]])
                    wu, wur = WS.get(wb + 2 * q + 1)
                    for c in range(4):
                        fc = 4 * q + c
                        pu, pur = next_ps()
                        for kc in range(KC):
                            ins("pe", "matmul", pu[:, :], wu[:, kc, c * 128:(c + 1) * 128], hT[:, kc, :],
                                start=(kc == 0), stop=(kc == KC - 1), reads=[wur, R_hT], writes=[pur])
                        ins("dve", "tensor_tensor", out=aT[:, fc, :], in0=pu[:, :], in1=sg[c][:, :], op=ALU.mult,
                            reads=[pur, R_sg[c]], writes=[R_aT])
                if it + 1 < NT:
                    for j in range(2):
                        g = (it + 1) * 4 + j
                        load_x7(g)
                        tok_norm_a(xb[g % NXB][:, :], Rxb[g % NXB], xs[g % 2], Rxs[g % 2], junk)
                for n in range(4):
                    banks = [next_ps() for _ in range(4)]
                    for kq in range(4):
                        wv, wr = WS.get(wb + 2 * NQ + n * 4 + kq)
                        for j in range(4):
                            pt, pr = banks[j]
                            for kc in range(11):
                                ins("pe", "matmul", pt[:, :], aT[:, kq * 11 + kc, j * 128:(j + 1) * 128], wv[:, kc, :],
                                    start=(kq == 0 and kc == 0), stop=(kq == 3 and kc == 10), reads=[wr, R_aT], writes=[pr])
                    for j in range(4):
                        pt, pr = banks[j]
                        bi = (it * 4 + j) % NXB
                        ins("dve", "tensor_tensor", out=xb[bi][:, n * 512:(n + 1) * 512], in0=pt[:, :],
                            in1=xb[bi][:, n * 512:(n + 1) * 512], op=ALU.add, reads=[pr, Rxb[bi]], writes=[Rxb[bi]])
                for j in range(4):
                    bi = (it * 4 + j) % NXB
                    st_ap, Rs = next_stat()
                    ins("act", "activation", out=junk[:, :], in_=xb[bi][:, :], func=AF.Square, scale=float(D) ** -0.5,
                        accum_out=st_ap[:, 0:1], reads=[Rxb[bi]], writes=[Rs])
                    ins("act", "activation", out=st_ap[:, 1:2], in_=st_ap[:, 0:1], func=AF.Sqrt, bias=eps_t[:, 0:1],
                        reads=[Rs, R_const], writes=[Rs])
                    ins("dve", "reciprocal", out=st_ap[:, 2:3], in_=st_ap[:, 1:2], reads=[Rs], writes=[Rs])
                    ins("dve", "scalar_tensor_tensor", out=xb[bi][:, :], in0=xb[bi][:, :], scalar=st_ap[:, 2:3], in1=gfin[:, :],
                        op0=ALU.mult, op1=ALU.mult, reads=[Rxb[bi], Rs, R_const], writes=[Rxb[bi]])
                    ins("pool", "dma_start", out=y_t[it * 4 + j], in_=xb[bi][:, :], reads=[Rxb[bi]], dma=s_xo[bi])
            T.barrier()


        T.barrier()
        with nc.Block() as block:
            @block.tensor
            def _(e):
                T.replay(e, "pe")

            @block.scalar
            def _(e):
                T.replay(e, "act")

            @block.vector
            def _(e):
                T.replay(e, "dve")

            @block.gpsimd
            def _(e):
                T.replay(e, "pool")

            @block.sync
            def _(e):
                T.replay(e, "sp")
    return nc


S_FULL = 8192
_NC_CACHE = {}


def kernel(x_prompt, x_sample, mem_prompt, mem_sample, **weights):
    xs = [np.asarray(x_prompt[i]) for i in range(x_prompt.shape[0])] + \
         [np.asarray(x_sample[i]) for i in range(x_sample.shape[0])]
    ms = [np.asarray(mem_prompt[i]) for i in range(mem_prompt.shape[0])] + \
         [np.asarray(mem_sample[i]) for i in range(mem_sample.shape[0])]
    nseq = len(xs)
    S = xs[0].shape[0]
    if S not in _NC_CACHE:
        _NC_CACHE[S] = build(S)
    nc = _NC_CACHE[S]
    shared = {}
    for name, shp in WEIGHT_SPECS:
        shared[name] = np.ascontiguousarray(np.asarray(weights[name], dtype=np.float32).reshape(shp))
    shared.update(host_constants(S))
    in_maps = []
    core_of_seq = [0, 1, 2, 4, 5, 6]
    seq_of_core = {c: i for i, c in enumerate(core_of_seq)}
    zx = np.zeros_like(np.ascontiguousarray(xs[0], dtype=np.float32))
    zm = np.zeros_like(np.ascontiguousarray(ms[0], dtype=np.float32))
    for c in range(8):
        m = dict(shared)
        if c in seq_of_core:
            m["x"] = np.ascontiguousarray(xs[seq_of_core[c]], dtype=np.float32)
            m["mem"] = np.ascontiguousarray(ms[seq_of_core[c]], dtype=np.float32)
        else:
            m["x"] = zx
            m["mem"] = zm
        in_maps.append(m)
    res = run_bass_kernel_spmd(nc, in_maps, core_ids=list(range(8)))
    ys = [np.asarray(res.results[core_of_seq[i]]["y"], dtype=np.float32) for i in range(nseq)]
    nb = x_prompt.shape[0]
    y_prompt = np.stack(ys[:nb], axis=0)
    y_sample = np.stack(ys[nb:], axis=0)
    return (y_prompt, y_sample)
```

```python
import math
from contextlib import ExitStack

import numpy as np
import ml_dtypes
import concourse.bass as bass
import concourse.mybir as mybir
from concourse.bass_utils import run_bass_kernel_spmd

F32 = mybir.dt.float32
BF16 = mybir.dt.bfloat16
AF = mybir.ActivationFunctionType
ALU = mybir.AluOpType

D = 2048
NF = 1024
NG = 4
GC = 256
QL = 512
KVL = 512
NH = 8
DN = 128
DR = 64
DV = 128
NMEM = 256
NCH = 4
CHD = 512
DFF = 5632
FC = DFF // 128
EPS = 1e-6
KC = D // 128
TT = 512


class Sem:
    __slots__ = ("h", "total", "is_dma", "name")

    def __init__(self, h, is_dma, name):
        self.h = h
        self.total = 0
        self.is_dma = is_dma
        self.name = name


class Res:
    __slots__ = ("w", "r")

    def __init__(self):
        self.w = None
        self.r = {}


class Stream:
    def __init__(self, name):
        self.name = name
        self.sem = None
        self.ops = []
        self.known = {}


class Tracker:
    def __init__(self, nc, es):
        self.nc = nc
        self.es = es
        self.sems = []
        self.streams = {}
        for name in ("pe", "act", "dve", "pool", "sp"):
            st = Stream(name)
            if name != "sp":
                st.sem = self.new_sem("c_" + name, False)
            self.streams[name] = st
        self.n_ins = 0

    def new_sem(self, name, is_dma=True):
        h = self.es.enter_context(self.nc.semaphore(name))
        s = Sem(h, is_dma, name)
        self.sems.append(s)
        return s

    def _wait(self, st, ev):
        sem, val = ev
        if sem.is_dma:
            val = sem.total
        if st.known.get(sem, 0) >= val:
            return
        st.known[sem] = val
        st.ops.append(("w", sem.h, val))

    def op(self, stname, fn, reads=(), writes=(), dma=None):
        st = self.streams[stname]
        for r in reads:
            if r.w is not None:
                if not (stname == "pe" and r.w[0] is st.sem):
                    self._wait(st, r.w)
        for w in writes:
            if w.w is not None:
                if not (stname == "pe" and w.w[0] is st.sem):
                    self._wait(st, w.w)
            for ev in w.r.values():
                if not (stname == "pe" and ev[0] is st.sem):
                    self._wait(st, ev)
        if dma is not None:
            dma.total += 16
            ev = (dma, dma.total)
            st.ops.append(("i", fn, dma.h, 16))
        else:
            st.sem.total += 1
            ev = (st.sem, st.sem.total)
            st.ops.append(("i", fn, st.sem.h, 1))
        key = ev[0]
        for r in reads:
            r.r[key] = ev
        for w in writes:
            w.w = ev
            w.r = {}
        self.n_ins += 1

    def barrier(self):
        for st in self.streams.values():
            for sem in self.sems:
                if sem.total > 0:
                    self._wait(st, (sem, sem.total))

    def ins(self, stname, method, *args, reads=(), writes=(), dma=None, **kw):
        self.op(stname, (method, args, kw), reads, writes, dma)

    def replay(self, eng, stname):
        for o in self.streams[stname].ops:
            if o[0] == "w":
                eng.wait_ge(o[1], o[2])
            else:
                m, a, kw = o[1]
                getattr(eng, m)(*a, **kw).then_inc(o[2], o[3])


def host_constants(S):
    N1 = 128
    N2 = S // 128
    c = {}
    c["c_ident"] = np.eye(128, dtype=np.float32)
    j = np.arange(GC)
    ang = 2.0 * np.pi * np.outer(j, j) / GC
    c["c_cdft"] = (np.concatenate([np.cos(ang), -np.sin(ang)], axis=1) / math.sqrt(GC)).astype(np.float32)
    s1 = np.arange(N1)[:, None, None].astype(np.float64)
    s2 = np.arange(N2)[None, :, None].astype(np.float64)
    k1 = np.arange(N1)[None, None, :].astype(np.float64)
    th = 2.0 * np.pi * (s1 * k1 / N1 + s2 * k1 / S)
    far = np.cos(th) / math.sqrt(N1)
    fai = -np.sin(th) / math.sqrt(N1)
    c["c_fa"] = np.stack([far, fai, -fai], axis=2).astype(np.float32).astype(ml_dtypes.bfloat16)
    s2b = np.arange(N2)[:, None].astype(np.float64)
    k2 = np.arange(N2)[None, :].astype(np.float64)
    ph = 2.0 * np.pi * s2b * k2 / N2
    c["c_fb"] = (np.concatenate([np.cos(ph), np.sin(ph)], axis=0) / math.sqrt(N2)).astype(np.float32).astype(ml_dtypes.bfloat16)
    inv = 10000.0 ** (-np.arange(0, DR, 2, dtype=np.float32) / DR)
    a = np.arange(S, dtype=np.float32)[None, :] * inv[:, None].astype(np.float32)
    c["c_cos"] = np.concatenate([np.cos(a), np.cos(a)], axis=0).astype(np.float32)
    c["c_sin"] = np.concatenate([np.sin(a), np.sin(a)], axis=0).astype(np.float32)
    return c


WEIGHT_SPECS = [
    ("norm_mix_g", [D]), ("w_in", [D, 2112]), ("q_norm_g", [QL]), ("w_uq", [QL, 1536]),
    ("kv_norm_g", [KVL]), ("w_ukv", [KVL, 2048]), ("fourier_out_g", [NF]), ("mla_out_g", [1024]),
    ("w_out", [D, D]), ("norm_cross_g", [D]), ("norm_mem_g", [D]), ("w_cq", [D, D]),
    ("w_ckv", [D, 2 * D]), ("w_co", [D, D]), ("norm_ffn_g", [D]), ("w_gate", [D, DFF]),
    ("w_up", [D, DFF]), ("conv_w", [3, DFF]), ("conv_b", [DFF]), ("w_down", [DFF, D]),
    ("final_norm_g", [D]),
]


def build(S, dbg=False, phases=99):
    N2 = S // 128
    NT = S // TT
    nc = bass.Bass("TRN2", target_bir_lowering=False)
    I = {}
    I["x"] = nc.dram_tensor("x", [S, D], F32, kind="ExternalInput").ap()
    I["mem"] = nc.dram_tensor("mem", [NMEM, D], F32, kind="ExternalInput").ap()
    for name, shp in WEIGHT_SPECS:
        I[name] = nc.dram_tensor(name, shp, F32, kind="ExternalInput").ap()
    cshapes = {"c_ident": [128, 128], "c_cdft": [GC, 2 * GC], "c_fa": [128, N2, 3, 128],
               "c_fb": [2 * N2, N2], "c_cos": [DR, S], "c_sin": [DR, S]}
    for name, shp in cshapes.items():
        I[name] = nc.dram_tensor(name, shp, BF16 if name in ("c_fa", "c_fb") else F32, kind="ExternalInput").ap()
    y_out = nc.dram_tensor("y", [S, D], F32, kind="ExternalOutput").ap()

    skind = "ExternalOutput" if dbg else "Internal"

    def scratch(name, shape, dt):
        return nc.dram_tensor(name, shape, dt, kind=skind).ap()

    WIN = scratch("s_win", [D, 2176], BF16)
    WUQ = scratch("s_wuq", [QL, 2048], BF16)
    WUKV = scratch("s_wukv", [KVL, 2048], BF16)
    WOUT = scratch("s_wout", [D, D], BF16)
    WCQ = scratch("s_wcq", [D, D], BF16)
    WCKV = scratch("s_wckv", [D, 2 * D], BF16)
    WCO = scratch("s_wco", [D, D], BF16)
    WG = scratch("s_wg", [D, DFF], BF16)
    WU = scratch("s_wu", [D, DFF], BF16)
    WD = scratch("s_wd", [DFF, D], BF16)
    VS = scratch("s_vs", [S, NG, 2, GC], BF16)
    YS = scratch("s_ys", [NG, 2, N2, 128, GC], BF16)
    FS = scratch("s_fs", [S, NF], F32)
    QN = scratch("s_qn", [NH, DN, S], BF16)
    QR = scratch("s_qr", [NH, DR, S], BF16)
    KN = scratch("s_kn", [NH, DN, S], BF16)
    KR = scratch("s_kr", [DR, S], BF16)
    VV = scratch("s_vv", [NH, 128, S // 128, DV], BF16)
    OT = scratch("s_ot", [NH * DV, S], F32)
    X2 = scratch("s_x2", [S, D], F32)

    es = ExitStack()
    with es:
        T = Tracker(nc, es)
        op = T.op
        ins = T.ins

        def sb(stack, name, shape, dt):
            return stack.enter_context(nc.sbuf_tensor(name, shape, dt))

        PS = [es.enter_context(nc.psum_tensor("psb%d" % i, [128, 512], F32)) for i in range(8)]
        PSR = [Res() for _ in range(8)]
        ps_i = [0]

        def next_ps():
            i = ps_i[0] % 8
            ps_i[0] += 1
            return PS[i], PSR[i]

        ident_f = sb(es, "ident_f", [128, 128], F32)
        ident_b = sb(es, "ident_b", [128, 128], BF16)
        ones_b = sb(es, "ones_b", [128, 128], BF16)
        o512_b = sb(es, "o512_b", [128, 128], BF16)
        o1024_b = sb(es, "o1024_b", [128, 128], BF16)
        gvec = sb(es, "gvec", [128, 88], F32)
        cwv = sb(es, "cwv", [128, 4, FC], F32)
        gfin = sb(es, "gfin", [128, D], F32)
        R_const = Res()
        s_const = T.new_sem("d_const")

        G_OFF = {"norm_mix_g": 0, "q_norm_g": 16, "kv_norm_g": 20, "fourier_out_g": 24, "mla_out_g": 32,
                 "norm_cross_g": 40, "norm_mem_g": 56, "norm_ffn_g": 72}

        with ExitStack() as ph:
            va = sb(ph, "va", [88, 128], F32)
            vb = sb(ph, "vb", [88, 128], F32)
            vc = sb(ph, "vc", [88, 128], F32)
            R_v = Res()
            ins("sp", "dma_start", out=ident_f[:], in_=I["c_ident"], writes=[R_const], dma=s_const)
            for name, off in G_OFF.items():
                n = I[name].shape[0] // 128
                src = I[name].rearrange("(k p) -> k p", p=128)
                ins("sp", "dma_start", out=va[off:off + n, :], in_=src,
                   writes=[R_v], dma=s_const)
            for j in range(3):
                dst = (vb, vb, vc)[j]
                o = (0, 44, 0)[j]
                src = I["conv_w"][j].rearrange("(k p) -> k p", p=128)
                ins("sp", "dma_start", out=dst[o:o + 44, :], in_=src,
                   writes=[R_v], dma=s_const)
            src = I["conv_b"].rearrange("(k p) -> k p", p=128)
            ins("sp", "dma_start", out=vc[44:88, :], in_=src, writes=[R_v], dma=s_const)
            ins("sp", "dma_start", out=gfin[:], in_=I["final_norm_g"].partition_broadcast(128),
               writes=[R_const], dma=s_const)
            ins("dve", "tensor_copy", out=ident_b[:], in_=ident_f[:], reads=[R_const], writes=[R_const])
            ins("dve", "memset", ones_b[:], 1.0, writes=[R_const])
            ins("dve", "memset", o512_b[:], 1.0 / 512, writes=[R_const])
            ins("dve", "memset", o1024_b[:], 1.0 / 1024, writes=[R_const])
            for src_t, dst_ap in ((va, gvec[:, :]), (vb, cwv[:, 0:2, :]), (vc, cwv[:, 2:4, :])):
                pt, pr = next_ps()
                ins("pe", "transpose", out=pt[:, 0:88], in_=src_t[:, :], identity=ident_f[0:88, 0:88],
                   reads=[R_v, R_const], writes=[pr])
                if dst_ap.ndim == 3:
                    srcv = pt[:, 0:88].rearrange("p (a b) -> p a b", a=2)
                else:
                    srcv = pt[:, 0:88]
                ins("dve", "tensor_copy", out=dst_ap, in_=srcv,
                   reads=[pr], writes=[R_const])
            T.barrier()

        deferred = []
        with ExitStack() as ph:
            NSL = 3
            stf = [sb(ph, "stf%d" % i, [128, 2176], F32) for i in range(NSL)]
            stb = [sb(ph, "stb%d" % i, [128, 2176], BF16) for i in range(NSL)]
            Rf = [Res() for _ in range(NSL)]
            Rb = [Res() for _ in range(NSL)]
            s_ld = [T.new_sem("d_p0l%d" % i) for i in range(NSL)]
            s_st = [T.new_sem("d_p0s%d" % i) for i in range(NSL)]
            cnt = [0]

            def conv_block(src_ap, dst_ap, g_ap, pieces, kc_i):
                i = cnt[0] % NSL
                ceng = ("act", "dve")[cnt[0] % 2]
                cnt[0] += 1
                ws = src_ap.shape[1]
                wd = dst_ap.shape[1]
                ins("sp", "dma_start", out=stf[i][:, 0:ws], in_=src_ap, writes=[Rf[i]], dma=s_ld[i])
                for (dv, sv, sign) in pieces:
                    o_ap = dv(stb[i])
                    i_ap = sv(stf[i])
                    if sign < 0 or ceng == "dve":
                        if g_ap is None:
                            ins("dve", "tensor_scalar",
                                out=o_ap, in0=i_ap, scalar1=float(sign), scalar2=None, op0=ALU.mult,
                                reads=[Rf[i], R_const], writes=[Rb[i]])
                        else:
                            ins("dve", "tensor_scalar",
                                out=o_ap, in0=i_ap, scalar1=g_ap, scalar2=float(sign), op0=ALU.mult, op1=ALU.mult,
                                reads=[Rf[i], R_const], writes=[Rb[i]])
                    else:
                        if g_ap is None:
                            ins("act", "copy", out=o_ap, in_=i_ap,
                               reads=[Rf[i], R_const], writes=[Rb[i]])
                        else:
                            ins("act", "mul", out=o_ap, in_=i_ap, mul=g_ap,
                               reads=[Rf[i], R_const], writes=[Rb[i]])
                ins("pool", "dma_start", out=dst_ap, in_=stb[i][:, 0:wd], reads=[Rb[i]], dma=s_st[i])

            def simple(src, dst, gname, goff2=0):
                din, dout = src.shape
                for kc in range(din // 128):
                    g_ap = None
                    if gname is not None:
                        col = G_OFF[gname] + kc - goff2
                        g_ap = gvec[:, col:col + 1]
                    for c0 in range(0, dout, 2048):
                        w = min(2048, dout - c0)
                        deferred.append((src[kc * 128:(kc + 1) * 128, c0:c0 + w], dst[kc * 128:(kc + 1) * 128, c0:c0 + w], g_ap, w))

            for kc in range(KC):
                g_ap = gvec[:, kc:kc + 1]
                conv_block(I["w_in"][kc * 128:(kc + 1) * 128, :], WIN[kc * 128:(kc + 1) * 128, :], g_ap,
                           [(lambda t: t[:, 0:2112], lambda t: t[:, 0:2112], 1),
                            (lambda t: t[:, 2112:2144], lambda t: t[:, 2080:2112], -1),
                            (lambda t: t[:, 2144:2176], lambda t: t[:, 2048:2080], 1)], kc)
            for kc in range(QL // 128):
                g_ap = gvec[:, 16 + kc:17 + kc]

                def sv(t, a, b):
                    return t[:, 0:1536].rearrange("p (h c) -> p h c", c=192)[:, :, a:b]

                conv_block(I["w_uq"][kc * 128:(kc + 1) * 128, :], WUQ[kc * 128:(kc + 1) * 128, :], g_ap,
                           [(lambda t: t[:, 0:1024].rearrange("p (h c) -> p h c", c=128), lambda t: sv(t, 0, 128), 1),
                            (lambda t: t[:, 1024:1536].rearrange("p (h c) -> p h c", c=64), lambda t: sv(t, 128, 192), 1),
                            (lambda t: t[:, 1536:2048].rearrange("p (h c) -> p h c", c=64)[:, :, 0:32], lambda t: sv(t, 160, 192), -1),
                            (lambda t: t[:, 1536:2048].rearrange("p (h c) -> p h c", c=64)[:, :, 32:64], lambda t: sv(t, 128, 160), 1)],
                           kc)
            for kc in range(KVL // 128):
                g_ap = gvec[:, 20 + kc:21 + kc]

                def sv2(t, a, b):
                    return t[:, 0:2048].rearrange("p (h c) -> p h c", c=256)[:, :, a:b]

                conv_block(I["w_ukv"][kc * 128:(kc + 1) * 128, :], WUKV[kc * 128:(kc + 1) * 128, :], g_ap,
                           [(lambda t: t[:, 0:1024].rearrange("p (h c) -> p h c", c=128), lambda t: sv2(t, 0, 128), 1),
                            (lambda t: t[:, 1024:2048].rearrange("p (h c) -> p h c", c=128), lambda t: sv2(t, 128, 256), 1)],
                           kc)
            if phases >= 4:
                simple(I["w_out"], WOUT, "fourier_out_g")
                simple(I["w_cq"], WCQ, "norm_cross_g")
                simple(I["w_ckv"], WCKV, "norm_mem_g")
                simple(I["w_co"], WCO, None)
            if phases >= 6:
                simple(I["w_gate"], WG, "norm_ffn_g")
                simple(I["w_up"], WU, "norm_ffn_g")
                simple(I["w_down"], WD, None)
            T.barrier()


        def load_w(stack_slot, Rslot, ssem, dram_ap, nkc, wcols):
            flat = stack_slot[:].rearrange("p a b -> p (a b)")
            dst = flat[:, 0:nkc * wcols].rearrange("p (k c) -> p k c", k=nkc)
            src = dram_ap.rearrange("(k p) c -> p k c", p=128)
            ins("sp", "dma_start", out=dst, in_=src, writes=[Rslot], dma=ssem)
            return dst


        def make_deferred_emitter(stack, tag, ceng):
            NS2 = 2
            dstf = [sb(stack, "dstf%s%d" % (tag, i), [128, 2048], F32) for i in range(NS2)]
            dstb = [sb(stack, "dstb%s%d" % (tag, i), [128, 2048], BF16) for i in range(NS2)]
            dRf = [Res() for _ in range(NS2)]
            dRb = [Res() for _ in range(NS2)]
            d_ld = [T.new_sem("d_dl%s%d" % (tag, i)) for i in range(NS2)]
            d_st = [T.new_sem("d_ds%s%d" % (tag, i)) for i in range(NS2)]
            cnt = [0]

            def emit(n):
                for _ in range(n):
                    if not deferred:
                        return
                    src_ap, dst_ap, g_ap, w = deferred.pop(0)
                    i = cnt[0] % NS2
                    cnt[0] += 1
                    ins("sp", "dma_start", out=dstf[i][:, 0:w], in_=src_ap, writes=[dRf[i]], dma=d_ld[i])
                    sc1 = g_ap if g_ap is not None else 1.0
                    if ceng == "act":
                        if g_ap is None:
                            ins("act", "copy", out=dstb[i][:, 0:w], in_=dstf[i][:, 0:w], reads=[dRf[i]], writes=[dRb[i]])
                        else:
                            ins("act", "mul", out=dstb[i][:, 0:w], in_=dstf[i][:, 0:w], mul=g_ap,
                                reads=[dRf[i], R_const], writes=[dRb[i]])
                    else:
                        ins(ceng, "tensor_scalar", out=dstb[i][:, 0:w], in0=dstf[i][:, 0:w], scalar1=sc1, scalar2=1.0,
                            op0=ALU.mult, op1=ALU.mult, reads=[dRf[i], R_const], writes=[dRb[i]])
                    ins("pool", "dma_start", out=dst_ap, in_=dstb[i][:, 0:w], reads=[dRb[i]], dma=d_st[i])
            return emit

        class WStream:
            def __init__(self, slots, Rs, sems, specs):
                self.slots, self.Rs, self.sems, self.specs = slots, Rs, sems, specs
                self.nxt = 0
                self.views = {}

            def get(self, k):
                n = len(self.slots)
                while self.nxt <= min(k + n - 1, len(self.specs) - 1):
                    j = self.nxt
                    ap, nkc, wc = self.specs[j]
                    self.views[j] = load_w(self.slots[j % n], self.Rs[j % n], self.sems[j % n], ap, nkc, wc)
                    self.nxt += 1
                return self.views[k], self.Rs[k % n]

        stat = sb(es, "stat", [128, 64], F32)
        Rstat = [Res() for _ in range(16)]
        stat_i = [0]

        def next_stat():
            i = stat_i[0] % 16
            stat_i[0] += 1
            return stat[:, 4 * i:4 * i + 4], Rstat[i]

        def tok_norm_a(x_ap, Rx, xs_t, Rxs, junk_t, width=D):
            st_ap, Rs = next_stat()
            ins("act", "activation", out=junk_t[:, 0:width], in_=x_ap, func=AF.Square,
                scale=float(width) ** -0.5, accum_out=st_ap[:, 0:1], reads=[Rx], writes=[Rs])
            ins("act", "activation", out=st_ap[:, 1:2], in_=st_ap[:, 0:1], func=AF.Sqrt, bias=eps_t[:, 0:1],
                reads=[Rs, R_const], writes=[Rs])
            ins("dve", "reciprocal", out=st_ap[:, 2:3], in_=st_ap[:, 1:2], reads=[Rs], writes=[Rs])
            ins("dve", "tensor_scalar", out=xs_t[:, 0:width], in0=x_ap, scalar1=st_ap[:, 2:3], scalar2=None,
                op0=ALU.mult, reads=[Rx, Rs], writes=[Rxs])

        def tok_norm_b(xs_t, Rxs, hT_t, RhT, j, width=D):
            nk = width // 128
            for g0 in range(0, nk, 8):
                gn = min(8, nk - g0)
                pt, pr = next_ps()
                ptb = pt.bitcast(BF16)
                for k in range(gn):
                    ins("pe", "transpose", out=ptb[:, k * 128:(k + 1) * 128],
                        in_=xs_t[:, (g0 + k) * 128:(g0 + k + 1) * 128], identity=ident_b[:, :],
                        reads=[Rxs, R_const], writes=[pr])
                src = ptb[:, 0:gn * 128].rearrange("p (k c) -> p k c", k=gn)
                dst = hT_t[:, g0:g0 + gn, j * 128:(j + 1) * 128]
                if (g0 // 8) % 2 == 0:
                    ins("act", "copy", out=dst, in_=src, reads=[pr], writes=[RhT])
                else:
                    ins("dve", "tensor_copy", out=dst, in_=src, reads=[pr], writes=[RhT])

        def tok_norm_T(x_ap, Rx, xs_t, Rxs, junk_t, hT_t, RhT, j, width=D):
            tok_norm_a(x_ap, Rx, xs_t, Rxs, junk_t, width)
            tok_norm_b(xs_t, Rxs, hT_t, RhT, j, width)

        def norm_pipe(n, get_x, xs_l, Rxs_l, junk_t, hT_t, RhT, width=D, pre_load=None, done_a=0):
            for j in range(n + 1):
                if j < n and j >= done_a:
                    if pre_load is not None:
                        pre_load(j)
                    x_ap, Rx = get_x(j)
                    tok_norm_a(x_ap, Rx, xs_l[j % 2], Rxs_l[j % 2], junk_t, width)
                if j >= 1:
                    tok_norm_b(xs_l[(j - 1) % 2], Rxs_l[(j - 1) % 2], hT_t, RhT, j - 1, width)

        eps_t = sb(es, "eps_t", [128, 1], F32)
        ins("dve", "memset", eps_t[:], EPS, writes=[R_const])

        def fm_rstd(src_chunks, Rsrc, sq_t, Rsq, ones_t, rstd_t, Rrstd, n):
            nchunks = len(src_chunks)
            for c, a in enumerate(src_chunks):
                ins("act", "activation", out=sq_t[:, c, 0:n], in_=a, func=AF.Square,
                   reads=[Rsrc], writes=[Rsq])
            pt, pr = next_ps()
            for c in range(nchunks):
                ins("pe", "matmul", pt[:, 0:n], ones_t[:, :], sq_t[:, c, 0:n], start=(c == 0),
                                                 stop=(c == nchunks - 1), reads=[Rsq, R_const], writes=[pr])
            ins("act", "activation", out=rstd_t[:, 0:n], in_=pt[:, 0:n], func=AF.Sqrt, bias=eps_t[:, 0:1],
               reads=[pr, R_const], writes=[Rrstd])
            ins("dve", "reciprocal", out=rstd_t[:, 0:n], in_=rstd_t[:, 0:n], reads=[Rrstd], writes=[Rrstd])

        with ExitStack() as ph:
            wsl = [sb(ph, "w1_%d" % i, [128, 16, 512], BF16) for i in range(3)]
            Rw = [Res() for _ in range(3)]
            s_w = [T.new_sem("d_w1_%d" % i) for i in range(3)]
            wspecs = []
            for it in range(NT):
                for c in range(4):
                    wspecs.append((WIN[:, c * 512:(c + 1) * 512], 16, 512))
                wspecs.append((WIN[:, 2048:2176], 16, 128))
                wspecs.append((WUQ[:, :], 4, 2048))
                wspecs.append((WUKV[:, :], 4, 2048))
            WS = WStream(wsl, Rw, s_w, wspecs)

            cdft_f = sb(ph, "cdft_f", [128, 2, 512], F32)
            cdft_b = sb(ph, "cdft_b", [128, 2, 512], BF16)
            R_cd = Res()
            ins("sp", "dma_start", out=cdft_f[:], in_=I["c_cdft"].rearrange("(k p) c -> p k c", p=128),
               writes=[R_cd], dma=s_const)
            ins("dve", "tensor_copy", out=cdft_b[:], in_=cdft_f[:], reads=[R_cd], writes=[R_cd])
            xt = [sb(ph, "xt%d" % i, [128, D], F32) for i in range(2)]
            Rxt = [Res() for _ in range(2)]
            s_xt = [T.new_sem("d_xt%d" % i) for i in range(2)]
            xs = [sb(ph, "xs%d" % i, [128, D], BF16) for i in range(2)]
            Rxs = [Res() for _ in range(2)]
            junk = sb(ph, "junk", [128, D], BF16)
            hT = sb(ph, "hT", [128, KC, TT], BF16)
            RhT = Res()
            fT = sb(ph, "fT", [128, 8, TT], BF16)
            RfT = Res()
            vst = [sb(ph, "vst%d" % i, [128, 2048], BF16) for i in range(2)]
            Rvst = [Res() for _ in range(2)]
            s_vst = [T.new_sem("d_vst%d" % i) for i in range(2)]
            cset = []
            for nm in ("q", "kv"):
                cset.append(dict(
                    c=sb(ph, "c_%s" % nm, [128, 4, TT], F32), Rc=Res(),
                    sq=sb(ph, "sq_%s" % nm, [128, 4, TT], BF16), Rsq=Res(),
                    rstd=sb(ph, "rstd_%s" % nm, [128, TT], F32), Rrstd=Res(),
                    cn=sb(ph, "cn_%s" % nm, [128, 4, TT], BF16), Rcn=Res()))
            qk_o = sb(ph, "qk_o", [128, NH, TT], BF16)
            Rqk_o = Res()
            s_qk_o = T.new_sem("d_qk_o")
            qr_o = sb(ph, "qr_o", [64, NH, TT], BF16)
            Rqr_o = Res()
            s_qr_o = T.new_sem("d_qr_o")
            v_o = sb(ph, "v_o", [128, NH, 4, DV], BF16)
            Rv_o = Res()
            s_v_o = T.new_sem("d_v_o")
            kr_o = sb(ph, "kr_o", [64, TT], BF16)
            Rkr_o = Res()
            s_kr_o = T.new_sem("d_kr_o")
            cs_t = sb(ph, "cs_t", [64, 2, TT], F32)
            Rcs = Res()
            s_cs = T.new_sem("d_cs")
            rt = sb(ph, "rt", [64, 2, TT], F32)
            Rrt = Res()

            x_tiled = I["x"].rearrange("(n p) d -> n p d", p=128)
            xload_done = {}

            def load_x(g):
                if g in xload_done or g >= S // 128:
                    return
                xload_done[g] = True
                ins("sp", "dma_start", out=xt[g % 2][:, :], in_=x_tiled[g], writes=[Rxt[g % 2]], dma=s_xt[g % 2])

            def rope_combine(p_plain, pr_plain, p_rot, pr_rot, out_ap, Rout):
                ins("dve", "tensor_tensor", out=rt[:, 0, :], in0=p_plain[0:64, :], in1=cs_t[:, 0, :], op=ALU.mult,
                   reads=[pr_plain, Rcs], writes=[Rrt])
                ins("dve", "tensor_tensor", out=rt[:, 1, :], in0=p_rot[0:64, :], in1=cs_t[:, 1, :], op=ALU.mult,
                   reads=[pr_rot, Rcs], writes=[Rrt])
                ins("dve", "tensor_tensor", out=out_ap, in0=rt[:, 0, :], in1=rt[:, 1, :], op=ALU.add,
                   reads=[Rrt], writes=[Rout])

            evq = [0]

            def evac(dst, src, pr, Rdst):
                evq[0] += 1
                if evq[0] % 2 == 0:
                    ins("act", "copy", out=dst, in_=src, reads=[pr], writes=[Rdst])
                else:
                    ins("dve", "tensor_copy", out=dst, in_=src, reads=[pr], writes=[Rdst])

            for it in range(NT):
                t0 = it * TT
                ins("sp", "dma_start", out=cs_t[:, 0, :], in_=I["c_cos"][:, t0:t0 + TT], writes=[Rcs], dma=s_cs)
                ins("sp", "dma_start", out=cs_t[:, 1, :], in_=I["c_sin"][:, t0:t0 + TT], writes=[Rcs], dma=s_cs)
                if it == 0:
                    load_x(0)
                    load_x(1)
                norm_pipe(4, lambda j, it=it: (xt[(it * 4 + j) % 2][:, :], Rxt[(it * 4 + j) % 2]), xs, Rxs, junk, hT, RhT,
                          pre_load=lambda j, it=it: load_x(it * 4 + j), done_a=(0 if it == 0 else 2))
                wb = it * 7
                for m in range(8):
                    wv, wr = WS.get(wb + m // 4)
                    pt, pr = next_ps()
                    for kc in range(KC):
                        ins("pe", "matmul",
                            pt[:, :], wv[:, kc, (m % 4) * 128:(m % 4 + 1) * 128], hT[:, kc, :], start=(kc == 0), stop=(kc == KC - 1),
                            reads=[wr, RhT], writes=[pr])
                    evac(fT[:, m, :], pt[:, :], pr, RfT)
                for j in range(4):
                    g = it * 4 + j
                    vs_t, vs_r, vs_s = vst[g % 2], Rvst[g % 2], s_vst[g % 2]
                    for gi in range(NG):
                        pt, pr = next_ps()
                        for kc in range(2):
                            ins("pe", "matmul",
                                pt[:, :], fT[:, 2 * gi + kc, j * 128:(j + 1) * 128], cdft_b[:, kc, :], start=(kc == 0), stop=(kc == 1),
                                reads=[RfT, R_cd], writes=[pr])
                        evac(vs_t[:, gi * 512:(gi + 1) * 512], pt[:, :], pr, vs_r)
                    dst = VS[g * 128:(g + 1) * 128].rearrange("t g r c -> t (g r c)")
                    ins("pool", "dma_start", out=dst, in_=vs_t[:, :], reads=[vs_r], dma=vs_s)
                if it + 1 < NT:
                    for j in range(2):
                        g = (it + 1) * 4 + j
                        load_x(g)
                        tok_norm_a(xt[g % 2][:, :], Rxt[g % 2], xs[g % 2], Rxs[g % 2], junk)
                for ci in range(2):
                    cs_ = cset[ci]
                    wv, wr = WS.get(wb + 2 + ci)
                    for c in range(4):
                        pt, pr = next_ps()
                        for kc in range(KC):
                            ins("pe", "matmul",
                                pt[:, :], wv[:, kc, c * 128:(c + 1) * 128], hT[:, kc, :], start=(kc == 0), stop=(kc == KC - 1),
                                reads=[wr, RhT], writes=[pr])
                        ins("dve", "tensor_copy", out=cs_["c"][:, c, :], in_=pt[:, :],
                           reads=[pr], writes=[cs_["Rc"]])
                    fm_rstd([cs_["c"][:, c, :] for c in range(4)], cs_["Rc"], cs_["sq"], cs_["Rsq"], o512_b,
                            cs_["rstd"], cs_["Rrstd"], TT)
                    for c in range(4):
                        ins("dve", "tensor_tensor", out=cs_["cn"][:, c, :], in0=cs_["c"][:, c, :],
                                                                          in1=cs_["rstd"][:, :], op=ALU.mult,
                           reads=[cs_["Rc"], cs_["Rrstd"]], writes=[cs_["Rcn"]])
                wk, r_ = WS.get(wb + 4)
                pp = []
                for half in range(2):
                    pt, pr = next_ps()
                    for kc in range(KC):
                        ins("pe", "matmul",
                            pt[0:64, :], wk[:, kc, half * 64:(half + 1) * 64], hT[:, kc, :], start=(kc == 0), stop=(kc == KC - 1),
                            reads=[r_, RhT], writes=[pr])
                    pp.append((pt, pr))
                rope_combine(pp[0][0], pp[0][1], pp[1][0], pp[1][1], kr_o[:, :], Rkr_o)
                ins("pool", "dma_start", out=KR[:, t0:t0 + TT], in_=kr_o[:, :], reads=[Rkr_o], dma=s_kr_o)
                wq, r_ = WS.get(wb + 5)
                cn = cset[0]["cn"]
                Rcn = cset[0]["Rcn"]
                for h in range(NH):
                    pt, pr = next_ps()
                    for kc in range(4):
                        ins("pe", "matmul",
                            pt[:, :], wq[:, kc, h * 128:(h + 1) * 128], cn[:, kc, :], start=(kc == 0), stop=(kc == 3),
                            reads=[r_, Rcn], writes=[pr])
                    evac(qk_o[:, h, :], pt[:, :], pr, Rqk_o)
                ins("pool", "dma_start", out=QN[:, :, t0:t0 + TT].rearrange("h d t -> d h t"), in_=qk_o[:, :, :],
                   reads=[Rqk_o], dma=s_qk_o)
                for h in range(NH):
                    pp = []
                    for half in range(2):
                        pt, pr = next_ps()
                        c0 = 1024 + half * 512 + h * 64
                        for kc in range(4):
                            ins("pe", "matmul",
                                pt[0:64, :], wq[:, kc, c0:c0 + 64], cn[:, kc, :], start=(kc == 0), stop=(kc == 3),
                                reads=[r_, Rcn], writes=[pr])
                        pp.append((pt, pr))
                    rope_combine(pp[0][0], pp[0][1], pp[1][0], pp[1][1], qr_o[:, h, :], Rqr_o)
                ins("pool", "dma_start", out=QR[:, :, t0:t0 + TT].rearrange("h d t -> d h t"), in_=qr_o[:, :, :],
                   reads=[Rqr_o], dma=s_qr_o)
                wkv, r_ = WS.get(wb + 6)
                cn = cset[1]["cn"]
                Rcn = cset[1]["Rcn"]
                for h in range(NH):
                    pt, pr = next_ps()
                    for kc in range(4):
                        ins("pe", "matmul",
                            pt[:, :], wkv[:, kc, h * 128:(h + 1) * 128], cn[:, kc, :], start=(kc == 0), stop=(kc == 3),
                            reads=[r_, Rcn], writes=[pr])
                    evac(qk_o[:, h, :], pt[:, :], pr, Rqk_o)
                ins("pool", "dma_start", out=KN[:, :, t0:t0 + TT].rearrange("h d t -> d h t"), in_=qk_o[:, :, :],
                   reads=[Rqk_o], dma=s_qk_o)
                for j in range(4):
                    for n in range(2):
                        pt, pr = next_ps()
                        for kc in range(4):
                            ins("pe", "matmul",
                                pt[:, :], cn[:, kc, j * 128:(j + 1) * 128], wkv[:, kc, 1024 + n * 512:1024 + (n + 1) * 512],
                                start=(kc == 0), stop=(kc == 3), reads=[r_, Rcn], writes=[pr])
                        evac(v_o[:, 4 * n:4 * n + 4, j, :], pt[:, :].rearrange("p (h d) -> p h d", h=4), pr, Rv_o)
                ins("pool", "dma_start", out=VV[:, :, 4 * it:4 * it + 4, :].rearrange("h p j d -> p h j d"),
                                                        in_=v_o[:, :, :, :], reads=[Rv_o], dma=s_v_o)
            T.barrier()


        if phases >= 2:
          with ExitStack() as ph:
            At = sb(ph, "At", [128, N2, 512], BF16)
            fa_b = sb(ph, "fa_b", [128, N2, 3, 128], BF16)
            fb_b = sb(ph, "fb_b", [2 * N2, N2], BF16)
            NFP = 4 if N2 >= 4 else 1
            FPW = N2 // NFP
            R_fap = [Res() for _ in range(NFP)]
            R_fb = Res()
            s_f = T.new_sem("d_fft_c")
            for p_ in range(NFP):
                ins("sp", "dma_start", out=fa_b[:, p_ * FPW:(p_ + 1) * FPW], in_=I["c_fa"][:, p_ * FPW:(p_ + 1) * FPW],
                    writes=[R_fap[p_]], dma=s_f)
            s_fbf = T.new_sem("d_fft_fb")
            ins("sp", "dma_start", out=fb_b[:, :], in_=I["c_fb"], writes=[R_fb], dma=s_fbf)
            SB = min(8, N2)
            if dbg:
                d_fa = nc.dram_tensor("d_fa", [128, N2, 3, 128], BF16, kind="ExternalOutput").ap()
                s_dbg = T.new_sem("d_dbg")
                ins("sp", "dma_start", out=d_fa, in_=fa_b[:, :, :, :], reads=R_fap, dma=s_dbg)
            NP = 4 if N2 >= 4 else 1
            PW = N2 // NP
            R_A = [Res() for _ in range(NP)]
            s_A = [T.new_sem("d_A%d" % i) for i in range(NP)]
            ystg = [sb(ph, "ystg%d" % i, [128, SB, 512], BF16) for i in range(2)]
            R_ystg = [Res() for _ in range(2)]
            s_ystg = [T.new_sem("d_ystg%d" % i) for i in range(2)]
            NB1 = N2 // SB
            R_YS = [[Res() for _ in range(2 * NB1)] for _ in range(NG)]
            KB = 16
            y2 = [sb(ph, "y2_%d" % i, [2 * N2, KB, 256], BF16) for i in range(2)]
            R_y2 = [Res() for _ in range(2)]
            s_y2 = [T.new_sem("d_y2_%d" % i) for i in range(2)]
            fo = [sb(ph, "fo%d" % i, [N2, KB, 256], F32) for i in range(2)]
            R_fo = [Res() for _ in range(2)]
            s_fo = [T.new_sem("d_fo%d" % i) for i in range(2)]
            VSv = VS.rearrange("(s1 s2) g r c -> s1 s2 g (r c)", s2=N2)
            FSv = FS.rearrange("(k2 k1) f -> k2 k1 f", k1=128)
            cnt1 = [0]
            cnt2 = [0]

            def stage1(g):
                for p in range(NP):
                    ins("sp", "dma_start", out=At[:, p * PW:(p + 1) * PW, :], in_=VSv[:, p * PW:(p + 1) * PW, g, :],
                        writes=[R_A[p]], dma=s_A[p])
                for b in range(NB1):
                    i = cnt1[0] % 2
                    cnt1[0] += 1
                    for sl in range(SB):
                        s2 = b * SB + sl
                        ra = R_A[s2 // PW]
                        pt, pr = next_ps()
                        ar = At[:, s2, 0:256]
                        ai = At[:, s2, 256:512]
                        ins("pe", "matmul", pt[:, 0:256], fa_b[:, s2, 0, :], ar, start=True, stop=False, reads=[R_fap[s2 // FPW], ra], writes=[pr])
                        ins("pe", "matmul", pt[:, 0:256], fa_b[:, s2, 2, :], ai, start=False, stop=True, reads=[R_fap[s2 // FPW], ra], writes=[pr])
                        ins("pe", "matmul", pt[:, 256:512], fa_b[:, s2, 1, :], ar, start=True, stop=False, reads=[R_fap[s2 // FPW], ra], writes=[pr])
                        ins("pe", "matmul", pt[:, 256:512], fa_b[:, s2, 0, :], ai, start=False, stop=True, reads=[R_fap[s2 // FPW], ra], writes=[pr])
                        if sl % 2 == 0:
                            ins("act", "copy", out=ystg[i][:, sl, :], in_=pt[:, :], reads=[pr], writes=[R_ystg[i]])
                        else:
                            ins("dve", "tensor_copy", out=ystg[i][:, sl, :], in_=pt[:, :], reads=[pr], writes=[R_ystg[i]])
                    for r_ in range(2):
                        dst = YS[g, r_].rearrange("s k c -> k s c")[:, b * SB:(b + 1) * SB, :]
                        ins("pool", "dma_start", out=dst, in_=ystg[i][:, :, r_ * 256:(r_ + 1) * 256],
                            reads=[R_ystg[i]], writes=[R_YS[g][2 * b + r_]], dma=s_ystg[i])

            def stage2(g):
                Y2v = YS[g].rearrange("r s k c -> (r s) k c")
                for kb in range(128 // KB):
                    i = cnt2[0] % 2
                    cnt2[0] += 1
                    ins("sp", "dma_start", out=y2[i][:, :, :], in_=Y2v[:, kb * KB:(kb + 1) * KB, :], reads=R_YS[g],
                        writes=[R_y2[i]], dma=s_y2[i])
                    for kk in range(0, KB, 2):
                        pt, pr = next_ps()
                        ins("pe", "matmul", pt[0:N2, :], fb_b[:, :], y2[i][:, kk:kk + 2, :].rearrange("p a c -> p (a c)"),
                            start=True, stop=True, reads=[R_fb, R_y2[i]], writes=[pr])
                        dsto = fo[i][:, kk:kk + 2, :].rearrange("p a c -> p (a c)")
                        if (kk // 2) % 2 == 0:
                            ins("act", "copy", out=dsto, in_=pt[0:N2, :], reads=[pr], writes=[R_fo[i]])
                        else:
                            ins("dve", "tensor_copy", out=dsto, in_=pt[0:N2, :], reads=[pr], writes=[R_fo[i]])
                    ins("pool", "dma_start", out=FSv[:, kb * KB:(kb + 1) * KB, g * 256:(g + 1) * 256], in_=fo[i][:, :, :],
                        reads=[R_fo[i]], dma=s_fo[i])

            stage1(0)
            for g in range(1, NG):
                stage1(g)
                stage2(g - 1)
            stage2(NG - 1)
            T.barrier()


        if phases >= 3:
          with ExitStack() as ph:
            NKC = S // 128
            NQT = S // TT
            kr_sb = sb(ph, "kr_sb", [128, S], BF16)
            R_kr = Res()
            s_kr = T.new_sem("d_kr")
            ins("dve", "memset", kr_sb[64:128, :], 0.0, writes=[R_kr])
            emit_def = make_deferred_emitter(ph, "a", "act")
            kn_sb = [sb(ph, "kn_sb%d" % i, [128, S], BF16) for i in range(2)]
            v_sb = [sb(ph, "v_sb%d" % i, [128, NKC, DV], BF16) for i in range(2)]
            R_kn = [Res() for _ in range(2)]
            R_v = [Res() for _ in range(2)]
            s_kn = [T.new_sem("d_kn%d" % i) for i in range(2)]
            s_v = [T.new_sem("d_v%d" % i) for i in range(2)]
            qn_sb = [sb(ph, "qn_sb%d" % i, [128, TT], BF16) for i in range(2)]
            qr_sb = [sb(ph, "qr_sb%d" % i, [128, TT], BF16) for i in range(2)]
            R_q = [Res() for _ in range(2)]
            for i in range(2):
                ins("dve", "memset", qr_sb[i][64:128, :], 0.0, writes=[R_q[i]])
            accL = [sb(ph, "accL%d" % i, [128, TT], F32) for i in range(4)]
            R_accL = [Res() for _ in range(4)]
            accb = sb(ph, "accb", [128, TT], BF16)
            R_accb = Res()
            accP = [sb(ph, "accP%d" % i, [128, TT], F32) for i in range(2)]
            R_accP = [Res() for _ in range(2)]
            s_q = [T.new_sem("d_q%d" % i) for i in range(2)]
            NPT = 8
            pT = [sb(ph, "pT%d" % i, [128, TT], BF16) for i in range(NPT)]
            R_pT = [Res() for _ in range(NPT)]
            rl = sb(ph, "rl", [128, TT], F32)
            R_rl = Res()
            ot_sb = [sb(ph, "ot_sb%d" % i, [128, TT], F32) for i in range(2)]
            R_ot = [Res() for _ in range(2)]
            s_ot = [T.new_sem("d_ot%d" % i) for i in range(2)]
            sm_scale = float(DN + DR) ** -0.5
            LOOK = 2
            POOL_SET = ()

            def load_head(h):
                i = h % 2
                ins("sp", "dma_start", out=kn_sb[i][:, :], in_=KN[h], writes=[R_kn[i]], dma=s_kn[i])
                ins("sp", "dma_start", out=v_sb[i][:, :, :], in_=VV[h], writes=[R_v[i]], dma=s_v[i])

            def load_q(n):
                h, qt = divmod(n, NQT)
                i = n % 2
                ins("sp", "dma_start", out=qn_sb[i][:, :], in_=QN[h, :, qt * TT:(qt + 1) * TT], writes=[R_q[i]], dma=s_q[i])
                ins("sp", "dma_start", out=qr_sb[i][0:64, :], in_=QR[h, :, qt * TT:(qt + 1) * TT], writes=[R_q[i]], dma=s_q[i])

            load_q(0)
            i0 = 0
            ins("sp", "dma_start", out=kn_sb[i0][:, :], in_=KN[0], writes=[R_kn[i0]], dma=s_kn[i0])
            ins("sp", "dma_start", out=kr_sb[0:64, :], in_=KR[:, :], writes=[R_kr], dma=s_kr)
            ins("sp", "dma_start", out=v_sb[i0][:, :, :], in_=VV[0], writes=[R_v[i0]], dma=s_v[i0])
            pcnt = [0]
            for h in range(NH):
                hi = h % 2
                for qt in range(NQT):
                    n = h * NQT + qt
                    qi = n % 2
                    if n + 1 < NH * NQT:
                        load_q(n + 1)
                    if qt == 1 and h + 1 < NH:
                        load_head(h + 1)
                    ob, obr = PS[4 + qi], PSR[4 + qi]
                    lb, lbr = PS[6 + qi], PSR[6 + qi]
                    pend = []
                    st_acc = {"p": False, "d": 0, "pe": False}

                    def s_tile(kc):
                        sbk, sbr = PS[pcnt[0] % 4], PSR[pcnt[0] % 4]
                        pi = pcnt[0] % NPT
                        pcnt[0] += 1
                        ins("pe", "matmul", sbk[:, :], kn_sb[hi][:, kc * 128:(kc + 1) * 128], qn_sb[qi][:, :], start=True, stop=False,
                            reads=[R_kn[hi], R_q[qi]], writes=[sbr])
                        ins("pe", "matmul", sbk[:, :], kr_sb[:, kc * 128:(kc + 1) * 128], qr_sb[qi][:, :], start=False, stop=True,
                            reads=[R_kr, R_q[qi]], writes=[sbr])
                        ins("act", "activation", out=pT[pi][:, :], in_=sbk[:, :], func=AF.Exp, scale=sm_scale,
                            reads=[sbr], writes=[R_pT[pi]])
                        return pi

                    def pv_tile(kc, pi):
                        ins("pe", "matmul", ob[:, :], v_sb[hi][:, kc, :], pT[pi][:, :], start=(kc == 0), stop=(kc == NKC - 1),
                            reads=[R_v[hi], R_pT[pi]], writes=[obr])
                        if kc % 8 == 7:
                            ins("pe", "matmul", lb[:, :], ones_b[:, :], pT[pi][:, :], start=(not st_acc["pe"]), stop=False,
                                reads=[R_const, R_pT[pi]], writes=[lbr])
                            st_acc["pe"] = True
                        elif (kc % 12) in POOL_SET:
                            if not st_acc["p"]:
                                st_acc["p"] = True
                                ins("pool", "tensor_copy", out=accP[qi][:, :], in_=pT[pi][:, :], reads=[R_pT[pi]], writes=[R_accP[qi]])
                            else:
                                ins("pool", "tensor_tensor", out=accP[qi][:, :], in0=accP[qi][:, :], in1=pT[pi][:, :], op=ALU.add,
                                    reads=[R_pT[pi], R_accP[qi]], writes=[R_accP[qi]])
                        else:
                            ai = 2 * qi + st_acc["d"] % 2
                            if st_acc["d"] < 2:
                                ins("dve", "tensor_copy", out=accL[ai][:, :], in_=pT[pi][:, :], reads=[R_pT[pi]], writes=[R_accL[ai]])
                            else:
                                ins("dve", "tensor_tensor", out=accL[ai][:, :], in0=accL[ai][:, :], in1=pT[pi][:, :], op=ALU.add,
                                    reads=[R_pT[pi], R_accL[ai]], writes=[R_accL[ai]])
                            st_acc["d"] += 1
                        if kc in (NKC // 4, (3 * NKC) // 4):
                            emit_def(1)

                    for kc in range(NKC + LOOK):
                        if kc < NKC:
                            pend.append((kc, s_tile(kc)))
                        if kc >= LOOK:
                            k0, p0 = pend.pop(0)
                            pv_tile(k0, p0)
                    ins("dve", "tensor_tensor", out=accL[2 * qi][:, :], in0=accL[2 * qi][:, :], in1=accL[2 * qi + 1][:, :], op=ALU.add,
                        reads=[R_accL[2 * qi], R_accL[2 * qi + 1]], writes=[R_accL[2 * qi]])
                    if st_acc["p"]:
                        ins("dve", "tensor_tensor", out=accb[:, :], in0=accL[2 * qi][:, :], in1=accP[qi][:, :], op=ALU.add,
                            reads=[R_accL[2 * qi], R_accP[qi]], writes=[R_accb])
                    else:
                        ins("dve", "tensor_copy", out=accb[:, :], in_=accL[2 * qi][:, :], reads=[R_accL[2 * qi]], writes=[R_accb])
                    ins("pe", "matmul", lb[:, :], ones_b[:, :], accb[:, :], start=(not st_acc["pe"]), stop=True,
                        reads=[R_const, R_accb], writes=[lbr])
                    ins("dve", "reciprocal", out=rl[:, :], in_=lb[:, :], reads=[lbr], writes=[R_rl])
                    ins("dve", "tensor_tensor", out=ot_sb[qi][:, :], in0=ob[:, :], in1=rl[:, :], op=ALU.mult,
                        reads=[obr, R_rl], writes=[R_ot[qi]])
                    ins("pool", "dma_start", out=OT[h * DV:(h + 1) * DV, qt * TT:(qt + 1) * TT], in_=ot_sb[qi][:, :],
                        reads=[R_ot[qi]], dma=s_ot[qi])
            emit_def(10 ** 6)
            T.barrier()


        if phases >= 4:
          with ExitStack() as ph:
            wsl = [sb(ph, "w4_%d" % i, [128, 16, 512], BF16) for i in range(3)]
            Rw = [Res() for _ in range(3)]
            s_w = [T.new_sem("d_w4_%d" % i) for i in range(3)]
            ckT = sb(ph, "ckT", [128, KC, NMEM], BF16)
            cv = sb(ph, "cv", [128, 2, D], BF16)
            R_ck, R_cv = Res(), Res()
            xs = [sb(ph, "xs4_%d" % i, [128, D], BF16) for i in range(2)]
            Rxs = [Res() for _ in range(2)]
            junk = sb(ph, "junk4", [128, D], BF16)
            evq = [0]

            def evac(dst, src, pr, Rdst):
                evq[0] += 1
                if evq[0] % 2 == 0:
                    ins("act", "copy", out=dst, in_=src, reads=[pr], writes=[Rdst])
                else:
                    ins("dve", "tensor_copy", out=dst, in_=src, reads=[pr], writes=[Rdst])

            with ExitStack() as pre:
                memx = [sb(pre, "memx%d" % i, [128, D], F32) for i in range(2)]
                Rmx = [Res() for _ in range(2)]
                s_mx = [T.new_sem("d_mx%d" % i) for i in range(2)]
                memT = sb(pre, "memT", [128, KC, NMEM], BF16)
                RmT = Res()
                wspecs = [(WCKV[:, c * 512:(c + 1) * 512], 16, 512) for c in range(8)]
                WS = WStream(wsl, Rw, s_w, wspecs)
                for j in range(2):
                    ins("sp", "dma_start", out=memx[j][:, :], in_=I["mem"][j * 128:(j + 1) * 128, :], writes=[Rmx[j]], dma=s_mx[j])
                    tok_norm_T(memx[j][:, :], Rmx[j], xs[j], Rxs[j], junk, memT, RmT, j)
                for m in range(KC):
                    wv, wr = WS.get(m // 4)
                    pt, pr = next_ps()
                    for kc in range(KC):
                        ins("pe", "matmul", pt[:, 0:NMEM], wv[:, kc, (m % 4) * 128:(m % 4 + 1) * 128], memT[:, kc, :],
                            start=(kc == 0), stop=(kc == KC - 1), reads=[wr, RmT], writes=[pr])
                    evac(ckT[:, m, :], pt[:, 0:NMEM], pr, R_ck)
                for n in range(4):
                    wv, wr = WS.get(4 + n)
                    for kk in range(2):
                        pt, pr = next_ps()
                        for kc in range(KC):
                            ins("pe", "matmul", pt[:, :], memT[:, kc, kk * 128:(kk + 1) * 128], wv[:, kc, :],
                                start=(kc == 0), stop=(kc == KC - 1), reads=[wr, RmT], writes=[pr])
                        evac(cv[:, kk, n * 512:(n + 1) * 512], pt[:, :], pr, R_cv)
                T.barrier()

            xt = sb(ph, "xt4", [128, 4, D], F32)
            Rxt = [Res() for _ in range(4)]
            s_xt = [T.new_sem("d_xt4_%d" % i) for i in range(4)]
            s_xo = [T.new_sem("d_xo4_%d" % i) for i in range(4)]
            otl = sb(ph, "otl", [128, NH, TT], F32)
            R_otl = Res()
            s_otl = T.new_sem("d_otl")
            fsl = [sb(ph, "fsl%d" % i, [128, NF], F32) for i in range(2)]
            R_fsl = [Res() for _ in range(2)]
            s_fsl = [T.new_sem("d_fsl%d" % i) for i in range(2)]
            sq = sb(ph, "sq4", [128, NH, TT], BF16)
            R_sq = Res()
            rstd = sb(ph, "rstd4", [128, TT], F32)
            R_rstd = Res()
            bufA = sb(ph, "bufA", [128, KC, TT], BF16)
            bufB = sb(ph, "bufB", [128, KC, TT], BF16)
            R_A4, R_B4 = Res(), Res()
            pTc = [sb(ph, "pTc%d" % i, [128, 2, TT], BF16) for i in range(2)]
            R_pTc = [Res() for _ in range(2)]
            rlc = sb(ph, "rlc", [128, TT], F32)
            R_rlc = Res()
            wspecs = []
            for it in range(NT):
                for W_ in (WOUT, WCQ, WCO):
                    for c in range(4):
                        wspecs.append((W_[:, c * 512:(c + 1) * 512], 16, 512))
            WS = WStream(wsl, Rw, s_w, wspecs)
            x_t4 = I["x"].rearrange("(n p) d -> n p d", p=128)
            x2_t4 = X2.rearrange("(n p) d -> n p d", p=128)
            fs_t4 = FS.rearrange("(n p) d -> n p d", p=128)
            OTv = OT.rearrange("(h d) t -> d h t", d=128)
            c_scale = float(CHD) ** -0.5
            hcnt = [0]

            pf_done = set()

            def load_otl(i):
                if ("o", i) in pf_done:
                    return
                pf_done.add(("o", i))
                ins("sp", "dma_start", out=otl[:, :, :], in_=OTv[:, :, i * TT:(i + 1) * TT], writes=[R_otl], dma=s_otl)

            def load_fs(g):
                if ("f", g) in pf_done:
                    return
                pf_done.add(("f", g))
                ins("sp", "dma_start", out=fsl[g % 2][:, :], in_=fs_t4[g], writes=[R_fsl[g % 2]], dma=s_fsl[g % 2])

            rstd2 = [rstd, sb(ph, "rstd4b", [128, TT], F32)]
            R_rstd2 = [R_rstd, Res()]
            xsf = [sb(ph, "xsf%d" % i, [128, NF], BF16) for i in range(2)]
            Rxsf = [Res() for _ in range(2)]

            def aout_part1(i):
                load_otl(i)
                ins("act", "activation", out=sq[:, :, :], in_=otl[:, :, :], func=AF.Square, reads=[R_otl], writes=[R_sq])

            def aout_part2(i):
                pt, pr = next_ps()
                for h in range(NH):
                    ins("pe", "matmul", pt[:, :], o1024_b[:, :], sq[:, h, :], start=(h == 0), stop=(h == NH - 1),
                        reads=[R_sq, R_const], writes=[pr])
                ins("act", "activation", out=rstd2[i % 2][:, :], in_=pt[:, :], func=AF.Sqrt, bias=eps_t[:, 0:1],
                    reads=[pr, R_const], writes=[R_rstd2[i % 2]])
                ins("dve", "reciprocal", out=rstd2[i % 2][:, :], in_=rstd2[i % 2][:, :], reads=[R_rstd2[i % 2]], writes=[R_rstd2[i % 2]])

            def proj_residual(actT, R_act, wbase):
                for n in range(4):
                    wv, wr = WS.get(wbase + n)
                    for j in range(4):
                        pt, pr = next_ps()
                        for kc in range(KC):
                            ins("pe", "matmul", pt[:, :], actT[:, kc, j * 128:(j + 1) * 128], wv[:, kc, :],
                                start=(kc == 0), stop=(kc == KC - 1), reads=[wr, R_act], writes=[pr])
                        ins("dve", "tensor_tensor", out=xt[:, j, n * 512:(n + 1) * 512], in0=pt[:, :],
                            in1=xt[:, j, n * 512:(n + 1) * 512], op=ALU.add, reads=[pr, Rxt[j]], writes=[Rxt[j]])

            for it in range(NT):
                t0 = it * TT
                wb = it * 12
                if it == 0:
                    aout_part1(0)
                    aout_part2(0)
                rs_, Rrs_ = rstd2[it % 2], R_rstd2[it % 2]
                for h in range(NH):
                    ins("dve", "tensor_tensor", out=bufA[:, 8 + h, :], in0=otl[:, h, :], in1=rs_[:, :], op=ALU.mult,
                        reads=[R_otl, Rrs_], writes=[R_A4])
                norm_pipe(4, lambda j, it=it: (fsl[(it * 4 + j) % 2][:, :], R_fsl[(it * 4 + j) % 2]), xsf, Rxsf, junk, bufA, R_A4,
                          width=NF, pre_load=lambda j, it=it: load_fs(it * 4 + j), done_a=(0 if it == 0 else 2))
                for j in range(4):
                    ins("sp", "dma_start", out=xt[:, j, :], in_=x_t4[it * 4 + j], writes=[Rxt[j]], dma=s_xt[j])
                proj_residual(bufA, R_A4, wb)
                if it + 1 < NT:
                    aout_part1(it + 1)
                    for j in range(2):
                        g = (it + 1) * 4 + j
                        load_fs(g)
                        tok_norm_a(fsl[g % 2][:, :], R_fsl[g % 2], xsf[g % 2], Rxsf[g % 2], junk, NF)
                norm_pipe(4, lambda j: (xt[:, j, :], Rxt[j]), xs, Rxs, junk, bufB, R_B4)
                for m in range(KC):
                    wv, wr = WS.get(wb + 4 + m // 4)
                    pt, pr = next_ps()
                    for kc in range(KC):
                        ins("pe", "matmul", pt[:, :], wv[:, kc, (m % 4) * 128:(m % 4 + 1) * 128], bufB[:, kc, :],
                            start=(kc == 0), stop=(kc == KC - 1), reads=[wr, R_B4], writes=[pr])
                    evac(bufA[:, m, :], pt[:, :], pr, R_A4)
                if it + 1 < NT:
                    aout_part2(it + 1)
                for hc in range(NCH):
                    pi = hcnt[0] % 2
                    hcnt[0] += 1
                    for kk in range(2):
                        pt, pr = next_ps()
                        for dc in range(4):
                            ins("pe", "matmul", pt[:, :], ckT[:, hc * 4 + dc, kk * 128:(kk + 1) * 128], bufA[:, hc * 4 + dc, :],
                                start=(dc == 0), stop=(dc == 3), reads=[R_ck, R_A4], writes=[pr])
                        ins("act", "activation", out=pTc[pi][:, kk, :], in_=pt[:, :], func=AF.Exp, scale=c_scale,
                            reads=[pr], writes=[R_pTc[pi]])
                    pt, pr = next_ps()
                    for kk in range(2):
                        ins("pe", "matmul", pt[:, :], ones_b[:, :], pTc[pi][:, kk, :], start=(kk == 0), stop=(kk == 1),
                            reads=[R_const, R_pTc[pi]], writes=[pr])
                    ins("dve", "reciprocal", out=rlc[:, :], in_=pt[:, :], reads=[pr], writes=[R_rlc])
                    for dvc in range(4):
                        pt, pr = next_ps()
                        c0 = hc * CHD + dvc * 128
                        for kk in range(2):
                            ins("pe", "matmul", pt[:, :], cv[:, kk, c0:c0 + 128], pTc[pi][:, kk, :], start=(kk == 0), stop=(kk == 1),
                                reads=[R_cv, R_pTc[pi]], writes=[pr])
                        ins("dve", "tensor_tensor", out=bufB[:, hc * 4 + dvc, :], in0=pt[:, :], in1=rlc[:, :], op=ALU.mult,
                            reads=[pr, R_rlc], writes=[R_B4])
                proj_residual(bufB, R_B4, wb + 8)
                for j in range(4):
                    ins("pool", "dma_start", out=x2_t4[it * 4 + j], in_=xt[:, j, :], reads=[Rxt[j]], dma=s_xo[j])
            T.barrier()


        if phases >= 6:
          with ExitStack() as ph:
            wsl = [sb(ph, "w7_%d" % i, [128, 16, 512], BF16) for i in range(3)]
            Rw = [Res() for _ in range(3)]
            s_w = [T.new_sem("d_w7_%d" % i) for i in range(3)]
            gbh = sb(ph, "gbh", [128, FC, 32], F32)
            R_gbh = Res()
            xs = [sb(ph, "xs7_%d" % i, [128, D], BF16) for i in range(2)]
            Rxs = [Res() for _ in range(2)]
            junk = sb(ph, "junk7", [128, D], BF16)
            x2_t = X2.rearrange("(n p) d -> n p d", p=128)
            y_t = y_out.rearrange("(n p) d -> n p d", p=128)
            with ExitStack() as pre:
                xh = sb(pre, "xh", [32, D], F32)
                R_xh = Res()
                s_xh = T.new_sem("d_xh")
                hTh = sb(pre, "hTh", [128, KC, 32], BF16)
                R_hTh = Res()
                ins("dve", "memset", xh[:, :], 0.0, writes=[R_xh])
                if NT > 1:
                    X2v = X2.rearrange("(n t) d -> n t d", t=TT)
                    ins("sp", "dma_start", out=xh[0:NT - 1, :], in_=X2v[0:NT - 1, TT - 1, :], writes=[R_xh], dma=s_xh)
                    ins("sp", "dma_start", out=xh[16:16 + NT - 1, :], in_=X2v[1:NT, 0, :], writes=[R_xh], dma=s_xh)
                st_ap, Rs = next_stat()
                ins("act", "activation", out=junk[0:32, :], in_=xh[:, :], func=AF.Square, scale=float(D) ** -0.5,
                    accum_out=st_ap[0:32, 0:1], reads=[R_xh], writes=[Rs])
                ins("act", "activation", out=st_ap[0:32, 1:2], in_=st_ap[0:32, 0:1], func=AF.Sqrt, bias=eps_t[0:32, 0:1],
                    reads=[Rs, R_const], writes=[Rs])
                ins("dve", "reciprocal", out=st_ap[0:32, 2:3], in_=st_ap[0:32, 1:2], reads=[Rs], writes=[Rs])
                ins("dve", "tensor_scalar", out=xs[0][0:32, :], in0=xh[:, :], scalar1=st_ap[0:32, 2:3], scalar2=None, op0=ALU.mult,
                    reads=[R_xh, Rs], writes=[Rxs[0]])
                pt, pr = next_ps()
                ptb = pt.bitcast(BF16)
                for k in range(KC):
                    ins("pe", "transpose", out=ptb[:, k * 32:(k + 1) * 32], in_=xs[0][0:32, k * 128:(k + 1) * 128],
                        identity=ident_b[0:32, 0:32], reads=[Rxs[0], R_const], writes=[pr])
                ins("dve", "tensor_copy", out=hTh[:, :, :], in_=ptb[:, 0:KC * 32].rearrange("p (k c) -> p k c", k=KC),
                    reads=[pr], writes=[R_hTh])
                wspecs = [(WG[:, c * 512:(c + 1) * 512], 16, 512) for c in range(FC // 4)]
                WS = WStream(wsl, Rw, s_w, wspecs)
                for fc in range(FC):
                    wv, wr = WS.get(fc // 4)
                    pt, pr = next_ps()
                    for kc in range(KC):
                        ins("pe", "matmul", pt[:, 0:32], wv[:, kc, (fc % 4) * 128:(fc % 4 + 1) * 128], hTh[:, kc, :],
                            start=(kc == 0), stop=(kc == KC - 1), reads=[wr, R_hTh], writes=[pr])
                    ins("dve", "tensor_copy", out=gbh[:, fc, :], in_=pt[:, 0:32], reads=[pr], writes=[R_gbh])
                T.barrier()

            NXB = 7
            xb = [sb(ph, "xb7_%d" % i, [128, D], F32) for i in range(NXB)]
            Rxb = [Res() for _ in range(NXB)]
            s_xt = [T.new_sem("d_xt7_%d" % i) for i in range(NXB)]
            s_xo = [T.new_sem("d_xo7_%d" % i) for i in range(NXB)]
            x7_done = set()

            def load_x7(g):
                if g in x7_done or g >= 4 * NT:
                    return
                x7_done.add(g)
                ins("sp", "dma_start", out=xb[g % NXB][:, :], in_=x2_t[g], writes=[Rxb[g % NXB]], dma=s_xt[g % NXB])
            hT = sb(ph, "hT7", [128, KC, TT], BF16)
            R_hT = Res()
            aT = sb(ph, "aT", [128, FC, TT], BF16)
            R_aT = Res()
            NACC = 2
            acc = [sb(ph, "acc%d" % i, [128, TT], F32) for i in range(NACC)]
            R_acc = [Res() for _ in range(NACC)]
            sg = [sb(ph, "sg%d" % i, [128, TT], F32) for i in range(4)]
            R_sg = [Res() for _ in range(4)]
            hl = sb(ph, "hl", [128, 2, FC], F32)
            R_hl = Res()
            NQ = FC // 4
            wspecs = []
            for it in range(NT):
                for q in range(NQ):
                    wspecs.append((WG[:, q * 512:(q + 1) * 512], 16, 512))
                    wspecs.append((WU[:, q * 512:(q + 1) * 512], 16, 512))
                for n in range(4):
                    for kq in range(4):
                        wspecs.append((WD[kq * 1408:(kq + 1) * 1408, n * 512:(n + 1) * 512], 11, 512))
            WS = WStream(wsl, Rw, s_w, wspecs)
            WPT = 2 * NQ + 16
            fcnt = [0]
            for it in range(NT):
                wb = it * WPT
                if it == 0:
                    load_x7(0)
                    load_x7(1)
                norm_pipe(4, lambda j, it=it: (xb[(it * 4 + j) % NXB][:, :], Rxb[(it * 4 + j) % NXB]), xs, Rxs, junk, hT, R_hT,
                          pre_load=lambda j, it=it: load_x7(it * 4 + j), done_a=(0 if it == 0 else 2))
                if it > 0:
                    ins("dve", "tensor_tensor", out=hl[:, 0, :], in0=cwv[:, 0, :], in1=gbh[:, :, it - 1], op=ALU.mult,
                        reads=[R_const, R_gbh], writes=[R_hl])
                if it < NT - 1:
                    ins("dve", "tensor_tensor", out=hl[:, 1, :], in0=cwv[:, 2, :], in1=gbh[:, :, 16 + it], op=ALU.mult,
                        reads=[R_const, R_gbh], writes=[R_hl])
                for q in range(NQ):
                    wg, wgr = WS.get(wb + 2 * q)
                    for c in range(4):
                        fc = 4 * q + c
                        ai = fcnt[0] % NACC
                        fcnt[0] += 1
                        pg, pgr = next_ps()
                        for kc in range(KC):
                            ins("pe", "matmul", pg[:, :], wg[:, kc, c * 128:(c + 1) * 128], hT[:, kc, :],
                                start=(kc == 0), stop=(kc == KC - 1), reads=[wgr, R_hT], writes=[pgr])
                        a_ = acc[ai]
                        Ra = R_acc[ai]
                        ins("act", "activation", out=a_[:, :], in_=pg[:, :], func=AF.Identity, scale=cwv[:, 1, fc:fc + 1],
                            bias=cwv[:, 3, fc:fc + 1], reads=[pgr, R_const], writes=[Ra])
                        ins("dve", "scalar_tensor_tensor", out=a_[:, 1:TT], in0=pg[:, 0:TT - 1], scalar=cwv[:, 0, fc:fc + 1],
                            in1=a_[:, 1:TT], op0=ALU.mult, op1=ALU.add, reads=[pgr, Ra, R_const], writes=[Ra])
                        ins("dve", "scalar_tensor_tensor", out=a_[:, 0:TT - 1], in0=pg[:, 1:TT], scalar=cwv[:, 2, fc:fc + 1],
                            in1=a_[:, 0:TT - 1], op0=ALU.mult, op1=ALU.add, reads=[pgr, Ra, R_const], writes=[Ra])
                        if it > 0:
                            ins("pool", "tensor_tensor", out=a_[:, 0:1], in0=a_[:, 0:1], in1=hl[:, 0, fc:fc + 1], op=ALU.add,
                                reads=[Ra, R_hl], writes=[Ra])
                        if it < NT - 1:
                            ins("pool", "tensor_tensor", out=a_[:, TT - 1:TT], in0=a_[:, TT - 1:TT], in1=hl[:, 1, fc:fc + 1], op=ALU.add,
                                reads=[Ra, R_hl], writes=[Ra])
                        ins("act", "activation", out=sg[c][:, :], in_=a_[:, :], func=AF.Silu, reads=[Ra], writes=[R_sg[c]])
                    wu, wur = WS.get(wb + 2 * q + 1)
                    for c in range(4):
                        fc = 4 * q + c
                        pu, pur = next_ps()
                        for kc in range(KC):
                            ins("pe", "matmul", pu[:, :], wu[:, kc, c * 128:(c + 1) * 128], hT[:, kc, :],
                                start=(kc == 0), stop=(kc == KC - 1), reads=[wur, R_hT], writes=[pur])
                        ins("dve", "tensor_tensor", out=aT[:, fc, :], in0=pu[:, :], in1=sg[c][:, :], op=ALU.mult,
                            reads=[pur, R_sg[c]], writes=[R_aT])
                if it + 1 < NT:
                    for j in range(2):
                        g = (it + 1) * 4 + j
                        load_x7(g)
                        tok_norm_a(xb[g % NXB][:, :], Rxb[g % NXB], xs[g % 2], Rxs[g % 2], junk)
                for n in range(4):
                    banks = [next_ps() for _ in range(4)]
                    for kq in range(4):
                        wv, wr = WS.get(wb + 2 * NQ + n * 4 + kq)
                        for j in range(4):
                            pt, pr = banks[j]
                            for kc in range(11):
                                ins("pe", "matmul", pt[:, :], aT[:, kq * 11 + kc, j * 128:(j + 1) * 128], wv[:, kc, :],
                                    start=(kq == 0 and kc == 0), stop=(kq == 3 and kc == 10), reads=[wr, R_aT], writes=[pr])
                    for j in range(4):
                        pt, pr = banks[j]
                        bi = (it * 4 + j) % NXB
                        ins("dve", "tensor_tensor", out=xb[bi][:, n * 512:(n + 1) * 512], in0=pt[:, :],
                            in1=xb[bi][:, n * 512:(n + 1) * 512], op=ALU.add, reads=[pr, Rxb[bi]], writes=[Rxb[bi]])
                for j in range(4):
                    bi = (it * 4 + j) % NXB
                    st_ap, Rs = next_stat()
                    ins("act", "activation", out=junk[:, :], in_=xb[bi][:, :], func=AF.Square, scale=float(D) ** -0.5,
                        accum_out=st_ap[:, 0:1], reads=[Rxb[bi]], writes=[Rs])
                    ins("act", "activation", out=st_ap[:, 1:2], in_=st_ap[:, 0:1], func=AF.Sqrt, bias=eps_t[:, 0:1],
                        reads=[Rs, R_const], writes=[Rs])
                    ins("dve", "reciprocal", out=st_ap[:, 2:3], in_=st_ap[:, 1:2], reads=[Rs], writes=[Rs])
                    ins("dve", "scalar_tensor_tensor", out=xb[bi][:, :], in0=xb[bi][:, :], scalar=st_ap[:, 2:3], in1=gfin[:, :],
                        op0=ALU.mult, op1=ALU.mult, reads=[Rxb[bi], Rs, R_const], writes=[Rxb[bi]])
                    ins("pool", "dma_start", out=y_t[it * 4 + j], in_=xb[bi][:, :], reads=[Rxb[bi]], dma=s_xo[bi])
            T.barrier()


        T.barrier()
        with nc.Block() as block:
            @block.tensor
            def _(e):
                T.replay(e, "pe")

            @block.scalar
            def _(e):
                T.replay(e, "act")

            @block.vector
            def _(e):
                T.replay(e, "dve")

            @block.gpsimd
            def _(e):
                T.replay(e, "pool")

            @block.sync
            def _(e):
                T.replay(e, "sp")
    return nc


S_FULL = 8192
_NC_CACHE = {}


def kernel(x_prompt, x_sample, mem_prompt, mem_sample, **weights):
    xs = [np.asarray(x_prompt[i]) for i in range(x_prompt.shape[0])] + \
         [np.asarray(x_sample[i]) for i in range(x_sample.shape[0])]
    ms = [np.asarray(mem_prompt[i]) for i in range(mem_prompt.shape[0])] + \
         [np.asarray(mem_sample[i]) for i in range(mem_sample.shape[0])]
    nseq = len(xs)
    S = xs[0].shape[0]
    if S not in _NC_CACHE:
        _NC_CACHE[S] = build(S)
    nc = _NC_CACHE[S]
    shared = {}
    for name, shp in WEIGHT_SPECS:
        shared[name] = np.ascontiguousarray(np.asarray(weights[name], dtype=np.float32).reshape(shp))
    shared.update(host_constants(S))
    in_maps = []
    core_of_seq = [0, 1, 2, 4, 5, 6]
    seq_of_core = {c: i for i, c in enumerate(core_of_seq)}
    zx = np.zeros_like(np.ascontiguousarray(xs[0], dtype=np.float32))
    zm = np.zeros_like(np.ascontiguousarray(ms[0], dtype=np.float32))
    for c in range(8):
        m = dict(shared)
        if c in seq_of_core:
            m["x"] = np.ascontiguousarray(xs[seq_of_core[c]], dtype=np.float32)
            m["mem"] = np.ascontiguousarray(ms[seq_of_core[c]], dtype=np.float32)
        else:
            m["x"] = zx
            m["mem"] = zm
        in_maps.append(m)
    res = run_bass_kernel_spmd(nc, in_maps, core_ids=list(range(8)))
    ys = [np.asarray(res.results[core_of_seq[i]]["y"], dtype=np.float32) for i in range(nseq)]
    nb = x_prompt.shape[0]
    y_prompt = np.stack(ys[:nb], axis=0)
    y_sample = np.stack(ys[nb:], axis=0)
    return (y_prompt, y_sample)
```

```python
import math
from contextlib import ExitStack

import numpy as np
import ml_dtypes
import concourse.bass as bass
import concourse.mybir as mybir
from concourse.bass_utils import run_bass_kernel_spmd

F32 = mybir.dt.float32
BF16 = mybir.dt.bfloat16
AF = mybir.ActivationFunctionType
ALU = mybir.AluOpType

D = 2048
NF = 1024
NG = 4
GC = 256
QL = 512
KVL = 512
NH = 8
DN = 128
DR = 64
DV = 128
NMEM = 256
NCH = 4
CHD = 512
DFF = 5632
FC = DFF // 128
EPS = 1e-6
KC = D // 128
TT = 512


class Sem:
    __slots__ = ("h", "total", "is_dma", "name")

    def __init__(self, h, is_dma, name):
        self.h = h
        self.total = 0
        self.is_dma = is_dma
        self.name = name


class Res:
    __slots__ = ("w", "r")

    def __init__(self):
        self.w = None
        self.r = {}


class Stream:
    def __init__(self, name):
        self.name = name
        self.sem = None
        self.ops = []
        self.known = {}


class Tracker:
    def __init__(self, nc, es):
        self.nc = nc
        self.es = es
        self.sems = []
        self.streams = {}
        for name in ("pe", "act", "dve", "pool", "sp"):
            st = Stream(name)
            if name != "sp":
                st.sem = self.new_sem("c_" + name, False)
            self.streams[name] = st
        self.n_ins = 0

    def new_sem(self, name, is_dma=True):
        h = self.es.enter_context(self.nc.semaphore(name))
        s = Sem(h, is_dma, name)
        self.sems.append(s)
        return s

    def _wait(self, st, ev):
        sem, val = ev
        if sem.is_dma:
            val = sem.total
        if st.known.get(sem, 0) >= val:
            return
        st.known[sem] = val
        st.ops.append(("w", sem.h, val))

    def op(self, stname, fn, reads=(), writes=(), dma=None):
        st = self.streams[stname]
        for r in reads:
            if r.w is not None:
                if not (stname == "pe" and r.w[0] is st.sem):
                    self._wait(st, r.w)
        for w in writes:
            if w.w is not None:
                if not (stname == "pe" and w.w[0] is st.sem):
                    self._wait(st, w.w)
            for ev in w.r.values():
                if not (stname == "pe" and ev[0] is st.sem):
                    self._wait(st, ev)
        if dma is not None:
            dma.total += 16
            ev = (dma, dma.total)
            st.ops.append(("i", fn, dma.h, 16))
        else:
            st.sem.total += 1
            ev = (st.sem, st.sem.total)
            st.ops.append(("i", fn, st.sem.h, 1))
        key = ev[0]
        for r in reads:
            r.r[key] = ev
        for w in writes:
            w.w = ev
            w.r = {}
        self.n_ins += 1

    def barrier(self):
        for st in self.streams.values():
            for sem in self.sems:
                if sem.total > 0:
                    self._wait(st, (sem, sem.total))

    def ins(self, stname, method, *args, reads=(), writes=(), dma=None, **kw):
        self.op(stname, (method, args, kw), reads, writes, dma)

    def replay(self, eng, stname):
        for o in self.streams[stname].ops:
            if o[0] == "w":
                eng.wait_ge(o[1], o[2])
            else:
                m, a, kw = o[1]
                getattr(eng, m)(*a, **kw).then_inc(o[2], o[3])


def host_constants(S):
    N1 = 128
    N2 = S // 128
    c = {}
    c["c_ident"] = np.eye(128, dtype=np.float32)
    j = np.arange(GC)
    ang = 2.0 * np.pi * np.outer(j, j) / GC
    c["c_cdft"] = (np.concatenate([np.cos(ang), -np.sin(ang)], axis=1) / math.sqrt(GC)).astype(np.float32)
    s1 = np.arange(N1)[:, None, None].astype(np.float64)
    s2 = np.arange(N2)[None, :, None].astype(np.float64)
    k1 = np.arange(N1)[None, None, :].astype(np.float64)
    th = 2.0 * np.pi * (s1 * k1 / N1 + s2 * k1 / S)
    far = np.cos(th) / math.sqrt(N1)
    fai = -np.sin(th) / math.sqrt(N1)
    c["c_fa"] = np.stack([far, fai, -fai], axis=2).astype(np.float32).astype(ml_dtypes.bfloat16)
    s2b = np.arange(N2)[:, None].astype(np.float64)
    k2 = np.arange(N2)[None, :].astype(np.float64)
    ph = 2.0 * np.pi * s2b * k2 / N2
    c["c_fb"] = (np.concatenate([np.cos(ph), np.sin(ph)], axis=0) / math.sqrt(N2)).astype(np.float32).astype(ml_dtypes.bfloat16)
    inv = 10000.0 ** (-np.arange(0, DR, 2, dtype=np.float32) / DR)
    a = np.arange(S, dtype=np.float32)[None, :] * inv[:, None].astype(np.float32)
    c["c_cos"] = np.concatenate([np.cos(a), np.cos(a)], axis=0).astype(np.float32)
    c["c_sin"] = np.concatenate([np.sin(a), np.sin(a)], axis=0).astype(np.float32)
    return c


WEIGHT_SPECS = [
    ("norm_mix_g", [D]), ("w_in", [D, 2112]), ("q_norm_g", [QL]), ("w_uq", [QL, 1536]),
    ("kv_norm_g", [KVL]), ("w_ukv", [KVL, 2048]), ("fourier_out_g", [NF]), ("mla_out_g", [1024]),
    ("w_out", [D, D]), ("norm_cross_g", [D]), ("norm_mem_g", [D]), ("w_cq", [D, D]),
    ("w_ckv", [D, 2 * D]), ("w_co", [D, D]), ("norm_ffn_g", [D]), ("w_gate", [D, DFF]),
    ("w_up", [D, DFF]), ("conv_w", [3, DFF]), ("conv_b", [DFF]), ("w_down", [DFF, D]),
    ("final_norm_g", [D]),
]


def build(S, dbg=False, phases=99):
    N2 = S // 128
    NT = S // TT
    nc = bass.Bass("TRN2", target_bir_lowering=False)
    I = {}
    I["x"] = nc.dram_tensor("x", [S, D], F32, kind="ExternalInput").ap()
    I["mem"] = nc.dram_tensor("mem", [NMEM, D], F32, kind="ExternalInput").ap()
    for name, shp in WEIGHT_SPECS:
        I[name] = nc.dram_tensor(name, shp, F32, kind="ExternalInput").ap()
    cshapes = {"c_ident": [128, 128], "c_cdft": [GC, 2 * GC], "c_fa": [128, N2, 3, 128],
               "c_fb": [2 * N2, N2], "c_cos": [DR, S], "c_sin": [DR, S]}
    for name, shp in cshapes.items():
        I[name] = nc.dram_tensor(name, shp, BF16 if name in ("c_fa", "c_fb") else F32, kind="ExternalInput").ap()
    y_out = nc.dram_tensor("y", [S, D], F32, kind="ExternalOutput").ap()

    skind = "ExternalOutput" if dbg else "Internal"

    def scratch(name, shape, dt):
        return nc.dram_tensor(name, shape, dt, kind=skind).ap()

    WIN = scratch("s_win", [D, 2176], BF16)
    WUQ = scratch("s_wuq", [QL, 2048], BF16)
    WUKV = scratch("s_wukv", [KVL, 2048], BF16)
    WOUT = scratch("s_wout", [D, D], BF16)
    WCQ = scratch("s_wcq", [D, D], BF16)
    WCKV = scratch("s_wckv", [D, 2 * D], BF16)
    WCO = scratch("s_wco", [D, D], BF16)
    WG = scratch("s_wg", [D, DFF], BF16)
    WU = scratch("s_wu", [D, DFF], BF16)
    WD = scratch("s_wd", [DFF, D], BF16)
    VS = scratch("s_vs", [S, NG, 2, GC], BF16)
    YS = scratch("s_ys", [NG, 2, N2, 128, GC], BF16)
    FS = scratch("s_fs", [S, NF], F32)
    QN = scratch("s_qn", [NH, DN, S], BF16)
    QR = scratch("s_qr", [NH, DR, S], BF16)
    KN = scratch("s_kn", [NH, DN, S], BF16)
    KR = scratch("s_kr", [DR, S], BF16)
    VV = scratch("s_vv", [NH, 128, S // 128, DV], BF16)
    OT = scratch("s_ot", [NH * DV, S], F32)
    X2 = scratch("s_x2", [S, D], F32)

    es = ExitStack()
    with es:
        T = Tracker(nc, es)
        op = T.op
        ins = T.ins

        def sb(stack, name, shape, dt):
            return stack.enter_context(nc.sbuf_tensor(name, shape, dt))

        PS = [es.enter_context(nc.psum_tensor("psb%d" % i, [128, 512], F32)) for i in range(8)]
        PSR = [Res() for _ in range(8)]
        ps_i = [0]

        def next_ps():
            i = ps_i[0] % 8
            ps_i[0] += 1
            return PS[i], PSR[i]

        ident_f = sb(es, "ident_f", [128, 128], F32)
        ident_b = sb(es, "ident_b", [128, 128], BF16)
        ones_b = sb(es, "ones_b", [128, 128], BF16)
        o512_b = sb(es, "o512_b", [128, 128], BF16)
        o1024_b = sb(es, "o1024_b", [128, 128], BF16)
        gvec = sb(es, "gvec", [128, 88], F32)
        cwv = sb(es, "cwv", [128, 4, FC], F32)
        gfin = sb(es, "gfin", [128, D], F32)
        R_const = Res()
        s_const = T.new_sem("d_const")

        G_OFF = {"norm_mix_g": 0, "q_norm_g": 16, "kv_norm_g": 20, "fourier_out_g": 24, "mla_out_g": 32,
                 "norm_cross_g": 40, "norm_mem_g": 56, "norm_ffn_g": 72}

        with ExitStack() as ph:
            va = sb(ph, "va", [88, 128], F32)
            vb = sb(ph, "vb", [88, 128], F32)
            vc = sb(ph, "vc", [88, 128], F32)
            R_v = Res()
            ins("sp", "dma_start", out=ident_f[:], in_=I["c_ident"], writes=[R_const], dma=s_const)
            for name, off in G_OFF.items():
                n = I[name].shape[0] // 128
                src = I[name].rearrange("(k p) -> k p", p=128)
                ins("sp", "dma_start", out=va[off:off + n, :], in_=src,
                   writes=[R_v], dma=s_const)
            for j in range(3):
                dst = (vb, vb, vc)[j]
                o = (0, 44, 0)[j]
                src = I["conv_w"][j].rearrange("(k p) -> k p", p=128)
                ins("sp", "dma_start", out=dst[o:o + 44, :], in_=src,
                   writes=[R_v], dma=s_const)
            src = I["conv_b"].rearrange("(k p) -> k p", p=128)
            ins("sp", "dma_start", out=vc[44:88, :], in_=src, writes=[R_v], dma=s_const)
            ins("sp", "dma_start", out=gfin[:], in_=I["final_norm_g"].partition_broadcast(128),
               writes=[R_const], dma=s_const)
            ins("dve", "tensor_copy", out=ident_b[:], in_=ident_f[:], reads=[R_const], writes=[R_const])
            ins("dve", "memset", ones_b[:], 1.0, writes=[R_const])
            ins("dve", "memset", o512_b[:], 1.0 / 512, writes=[R_const])
            ins("dve", "memset", o1024_b[:], 1.0 / 1024, writes=[R_const])
            for src_t, dst_ap in ((va, gvec[:, :]), (vb, cwv[:, 0:2, :]), (vc, cwv[:, 2:4, :])):
                pt, pr = next_ps()
                ins("pe", "transpose", out=pt[:, 0:88], in_=src_t[:, :], identity=ident_f[0:88, 0:88],
                   reads=[R_v, R_const], writes=[pr])
                if dst_ap.ndim == 3:
                    srcv = pt[:, 0:88].rearrange("p (a b) -> p a b", a=2)
                else:
                    srcv = pt[:, 0:88]
                ins("dve", "tensor_copy", out=dst_ap, in_=srcv,
                   reads=[pr], writes=[R_const])
            T.barrier()

        deferred = []
        with ExitStack() as ph:
            NSL = 3
            stf = [sb(ph, "stf%d" % i, [128, 2176], F32) for i in range(NSL)]
            stb = [sb(ph, "stb%d" % i, [128, 2176], BF16) for i in range(NSL)]
            Rf = [Res() for _ in range(NSL)]
            Rb = [Res() for _ in range(NSL)]
            s_ld = [T.new_sem("d_p0l%d" % i) for i in range(NSL)]
            s_st = [T.new_sem("d_p0s%d" % i) for i in range(NSL)]
            cnt = [0]

            def conv_block(src_ap, dst_ap, g_ap, pieces, kc_i):
                i = cnt[0] % NSL
                ceng = ("act", "dve")[cnt[0] % 2]
                cnt[0] += 1
                ws = src_ap.shape[1]
                wd = dst_ap.shape[1]
                ins("sp", "dma_start", out=stf[i][:, 0:ws], in_=src_ap, writes=[Rf[i]], dma=s_ld[i])
                for (dv, sv, sign) in pieces:
                    o_ap = dv(stb[i])
                    i_ap = sv(stf[i])
                    if sign < 0 or ceng == "dve":
                        if g_ap is None:
                            ins("dve", "tensor_scalar",
                                out=o_ap, in0=i_ap, scalar1=float(sign), scalar2=None, op0=ALU.mult,
                                reads=[Rf[i], R_const], writes=[Rb[i]])
                        else:
                            ins("dve", "tensor_scalar",
                                out=o_ap, in0=i_ap, scalar1=g_ap, scalar2=float(sign), op0=ALU.mult, op1=ALU.mult,
                                reads=[Rf[i], R_const], writes=[Rb[i]])
                    else:
                        if g_ap is None:
                            ins("act", "copy", out=o_ap, in_=i_ap,
                               reads=[Rf[i], R_const], writes=[Rb[i]])
                        else:
                            ins("act", "mul", out=o_ap, in_=i_ap, mul=g_ap,
                               reads=[Rf[i], R_const], writes=[Rb[i]])
                ins("pool", "dma_start", out=dst_ap, in_=stb[i][:, 0:wd], reads=[Rb[i]], dma=s_st[i])

            def simple(src, dst, gname, goff2=0):
                din, dout = src.shape
                for kc in range(din // 128):
                    g_ap = None
                    if gname is not None:
                        col = G_OFF[gname] + kc - goff2
                        g_ap = gvec[:, col:col + 1]
                    for c0 in range(0, dout, 2048):
                        w = min(2048, dout - c0)
                        deferred.append((src[kc * 128:(kc + 1) * 128, c0:c0 + w], dst[kc * 128:(kc + 1) * 128, c0:c0 + w], g_ap, w))

            for kc in range(KC):
                g_ap = gvec[:, kc:kc + 1]
                conv_block(I["w_in"][kc * 128:(kc + 1) * 128, :], WIN[kc * 128:(kc + 1) * 128, :], g_ap,
                           [(lambda t: t[:, 0:2112], lambda t: t[:, 0:2112], 1),
                            (lambda t: t[:, 2112:2144], lambda t: t[:, 2080:2112], -1),
                            (lambda t: t[:, 2144:2176], lambda t: t[:, 2048:2080], 1)], kc)
            for kc in range(QL // 128):
                g_ap = gvec[:, 16 + kc:17 + kc]

                def sv(t, a, b):
                    return t[:, 0:1536].rearrange("p (h c) -> p h c", c=192)[:, :, a:b]

                conv_block(I["w_uq"][kc * 128:(kc + 1) * 128, :], WUQ[kc * 128:(kc + 1) * 128, :], g_ap,
                           [(lambda t: t[:, 0:1024].rearrange("p (h c) -> p h c", c=128), lambda t: sv(t, 0, 128), 1),
                            (lambda t: t[:, 1024:1536].rearrange("p (h c) -> p h c", c=64), lambda t: sv(t, 128, 192), 1),
                            (lambda t: t[:, 1536:2048].rearrange("p (h c) -> p h c", c=64)[:, :, 0:32], lambda t: sv(t, 160, 192), -1),
                            (lambda t: t[:, 1536:2048].rearrange("p (h c) -> p h c", c=64)[:, :, 32:64], lambda t: sv(t, 128, 160), 1)],
                           kc)
            for kc in range(KVL // 128):
                g_ap = gvec[:, 20 + kc:21 + kc]

                def sv2(t, a, b):
                    return t[:, 0:2048].rearrange("p (h c) -> p h c", c=256)[:, :, a:b]

                conv_block(I["w_ukv"][kc * 128:(kc + 1) * 128, :], WUKV[kc * 128:(kc + 1) * 128, :], g_ap,
                           [(lambda t: t[:, 0:1024].rearrange("p (h c) -> p h c", c=128), lambda t: sv2(t, 0, 128), 1),
                            (lambda t: t[:, 1024:2048].rearrange("p (h c) -> p h c", c=128), lambda t: sv2(t, 128, 256), 1)],
                           kc)
            if phases >= 4:
                simple(I["w_out"], WOUT, "fourier_out_g")
                simple(I["w_cq"], WCQ, "norm_cross_g")
                simple(I["w_ckv"], WCKV, "norm_mem_g")
                simple(I["w_co"], WCO, None)
            if phases >= 6:
                simple(I["w_gate"], WG, "norm_ffn_g")
                simple(I["w_up"], WU, "norm_ffn_g")
                simple(I["w_down"], WD, None)
            T.barrier()


        def load_w(stack_slot, Rslot, ssem, dram_ap, nkc, wcols):
            flat = stack_slot[:].rearrange("p a b -> p (a b)")
            dst = flat[:, 0:nkc * wcols].rearrange("p (k c) -> p k c", k=nkc)
            src = dram_ap.rearrange("(k p) c -> p k c", p=128)
            ins("sp", "dma_start", out=dst, in_=src, writes=[Rslot], dma=ssem)
            return dst


        def make_deferred_emitter(stack, tag, ceng):
            NS2 = 2
            dstf = [sb(stack, "dstf%s%d" % (tag, i), [128, 2048], F32) for i in range(NS2)]
            dstb = [sb(stack, "dstb%s%d" % (tag, i), [128, 2048], BF16) for i in range(NS2)]
            dRf = [Res() for _ in range(NS2)]
            dRb = [Res() for _ in range(NS2)]
            d_ld = [T.new_sem("d_dl%s%d" % (tag, i)) for i in range(NS2)]
            d_st = [T.new_sem("d_ds%s%d" % (tag, i)) for i in range(NS2)]
            cnt = [0]

            def emit(n):
                for _ in range(n):
                    if not deferred:
                        return
                    src_ap, dst_ap, g_ap, w = deferred.pop(0)
                    i = cnt[0] % NS2
                    cnt[0] += 1
                    ins("sp", "dma_start", out=dstf[i][:, 0:w], in_=src_ap, writes=[dRf[i]], dma=d_ld[i])
                    sc1 = g_ap if g_ap is not None else 1.0
                    if ceng == "act":
                        if g_ap is None:
                            ins("act", "copy", out=dstb[i][:, 0:w], in_=dstf[i][:, 0:w], reads=[dRf[i]], writes=[dRb[i]])
                        else:
                            ins("act", "mul", out=dstb[i][:, 0:w], in_=dstf[i][:, 0:w], mul=g_ap,
                                reads=[dRf[i], R_const], writes=[dRb[i]])
                    else:
                        ins(ceng, "tensor_scalar", out=dstb[i][:, 0:w], in0=dstf[i][:, 0:w], scalar1=sc1, scalar2=1.0,
                            op0=ALU.mult, op1=ALU.mult, reads=[dRf[i], R_const], writes=[dRb[i]])
                    ins("pool", "dma_start", out=dst_ap, in_=dstb[i][:, 0:w], reads=[dRb[i]], dma=d_st[i])
            return emit

        class WStream:
            def __init__(self, slots, Rs, sems, specs):
                self.slots, self.Rs, self.sems, self.specs = slots, Rs, sems, specs
                self.nxt = 0
                self.views = {}

            def get(self, k):
                n = len(self.slots)
                while self.nxt <= min(k + n - 1, len(self.specs) - 1):
                    j = self.nxt
                    ap, nkc, wc = self.specs[j]
                    self.views[j] = load_w(self.slots[j % n], self.Rs[j % n], self.sems[j % n], ap, nkc, wc)
                    self.nxt += 1
                return self.views[k], self.Rs[k % n]

        stat = sb(es, "stat", [128, 64], F32)
        Rstat = [Res() for _ in range(16)]
        stat_i = [0]

        def next_stat():
            i = stat_i[0] % 16
            stat_i[0] += 1
            return stat[:, 4 * i:4 * i + 4], Rstat[i]

        def tok_norm_a(x_ap, Rx, xs_t, Rxs, junk_t, width=D):
            st_ap, Rs = next_stat()
            ins("act", "activation", out=junk_t[:, 0:width], in_=x_ap, func=AF.Square,
                scale=float(width) ** -0.5, accum_out=st_ap[:, 0:1], reads=[Rx], writes=[Rs])
            ins("act", "activation", out=st_ap[:, 1:2], in_=st_ap[:, 0:1], func=AF.Sqrt, bias=eps_t[:, 0:1],
                reads=[Rs, R_const], writes=[Rs])
            ins("dve", "reciprocal", out=st_ap[:, 2:3], in_=st_ap[:, 1:2], reads=[Rs], writes=[Rs])
            ins("dve", "tensor_scalar", out=xs_t[:, 0:width], in0=x_ap, scalar1=st_ap[:, 2:3], scalar2=None,
                op0=ALU.mult, reads=[Rx, Rs], writes=[Rxs])

        def tok_norm_b(xs_t, Rxs, hT_t, RhT, j, width=D):
            nk = width // 128
            for g0 in range(0, nk, 8):
                gn = min(8, nk - g0)
                pt, pr = next_ps()
                ptb = pt.bitcast(BF16)
                for k in range(gn):
                    ins("pe", "transpose", out=ptb[:, k * 128:(k + 1) * 128],
                        in_=xs_t[:, (g0 + k) * 128:(g0 + k + 1) * 128], identity=ident_b[:, :],
                        reads=[Rxs, R_const], writes=[pr])
                src = ptb[:, 0:gn * 128].rearrange("p (k c) -> p k c", k=gn)
                dst = hT_t[:, g0:g0 + gn, j * 128:(j + 1) * 128]
                if (g0 // 8) % 2 == 0:
                    ins("act", "copy", out=dst, in_=src, reads=[pr], writes=[RhT])
                else:
                    ins("dve", "tensor_copy", out=dst, in_=src, reads=[pr], writes=[RhT])

        def tok_norm_T(x_ap, Rx, xs_t, Rxs, junk_t, hT_t, RhT, j, width=D):
            tok_norm_a(x_ap, Rx, xs_t, Rxs, junk_t, width)
            tok_norm_b(xs_t, Rxs, hT_t, RhT, j, width)

        def norm_pipe(n, get_x, xs_l, Rxs_l, junk_t, hT_t, RhT, width=D, pre_load=None, done_a=0):
            for j in range(n + 1):
                if j < n and j >= done_a:
                    if pre_load is not None:
                        pre_load(j)
                    x_ap, Rx = get_x(j)
                    tok_norm_a(x_ap, Rx, xs_l[j % 2], Rxs_l[j % 2], junk_t, width)
                if j >= 1:
                    tok_norm_b(xs_l[(j - 1) % 2], Rxs_l[(j - 1) % 2], hT_t, RhT, j - 1, width)

        eps_t = sb(es, "eps_t", [128, 1], F32)
        ins("dve", "memset", eps_t[:], EPS, writes=[R_const])

        def fm_rstd(src_chunks, Rsrc, sq_t, Rsq, ones_t, rstd_t, Rrstd, n):
            nchunks = len(src_chunks)
            for c, a in enumerate(src_chunks):
                ins("act", "activation", out=sq_t[:, c, 0:n], in_=a, func=AF.Square,
                   reads=[Rsrc], writes=[Rsq])
            pt, pr = next_ps()
            for c in range(nchunks):
                ins("pe", "matmul", pt[:, 0:n], ones_t[:, :], sq_t[:, c, 0:n], start=(c == 0),
                                                 stop=(c == nchunks - 1), reads=[Rsq, R_const], writes=[pr])
            ins("act", "activation", out=rstd_t[:, 0:n], in_=pt[:, 0:n], func=AF.Sqrt, bias=eps_t[:, 0:1],
               reads=[pr, R_const], writes=[Rrstd])
            ins("dve", "reciprocal", out=rstd_t[:, 0:n], in_=rstd_t[:, 0:n], reads=[Rrstd], writes=[Rrstd])

        with ExitStack() as ph:
            wsl = [sb(ph, "w1_%d" % i, [128, 16, 512], BF16) for i in range(3)]
            Rw = [Res() for _ in range(3)]
            s_w = [T.new_sem("d_w1_%d" % i) for i in range(3)]
            wspecs = []
            for it in range(NT):
                for c in range(4):
                    wspecs.append((WIN[:, c * 512:(c + 1) * 512], 16, 512))
                wspecs.append((WIN[:, 2048:2176], 16, 128))
                wspecs.append((WUQ[:, :], 4, 2048))
                wspecs.append((WUKV[:, :], 4, 2048))
            WS = WStream(wsl, Rw, s_w, wspecs)

            cdft_f = sb(ph, "cdft_f", [128, 2, 512], F32)
            cdft_b = sb(ph, "cdft_b", [128, 2, 512], BF16)
            R_cd = Res()
            ins("sp", "dma_start", out=cdft_f[:], in_=I["c_cdft"].rearrange("(k p) c -> p k c", p=128),
               writes=[R_cd], dma=s_const)
            ins("dve", "tensor_copy", out=cdft_b[:], in_=cdft_f[:], reads=[R_cd], writes=[R_cd])
            xt = [sb(ph, "xt%d" % i, [128, D], F32) for i in range(2)]
            Rxt = [Res() for _ in range(2)]
            s_xt = [T.new_sem("d_xt%d" % i) for i in range(2)]
            xs = [sb(ph, "xs%d" % i, [128, D], BF16) for i in range(2)]
            Rxs = [Res() for _ in range(2)]
            junk = sb(ph, "junk", [128, D], BF16)
            hT = sb(ph, "hT", [128, KC, TT], BF16)
            RhT = Res()
            fT = sb(ph, "fT", [128, 8, TT], BF16)
            RfT = Res()
            vst = [sb(ph, "vst%d" % i, [128, 2048], BF16) for i in range(2)]
            Rvst = [Res() for _ in range(2)]
            s_vst = [T.new_sem("d_vst%d" % i) for i in range(2)]
            cset = []
            for nm in ("q", "kv"):
                cset.append(dict(
                    c=sb(ph, "c_%s" % nm, [128, 4, TT], F32), Rc=Res(),
                    sq=sb(ph, "sq_%s" % nm, [128, 4, TT], BF16), Rsq=Res(),
                    rstd=sb(ph, "rstd_%s" % nm, [128, TT], F32), Rrstd=Res(),
                    cn=sb(ph, "cn_%s" % nm, [128, 4, TT], BF16), Rcn=Res()))
            qk_o = sb(ph, "qk_o", [128, NH, TT], BF16)
            Rqk_o = Res()
            s_qk_o = T.new_sem("d_qk_o")
            qr_o = sb(ph, "qr_o", [64, NH, TT], BF16)
            Rqr_o = Res()
            s_qr_o = T.new_sem("d_qr_o")
            v_o = sb(ph, "v_o", [128, NH, 4, DV], BF16)
            Rv_o = Res()
            s_v_o = T.new_sem("d_v_o")
            kr_o = sb(ph, "kr_o", [64, TT], BF16)
            Rkr_o = Res()
            s_kr_o = T.new_sem("d_kr_o")
            cs_t = sb(ph, "cs_t", [64, 2, TT], F32)
            Rcs = Res()
            s_cs = T.new_sem("d_cs")
            rt = sb(ph, "rt", [64, 2, TT], F32)
            Rrt = Res()

            x_tiled = I["x"].rearrange("(n p) d -> n p d", p=128)
            xload_done = {}

            def load_x(g):
                if g in xload_done or g >= S // 128:
                    return
                xload_done[g] = True
                ins("sp", "dma_start", out=xt[g % 2][:, :], in_=x_tiled[g], writes=[Rxt[g % 2]], dma=s_xt[g % 2])

            def rope_combine(p_plain, pr_plain, p_rot, pr_rot, out_ap, Rout):
                ins("dve", "tensor_tensor", out=rt[:, 0, :], in0=p_plain[0:64, :], in1=cs_t[:, 0, :], op=ALU.mult,
                   reads=[pr_plain, Rcs], writes=[Rrt])
                ins("dve", "tensor_tensor", out=rt[:, 1, :], in0=p_rot[0:64, :], in1=cs_t[:, 1, :], op=ALU.mult,
                   reads=[pr_rot, Rcs], writes=[Rrt])
                ins("dve", "tensor_tensor", out=out_ap, in0=rt[:, 0, :], in1=rt[:, 1, :], op=ALU.add,
                   reads=[Rrt], writes=[Rout])

            evq = [0]

            def evac(dst, src, pr, Rdst):
                evq[0] += 1
                if evq[0] % 2 == 0:
                    ins("act", "copy", out=dst, in_=src, reads=[pr], writes=[Rdst])
                else:
                    ins("dve", "tensor_copy", out=dst, in_=src, reads=[pr], writes=[Rdst])

            for it in range(NT):
                t0 = it * TT
                ins("sp", "dma_start", out=cs_t[:, 0, :], in_=I["c_cos"][:, t0:t0 + TT], writes=[Rcs], dma=s_cs)
                ins("sp", "dma_start", out=cs_t[:, 1, :], in_=I["c_sin"][:, t0:t0 + TT], writes=[Rcs], dma=s_cs)
                if it == 0:
                    load_x(0)
                    load_x(1)
                norm_pipe(4, lambda j, it=it: (xt[(it * 4 + j) % 2][:, :], Rxt[(it * 4 + j) % 2]), xs, Rxs, junk, hT, RhT,
                          pre_load=lambda j, it=it: load_x(it * 4 + j), done_a=(0 if it == 0 else 2))
                wb = it * 7
                for m in range(8):
                    wv, wr = WS.get(wb + m // 4)
                    pt, pr = next_ps()
                    for kc in range(KC):
                        ins("pe", "matmul",
                            pt[:, :], wv[:, kc, (m % 4) * 128:(m % 4 + 1) * 128], hT[:, kc, :], start=(kc == 0), stop=(kc == KC - 1),
                            reads=[wr, RhT], writes=[pr])
                    evac(fT[:, m, :], pt[:, :], pr, RfT)
                for j in range(4):
                    g = it * 4 + j
                    vs_t, vs_r, vs_s = vst[g % 2], Rvst[g % 2], s_vst[g % 2]
                    for gi in range(NG):
                        pt, pr = next_ps()
                        for kc in range(2):
                            ins("pe", "matmul",
                                pt[:, :], fT[:, 2 * gi + kc, j * 128:(j + 1) * 128], cdft_b[:, kc, :], start=(kc == 0), stop=(kc == 1),
                                reads=[RfT, R_cd], writes=[pr])
                        evac(vs_t[:, gi * 512:(gi + 1) * 512], pt[:, :], pr, vs_r)
                    dst = VS[g * 128:(g + 1) * 128].rearrange("t g r c -> t (g r c)")
                    ins("pool", "dma_start", out=dst, in_=vs_t[:, :], reads=[vs_r], dma=vs_s)
                if it + 1 < NT:
                    for j in range(2):
                        g = (it + 1) * 4 + j
                        load_x(g)
                        tok_norm_a(xt[g % 2][:, :], Rxt[g % 2], xs[g % 2], Rxs[g % 2], junk)
                for ci in range(2):
                    cs_ = cset[ci]
                    wv, wr = WS.get(wb + 2 + ci)
                    for c in range(4):
                        pt, pr = next_ps()
                        for kc in range(KC):
                            ins("pe", "matmul",
                                pt[:, :], wv[:, kc, c * 128:(c + 1) * 128], hT[:, kc, :], start=(kc == 0), stop=(kc == KC - 1),
                                reads=[wr, RhT], writes=[pr])
                        ins("dve", "tensor_copy", out=cs_["c"][:, c, :], in_=pt[:, :],
                           reads=[pr], writes=[cs_["Rc"]])
                    fm_rstd([cs_["c"][:, c, :] for c in range(4)], cs_["Rc"], cs_["sq"], cs_["Rsq"], o512_b,
                            cs_["rstd"], cs_["Rrstd"], TT)
                    for c in range(4):
                        ins("dve", "tensor_tensor", out=cs_["cn"][:, c, :], in0=cs_["c"][:, c, :],
                                                                          in1=cs_["rstd"][:, :], op=ALU.mult,
                           reads=[cs_["Rc"], cs_["Rrstd"]], writes=[cs_["Rcn"]])
                wk, r_ = WS.get(wb + 4)
                pp = []
                for half in range(2):
                    pt, pr = next_ps()
                    for kc in range(KC):
                        ins("pe", "matmul",
                            pt[0:64, :], wk[:, kc, half * 64:(half + 1) * 64], hT[:, kc, :], start=(kc == 0), stop=(kc == KC - 1),
                            reads=[r_, RhT], writes=[pr])
                    pp.append((pt, pr))
                rope_combine(pp[0][0], pp[0][1], pp[1][0], pp[1][1], kr_o[:, :], Rkr_o)
                ins("pool", "dma_start", out=KR[:, t0:t0 + TT], in_=kr_o[:, :], reads=[Rkr_o], dma=s_kr_o)
                wq, r_ = WS.get(wb + 5)
                cn = cset[0]["cn"]
                Rcn = cset[0]["Rcn"]
                for h in range(NH):
                    pt, pr = next_ps()
                    for kc in range(4):
                        ins("pe", "matmul",
                            pt[:, :], wq[:, kc, h * 128:(h + 1) * 128], cn[:, kc, :], start=(kc == 0), stop=(kc == 3),
                            reads=[r_, Rcn], writes=[pr])
                    evac(qk_o[:, h, :], pt[:, :], pr, Rqk_o)
                ins("pool", "dma_start", out=QN[:, :, t0:t0 + TT].rearrange("h d t -> d h t"), in_=qk_o[:, :, :],
                   reads=[Rqk_o], dma=s_qk_o)
                for h in range(NH):
                    pp = []
                    for half in range(2):
                        pt, pr = next_ps()
                        c0 = 1024 + half * 512 + h * 64
                        for kc in range(4):
                            ins("pe", "matmul",
                                pt[0:64, :], wq[:, kc, c0:c0 + 64], cn[:, kc, :], start=(kc == 0), stop=(kc == 3),
                                reads=[r_, Rcn], writes=[pr])
                        pp.append((pt, pr))
                    rope_combine(pp[0][0], pp[0][1], pp[1][0], pp[1][1], qr_o[:, h, :], Rqr_o)
                ins("pool", "dma_start", out=QR[:, :, t0:t0 + TT].rearrange("h d t -> d h t"), in_=qr_o[:, :, :],
                   reads=[Rqr_o], dma=s_qr_o)
                wkv, r_ = WS.get(wb + 6)
                cn = cset[1]["cn"]
                Rcn = cset[1]["Rcn"]
                for h in range(NH):
                    pt, pr = next_ps()
                    for kc in range(4):
                        ins("pe", "matmul",
                            pt[:, :], wkv[:, kc, h * 128:(h + 1) * 128], cn[:, kc, :], start=(kc == 0), stop=(kc == 3),
                            reads=[r_, Rcn], writes=[pr])
                    evac(qk_o[:, h, :], pt[:, :], pr, Rqk_o)
                ins("pool", "dma_start", out=KN[:, :, t0:t0 + TT].rearrange("h d t -> d h t"), in_=qk_o[:, :, :],
                   reads=[Rqk_o], dma=s_qk_o)
                for j in range(4):
                    for n in range(2):
                        pt, pr = next_ps()
                        for kc in range(4):
                            ins("pe", "matmul",
                                pt[:, :], cn[:, kc, j * 128:(j + 1) * 128], wkv[:, kc, 1024 + n * 512:1024 + (n + 1) * 512],
                                start=(kc == 0), stop=(kc == 3), reads=[r_, Rcn], writes=[pr])
                        evac(v_o[:, 4 * n:4 * n + 4, j, :], pt[:, :].rearrange("p (h d) -> p h d", h=4), pr, Rv_o)
                ins("pool", "dma_start", out=VV[:, :, 4 * it:4 * it + 4, :].rearrange("h p j d -> p h j d"),
                                                        in_=v_o[:, :, :, :], reads=[Rv_o], dma=s_v_o)
            T.barrier()


        if phases >= 2:
          with ExitStack() as ph:
            At = sb(ph, "At", [128, N2, 512], BF16)
            fa_b = sb(ph, "fa_b", [128, N2, 3, 128], BF16)
            fb_b = sb(ph, "fb_b", [2 * N2, N2], BF16)
            NFP = 4 if N2 >= 4 else 1
            FPW = N2 // NFP
            R_fap = [Res() for _ in range(NFP)]
            R_fb = Res()
            s_f = T.new_sem("d_fft_c")
            for p_ in range(NFP):
                ins("sp", "dma_start", out=fa_b[:, p_ * FPW:(p_ + 1) * FPW], in_=I["c_fa"][:, p_ * FPW:(p_ + 1) * FPW],
                    writes=[R_fap[p_]], dma=s_f)
            s_fbf = T.new_sem("d_fft_fb")
            ins("sp", "dma_start", out=fb_b[:, :], in_=I["c_fb"], writes=[R_fb], dma=s_fbf)
            SB = min(8, N2)
            if dbg:
                d_fa = nc.dram_tensor("d_fa", [128, N2, 3, 128], BF16, kind="ExternalOutput").ap()
                s_dbg = T.new_sem("d_dbg")
                ins("sp", "dma_start", out=d_fa, in_=fa_b[:, :, :, :], reads=R_fap, dma=s_dbg)
            NP = 4 if N2 >= 4 else 1
            PW = N2 // NP
            R_A = [Res() for _ in range(NP)]
            s_A = [T.new_sem("d_A%d" % i) for i in range(NP)]
            ystg = [sb(ph, "ystg%d" % i, [128, SB, 512], BF16) for i in range(2)]
            R_ystg = [Res() for _ in range(2)]
            s_ystg = [T.new_sem("d_ystg%d" % i) for i in range(2)]
            NB1 = N2 // SB
            R_YS = [[Res() for _ in range(2 * NB1)] for _ in range(NG)]
            KB = 16
            y2 = [sb(ph, "y2_%d" % i, [2 * N2, KB, 256], BF16) for i in range(2)]
            R_y2 = [Res() for _ in range(2)]
            s_y2 = [T.new_sem("d_y2_%d" % i) for i in range(2)]
            fo = [sb(ph, "fo%d" % i, [N2, KB, 256], F32) for i in range(2)]
            R_fo = [Res() for _ in range(2)]
            s_fo = [T.new_sem("d_fo%d" % i) for i in range(2)]
            VSv = VS.rearrange("(s1 s2) g r c -> s1 s2 g (r c)", s2=N2)
            FSv = FS.rearrange("(k2 k1) f -> k2 k1 f", k1=128)
            cnt1 = [0]
            cnt2 = [0]

            def stage1(g):
                for p in range(NP):
                    ins("sp", "dma_start", out=At[:, p * PW:(p + 1) * PW, :], in_=VSv[:, p * PW:(p + 1) * PW, g, :],
                        writes=[R_A[p]], dma=s_A[p])
                for b in range(NB1):
                    i = cnt1[0] % 2
                    cnt1[0] += 1
                    for sl in range(SB):
                        s2 = b * SB + sl
                        ra = R_A[s2 // PW]
                        pt, pr = next_ps()
                        ar = At[:, s2, 0:256]
                        ai = At[:, s2, 256:512]
                        ins("pe", "matmul", pt[:, 0:256], fa_b[:, s2, 0, :], ar, start=True, stop=False, reads=[R_fap[s2 // FPW], ra], writes=[pr])
                        ins("pe", "matmul", pt[:, 0:256], fa_b[:, s2, 2, :], ai, start=False, stop=True, reads=[R_fap[s2 // FPW], ra], writes=[pr])
                        ins("pe", "matmul", pt[:, 256:512], fa_b[:, s2, 1, :], ar, start=True, stop=False, reads=[R_fap[s2 // FPW], ra], writes=[pr])
                        ins("pe", "matmul", pt[:, 256:512], fa_b[:, s2, 0, :], ai, start=False, stop=True, reads=[R_fap[s2 // FPW], ra], writes=[pr])
                        if sl % 2 == 0:
                            ins("act", "copy", out=ystg[i][:, sl, :], in_=pt[:, :], reads=[pr], writes=[R_ystg[i]])
                        else:
                            ins("dve", "tensor_copy", out=ystg[i][:, sl, :], in_=pt[:, :], reads=[pr], writes=[R_ystg[i]])
                    for r_ in range(2):
                        dst = YS[g, r_].rearrange("s k c -> k s c")[:, b * SB:(b + 1) * SB, :]
                        ins("pool", "dma_start", out=dst, in_=ystg[i][:, :, r_ * 256:(r_ + 1) * 256],
                            reads=[R_ystg[i]], writes=[R_YS[g][2 * b + r_]], dma=s_ystg[i])

            def stage2(g):
                Y2v = YS[g].rearrange("r s k c -> (r s) k c")
                for kb in range(128 // KB):
                    i = cnt2[0] % 2
                    cnt2[0] += 1
                    ins("sp", "dma_start", out=y2[i][:, :, :], in_=Y2v[:, kb * KB:(kb + 1) * KB, :], reads=R_YS[g],
                        writes=[R_y2[i]], dma=s_y2[i])
                    for kk in range(0, KB, 2):
                        pt, pr = next_ps()
                        ins("pe", "matmul", pt[0:N2, :], fb_b[:, :], y2[i][:, kk:kk + 2, :].rearrange("p a c -> p (a c)"),
                            start=True, stop=True, reads=[R_fb, R_y2[i]], writes=[pr])
                        dsto = fo[i][:, kk:kk + 2, :].rearrange("p a c -> p (a c)")
                        if (kk // 2) % 2 == 0:
                            ins("act", "copy", out=dsto, in_=pt[0:N2, :], reads=[pr], writes=[R_fo[i]])
                        else:
                            ins("dve", "tensor_copy", out=dsto, in_=pt[0:N2, :], reads=[pr], writes=[R_fo[i]])
                    ins("pool", "dma_start", out=FSv[:, kb * KB:(kb + 1) * KB, g * 256:(g + 1) * 256], in_=fo[i][:, :, :],
                        reads=[R_fo[i]], dma=s_fo[i])

            stage1(0)
            for g in range(1, NG):
                stage1(g)
                stage2(g - 1)
            stage2(NG - 1)
            T.barrier()


        if phases >= 3:
          with ExitStack() as ph:
            NKC = S // 128
            NQT = S // TT
            kr_sb = sb(ph, "kr_sb", [128, S], BF16)
            R_kr = Res()
            s_kr = T.new_sem("d_kr")
            ins("dve", "memset", kr_sb[64:128, :], 0.0, writes=[R_kr])
            emit_def = make_deferred_emitter(ph, "a", "act")
            kn_sb = [sb(ph, "kn_sb%d" % i, [128, S], BF16) for i in range(2)]
            v_sb = [sb(ph, "v_sb%d" % i, [128, NKC, DV], BF16) for i in range(2)]
            R_kn = [Res() for _ in range(2)]
            R_v = [Res() for _ in range(2)]
            s_kn = [T.new_sem("d_kn%d" % i) for i in range(2)]
            s_v = [T.new_sem("d_v%d" % i) for i in range(2)]
            qn_sb = [sb(ph, "qn_sb%d" % i, [128, TT], BF16) for i in range(2)]
            qr_sb = [sb(ph, "qr_sb%d" % i, [128, TT], BF16) for i in range(2)]
            R_q = [Res() for _ in range(2)]
            for i in range(2):
                ins("dve", "memset", qr_sb[i][64:128, :], 0.0, writes=[R_q[i]])
            accL = [sb(ph, "accL%d" % i, [128, TT], F32) for i in range(4)]
            R_accL = [Res() for _ in range(4)]
            accb = sb(ph, "accb", [128, TT], BF16)
            R_accb = Res()
            accP = [sb(ph, "accP%d" % i, [128, TT], F32) for i in range(2)]
            R_accP = [Res() for _ in range(2)]
            s_q = [T.new_sem("d_q%d" % i) for i in range(2)]
            NPT = 8
            pT = [sb(ph, "pT%d" % i, [128, TT], BF16) for i in range(NPT)]
            R_pT = [Res() for _ in range(NPT)]
            rl = sb(ph, "rl", [128, TT], F32)
            R_rl = Res()
            ot_sb = [sb(ph, "ot_sb%d" % i, [128, TT], F32) for i in range(2)]
            R_ot = [Res() for _ in range(2)]
            s_ot = [T.new_sem("d_ot%d" % i) for i in range(2)]
            sm_scale = float(DN + DR) ** -0.5
            LOOK = 3
            POOL_SET = ()

            def load_head(h):
                i = h % 2
                ins("sp", "dma_start", out=kn_sb[i][:, :], in_=KN[h], writes=[R_kn[i]], dma=s_kn[i])
                ins("sp", "dma_start", out=v_sb[i][:, :, :], in_=VV[h], writes=[R_v[i]], dma=s_v[i])

            def load_q(n):
                h, qt = divmod(n, NQT)
                i = n % 2
                ins("sp", "dma_start", out=qn_sb[i][:, :], in_=QN[h, :, qt * TT:(qt + 1) * TT], writes=[R_q[i]], dma=s_q[i])
                ins("sp", "dma_start", out=qr_sb[i][0:64, :], in_=QR[h, :, qt * TT:(qt + 1) * TT], writes=[R_q[i]], dma=s_q[i])

            load_q(0)
            i0 = 0
            ins("sp", "dma_start", out=kn_sb[i0][:, :], in_=KN[0], writes=[R_kn[i0]], dma=s_kn[i0])
            ins("sp", "dma_start", out=kr_sb[0:64, :], in_=KR[:, :], writes=[R_kr], dma=s_kr)
            ins("sp", "dma_start", out=v_sb[i0][:, :, :], in_=VV[0], writes=[R_v[i0]], dma=s_v[i0])
            pcnt = [0]
            for h in range(NH):
                hi = h % 2
                for qt in range(NQT):
                    n = h * NQT + qt
                    qi = n % 2
                    if n + 1 < NH * NQT:
                        load_q(n + 1)
                    if qt == 1 and h + 1 < NH:
                        load_head(h + 1)
                    ob, obr = PS[4 + qi], PSR[4 + qi]
                    lb, lbr = PS[6 + qi], PSR[6 + qi]
                    pend = []
                    st_acc = {"p": False, "d": 0, "pe": False}

                    def s_tile(kc):
                        sbk, sbr = PS[pcnt[0] % 4], PSR[pcnt[0] % 4]
                        pi = pcnt[0] % NPT
                        pcnt[0] += 1
                        ins("pe", "matmul", sbk[:, :], kn_sb[hi][:, kc * 128:(kc + 1) * 128], qn_sb[qi][:, :], start=True, stop=False,
                            reads=[R_kn[hi], R_q[qi]], writes=[sbr])
                        ins("pe", "matmul", sbk[:, :], kr_sb[:, kc * 128:(kc + 1) * 128], qr_sb[qi][:, :], start=False, stop=True,
                            reads=[R_kr, R_q[qi]], writes=[sbr])
                        ins("act", "activation", out=pT[pi][:, :], in_=sbk[:, :], func=AF.Exp, scale=sm_scale,
                            reads=[sbr], writes=[R_pT[pi]])
                        return pi

                    def pv_tile(kc, pi):
                        ins("pe", "matmul", ob[:, :], v_sb[hi][:, kc, :], pT[pi][:, :], start=(kc == 0), stop=(kc == NKC - 1),
                            reads=[R_v[hi], R_pT[pi]], writes=[obr])
                        if kc % 8 == 7:
                            ins("pe", "matmul", lb[:, :], ones_b[:, :], pT[pi][:, :], start=(not st_acc["pe"]), stop=False,
                                reads=[R_const, R_pT[pi]], writes=[lbr])
                            st_acc["pe"] = True
                        elif (kc % 12) in POOL_SET:
                            if not st_acc["p"]:
                                st_acc["p"] = True
                                ins("pool", "tensor_copy", out=accP[qi][:, :], in_=pT[pi][:, :], reads=[R_pT[pi]], writes=[R_accP[qi]])
                            else:
                                ins("pool", "tensor_tensor", out=accP[qi][:, :], in0=accP[qi][:, :], in1=pT[pi][:, :], op=ALU.add,
                                    reads=[R_pT[pi], R_accP[qi]], writes=[R_accP[qi]])
                        else:
                            ai = 2 * qi + st_acc["d"] % 2
                            if st_acc["d"] < 2:
                                ins("dve", "tensor_copy", out=accL[ai][:, :], in_=pT[pi][:, :], reads=[R_pT[pi]], writes=[R_accL[ai]])
                            else:
                                ins("dve", "tensor_tensor", out=accL[ai][:, :], in0=accL[ai][:, :], in1=pT[pi][:, :], op=ALU.add,
                                    reads=[R_pT[pi], R_accL[ai]], writes=[R_accL[ai]])
                            st_acc["d"] += 1
                        if kc in (NKC // 4, (3 * NKC) // 4):
                            emit_def(1)

                    for kc in range(NKC + LOOK):
                        if kc < NKC:
                            pend.append((kc, s_tile(kc)))
                        if kc >= LOOK:
                            k0, p0 = pend.pop(0)
                            pv_tile(k0, p0)
                    ins("dve", "tensor_tensor", out=accL[2 * qi][:, :], in0=accL[2 * qi][:, :], in1=accL[2 * qi + 1][:, :], op=ALU.add,
                        reads=[R_accL[2 * qi], R_accL[2 * qi + 1]], writes=[R_accL[2 * qi]])
                    if st_acc["p"]:
                        ins("dve", "tensor_tensor", out=accb[:, :], in0=accL[2 * qi][:, :], in1=accP[qi][:, :], op=ALU.add,
                            reads=[R_accL[2 * qi], R_accP[qi]], writes=[R_accb])
                    else:
                        ins("dve", "tensor_copy", out=accb[:, :], in_=accL[2 * qi][:, :], reads=[R_accL[2 * qi]], writes=[R_accb])
                    ins("pe", "matmul", lb[:, :], ones_b[:, :], accb[:, :], start=(not st_acc["pe"]), stop=True,
                        reads=[R_const, R_accb], writes=[lbr])
                    ins("dve", "reciprocal", out=rl[:, :], in_=lb[:, :], reads=[lbr], writes=[R_rl])
                    ins("dve", "tensor_tensor", out=ot_sb[qi][:, :], in0=ob[:, :], in1=rl[:, :], op=ALU.mult,
                        reads=[obr, R_rl], writes=[R_ot[qi]])
                    ins("pool", "dma_start", out=OT[h * DV:(h + 1) * DV, qt * TT:(qt + 1) * TT], in_=ot_sb[qi][:, :],
                        reads=[R_ot[qi]], dma=s_ot[qi])
            emit_def(10 ** 6)
            T.barrier()


        if phases >= 4:
          with ExitStack() as ph:
            wsl = [sb(ph, "w4_%d" % i, [128, 16, 512], BF16) for i in range(3)]
            Rw = [Res() for _ in range(3)]
            s_w = [T.new_sem("d_w4_%d" % i) for i in range(3)]
            ckT = sb(ph, "ckT", [128, KC, NMEM], BF16)
            cv = sb(ph, "cv", [128, 2, D], BF16)
            R_ck, R_cv = Res(), Res()
            xs = [sb(ph, "xs4_%d" % i, [128, D], BF16) for i in range(2)]
            Rxs = [Res() for _ in range(2)]
            junk = sb(ph, "junk4", [128, D], BF16)
            evq = [0]

            def evac(dst, src, pr, Rdst):
                evq[0] += 1
                if evq[0] % 2 == 0:
                    ins("act", "copy", out=dst, in_=src, reads=[pr], writes=[Rdst])
                else:
                    ins("dve", "tensor_copy", out=dst, in_=src, reads=[pr], writes=[Rdst])

            with ExitStack() as pre:
                memx = [sb(pre, "memx%d" % i, [128, D], F32) for i in range(2)]
                Rmx = [Res() for _ in range(2)]
                s_mx = [T.new_sem("d_mx%d" % i) for i in range(2)]
                memT = sb(pre, "memT", [128, KC, NMEM], BF16)
                RmT = Res()
                wspecs = [(WCKV[:, c * 512:(c + 1) * 512], 16, 512) for c in range(8)]
                WS = WStream(wsl, Rw, s_w, wspecs)
                for j in range(2):
                    ins("sp", "dma_start", out=memx[j][:, :], in_=I["mem"][j * 128:(j + 1) * 128, :], writes=[Rmx[j]], dma=s_mx[j])
                    tok_norm_T(memx[j][:, :], Rmx[j], xs[j], Rxs[j], junk, memT, RmT, j)
                for m in range(KC):
                    wv, wr = WS.get(m // 4)
                    pt, pr = next_ps()
                    for kc in range(KC):
                        ins("pe", "matmul", pt[:, 0:NMEM], wv[:, kc, (m % 4) * 128:(m % 4 + 1) * 128], memT[:, kc, :],
                            start=(kc == 0), stop=(kc == KC - 1), reads=[wr, RmT], writes=[pr])
                    evac(ckT[:, m, :], pt[:, 0:NMEM], pr, R_ck)
                for n in range(4):
                    wv, wr = WS.get(4 + n)
                    for kk in range(2):
                        pt, pr = next_ps()
                        for kc in range(KC):
                            ins("pe", "matmul", pt[:, :], memT[:, kc, kk * 128:(kk + 1) * 128], wv[:, kc, :],
                                start=(kc == 0), stop=(kc == KC - 1), reads=[wr, RmT], writes=[pr])
                        evac(cv[:, kk, n * 512:(n + 1) * 512], pt[:, :], pr, R_cv)
                T.barrier()

            xt = sb(ph, "xt4", [128, 4, D], F32)
            Rxt = [Res() for _ in range(4)]
            s_xt = [T.new_sem("d_xt4_%d" % i) for i in range(4)]
            s_xo = [T.new_sem("d_xo4_%d" % i) for i in range(4)]
            otl = sb(ph, "otl", [128, NH, TT], F32)
            R_otl = Res()
            s_otl = T.new_sem("d_otl")
            fsl = [sb(ph, "fsl%d" % i, [128, NF], F32) for i in range(2)]
            R_fsl = [Res() for _ in range(2)]
            s_fsl = [T.new_sem("d_fsl%d" % i) for i in range(2)]
            sq = sb(ph, "sq4", [128, NH, TT], BF16)
            R_sq = Res()
            rstd = sb(ph, "rstd4", [128, TT], F32)
            R_rstd = Res()
            bufA = sb(ph, "bufA", [128, KC, TT], BF16)
            bufB = sb(ph, "bufB", [128, KC, TT], BF16)
            R_A4, R_B4 = Res(), Res()
            pTc = [sb(ph, "pTc%d" % i, [128, 2, TT], BF16) for i in range(2)]
            R_pTc = [Res() for _ in range(2)]
            rlc = sb(ph, "rlc", [128, TT], F32)
            R_rlc = Res()
            wspecs = []
            for it in range(NT):
                for W_ in (WOUT, WCQ, WCO):
                    for c in range(4):
                        wspecs.append((W_[:, c * 512:(c + 1) * 512], 16, 512))
            WS = WStream(wsl, Rw, s_w, wspecs)
            x_t4 = I["x"].rearrange("(n p) d -> n p d", p=128)
            x2_t4 = X2.rearrange("(n p) d -> n p d", p=128)
            fs_t4 = FS.rearrange("(n p) d -> n p d", p=128)
            OTv = OT.rearrange("(h d) t -> d h t", d=128)
            c_scale = float(CHD) ** -0.5
            hcnt = [0]

            pf_done = set()

            def load_otl(i):
                if ("o", i) in pf_done:
                    return
                pf_done.add(("o", i))
                ins("sp", "dma_start", out=otl[:, :, :], in_=OTv[:, :, i * TT:(i + 1) * TT], writes=[R_otl], dma=s_otl)

            def load_fs(g):
                if ("f", g) in pf_done:
                    return
                pf_done.add(("f", g))
                ins("sp", "dma_start", out=fsl[g % 2][:, :], in_=fs_t4[g], writes=[R_fsl[g % 2]], dma=s_fsl[g % 2])

            rstd2 = [rstd, sb(ph, "rstd4b", [128, TT], F32)]
            R_rstd2 = [R_rstd, Res()]
            xsf = [sb(ph, "xsf%d" % i, [128, NF], BF16) for i in range(2)]
            Rxsf = [Res() for _ in range(2)]

            def aout_part1(i):
                load_otl(i)
                ins("act", "activation", out=sq[:, :, :], in_=otl[:, :, :], func=AF.Square, reads=[R_otl], writes=[R_sq])

            def aout_part2(i):
                pt, pr = next_ps()
                for h in range(NH):
                    ins("pe", "matmul", pt[:, :], o1024_b[:, :], sq[:, h, :], start=(h == 0), stop=(h == NH - 1),
                        reads=[R_sq, R_const], writes=[pr])
                ins("act", "activation", out=rstd2[i % 2][:, :], in_=pt[:, :], func=AF.Sqrt, bias=eps_t[:, 0:1],
                    reads=[pr, R_const], writes=[R_rstd2[i % 2]])
                ins("dve", "reciprocal", out=rstd2[i % 2][:, :], in_=rstd2[i % 2][:, :], reads=[R_rstd2[i % 2]], writes=[R_rstd2[i % 2]])

            def proj_residual(actT, R_act, wbase):
                for n in range(4):
                    wv, wr = WS.get(wbase + n)
                    for j in range(4):
                        pt, pr = next_ps()
                        for kc in range(KC):
                            ins("pe", "matmul", pt[:, :], actT[:, kc, j * 128:(j + 1) * 128], wv[:, kc, :],
                                start=(kc == 0), stop=(kc == KC - 1), reads=[wr, R_act], writes=[pr])
                        ins("dve", "tensor_tensor", out=xt[:, j, n * 512:(n + 1) * 512], in0=pt[:, :],
                            in1=xt[:, j, n * 512:(n + 1) * 512], op=ALU.add, reads=[pr, Rxt[j]], writes=[Rxt[j]])

            for it in range(NT):
                t0 = it * TT
                wb = it * 12
                if it == 0:
                    aout_part1(0)
                    aout_part2(0)
                rs_, Rrs_ = rstd2[it % 2], R_rstd2[it % 2]
                for h in range(NH):
                    ins("dve", "tensor_tensor", out=bufA[:, 8 + h, :], in0=otl[:, h, :], in1=rs_[:, :], op=ALU.mult,
                        reads=[R_otl, Rrs_], writes=[R_A4])
                norm_pipe(4, lambda j, it=it: (fsl[(it * 4 + j) % 2][:, :], R_fsl[(it * 4 + j) % 2]), xsf, Rxsf, junk, bufA, R_A4,
                          width=NF, pre_load=lambda j, it=it: load_fs(it * 4 + j), done_a=(0 if it == 0 else 2))
                for j in range(4):
                    ins("sp", "dma_start", out=xt[:, j, :], in_=x_t4[it * 4 + j], writes=[Rxt[j]], dma=s_xt[j])
                proj_residual(bufA, R_A4, wb)
                if it + 1 < NT:
                    aout_part1(it + 1)
                    for j in range(2):
                        g = (it + 1) * 4 + j
                        load_fs(g)
                        tok_norm_a(fsl[g % 2][:, :], R_fsl[g % 2], xsf[g % 2], Rxsf[g % 2], junk, NF)
                    load_fs((it + 1) * 4 + 2)
                    load_fs((it + 1) * 4 + 3)
                norm_pipe(4, lambda j: (xt[:, j, :], Rxt[j]), xs, Rxs, junk, bufB, R_B4)
                for m in range(KC):
                    wv, wr = WS.get(wb + 4 + m // 4)
                    pt, pr = next_ps()
                    for kc in range(KC):
                        ins("pe", "matmul", pt[:, :], wv[:, kc, (m % 4) * 128:(m % 4 + 1) * 128], bufB[:, kc, :],
                            start=(kc == 0), stop=(kc == KC - 1), reads=[wr, R_B4], writes=[pr])
                    evac(bufA[:, m, :], pt[:, :], pr, R_A4)
                if it + 1 < NT:
                    aout_part2(it + 1)
                for hc in range(NCH):
                    pi = hcnt[0] % 2
                    hcnt[0] += 1
                    for kk in range(2):
                        pt, pr = next_ps()
                        for dc in range(4):
                            ins("pe", "matmul", pt[:, :], ckT[:, hc * 4 + dc, kk * 128:(kk + 1) * 128], bufA[:, hc * 4 + dc, :],
                                start=(dc == 0), stop=(dc == 3), reads=[R_ck, R_A4], writes=[pr])
                        ins("act", "activation", out=pTc[pi][:, kk, :], in_=pt[:, :], func=AF.Exp, scale=c_scale,
                            reads=[pr], writes=[R_pTc[pi]])
                    pt, pr = next_ps()
                    for kk in range(2):
                        ins("pe", "matmul", pt[:, :], ones_b[:, :], pTc[pi][:, kk, :], start=(kk == 0), stop=(kk == 1),
                            reads=[R_const, R_pTc[pi]], writes=[pr])
                    ins("dve", "reciprocal", out=rlc[:, :], in_=pt[:, :], reads=[pr], writes=[R_rlc])
                    for dvc in range(4):
                        pt, pr = next_ps()
                        c0 = hc * CHD + dvc * 128
                        for kk in range(2):
                            ins("pe", "matmul", pt[:, :], cv[:, kk, c0:c0 + 128], pTc[pi][:, kk, :], start=(kk == 0), stop=(kk == 1),
                                reads=[R_cv, R_pTc[pi]], writes=[pr])
                        ins("dve", "tensor_tensor", out=bufB[:, hc * 4 + dvc, :], in0=pt[:, :], in1=rlc[:, :], op=ALU.mult,
                            reads=[pr, R_rlc], writes=[R_B4])
                proj_residual(bufB, R_B4, wb + 8)
                for j in range(4):
                    ins("pool", "dma_start", out=x2_t4[it * 4 + j], in_=xt[:, j, :], reads=[Rxt[j]], dma=s_xo[j])
            T.barrier()


        if phases >= 6:
          with ExitStack() as ph:
            wsl = [sb(ph, "w7_%d" % i, [128, 16, 512], BF16) for i in range(3)]
            Rw = [Res() for _ in range(3)]
            s_w = [T.new_sem("d_w7_%d" % i) for i in range(3)]
            gbh = sb(ph, "gbh", [128, FC, 32], F32)
            R_gbh = Res()
            xs = [sb(ph, "xs7_%d" % i, [128, D], BF16) for i in range(2)]
            Rxs = [Res() for _ in range(2)]
            junk = sb(ph, "junk7", [128, D], BF16)
            x2_t = X2.rearrange("(n p) d -> n p d", p=128)
            y_t = y_out.rearrange("(n p) d -> n p d", p=128)
            with ExitStack() as pre:
                xh = sb(pre, "xh", [32, D], F32)
                R_xh = Res()
                s_xh = T.new_sem("d_xh")
                hTh = sb(pre, "hTh", [128, KC, 32], BF16)
                R_hTh = Res()
                ins("dve", "memset", xh[:, :], 0.0, writes=[R_xh])
                if NT > 1:
                    X2v = X2.rearrange("(n t) d -> n t d", t=TT)
                    ins("sp", "dma_start", out=xh[0:NT - 1, :], in_=X2v[0:NT - 1, TT - 1, :], writes=[R_xh], dma=s_xh)
                    ins("sp", "dma_start", out=xh[16:16 + NT - 1, :], in_=X2v[1:NT, 0, :], writes=[R_xh], dma=s_xh)
                st_ap, Rs = next_stat()
                ins("act", "activation", out=junk[0:32, :], in_=xh[:, :], func=AF.Square, scale=float(D) ** -0.5,
                    accum_out=st_ap[0:32, 0:1], reads=[R_xh], writes=[Rs])
                ins("act", "activation", out=st_ap[0:32, 1:2], in_=st_ap[0:32, 0:1], func=AF.Sqrt, bias=eps_t[0:32, 0:1],
                    reads=[Rs, R_const], writes=[Rs])
                ins("dve", "reciprocal", out=st_ap[0:32, 2:3], in_=st_ap[0:32, 1:2], reads=[Rs], writes=[Rs])
                ins("dve", "tensor_scalar", out=xs[0][0:32, :], in0=xh[:, :], scalar1=st_ap[0:32, 2:3], scalar2=None, op0=ALU.mult,
                    reads=[R_xh, Rs], writes=[Rxs[0]])
                pt, pr = next_ps()
                ptb = pt.bitcast(BF16)
                for k in range(KC):
                    ins("pe", "transpose", out=ptb[:, k * 32:(k + 1) * 32], in_=xs[0][0:32, k * 128:(k + 1) * 128],
                        identity=ident_b[0:32, 0:32], reads=[Rxs[0], R_const], writes=[pr])
                ins("dve", "tensor_copy", out=hTh[:, :, :], in_=ptb[:, 0:KC * 32].rearrange("p (k c) -> p k c", k=KC),
                    reads=[pr], writes=[R_hTh])
                wspecs = [(WG[:, c * 512:(c + 1) * 512], 16, 512) for c in range(FC // 4)]
                WS = WStream(wsl, Rw, s_w, wspecs)
                for fc in range(FC):
                    wv, wr = WS.get(fc // 4)
                    pt, pr = next_ps()
                    for kc in range(KC):
                        ins("pe", "matmul", pt[:, 0:32], wv[:, kc, (fc % 4) * 128:(fc % 4 + 1) * 128], hTh[:, kc, :],
                            start=(kc == 0), stop=(kc == KC - 1), reads=[wr, R_hTh], writes=[pr])
                    ins("dve", "tensor_copy", out=gbh[:, fc, :], in_=pt[:, 0:32], reads=[pr], writes=[R_gbh])
                T.barrier()

            NXB = 7
            xb = [sb(ph, "xb7_%d" % i, [128, D], F32) for i in range(NXB)]
            Rxb = [Res() for _ in range(NXB)]
            s_xt = [T.new_sem("d_xt7_%d" % i) for i in range(NXB)]
            s_xo = [T.new_sem("d_xo7_%d" % i) for i in range(NXB)]
            x7_done = set()

            def load_x7(g):
                if g in x7_done or g >= 4 * NT:
                    return
                x7_done.add(g)
                ins("sp", "dma_start", out=xb[g % NXB][:, :], in_=x2_t[g], writes=[Rxb[g % NXB]], dma=s_xt[g % NXB])
            hT = sb(ph, "hT7", [128, KC, TT], BF16)
            R_hT = Res()
            aT = sb(ph, "aT", [128, FC, TT], BF16)
            R_aT = Res()
            NACC = 2
            acc = [sb(ph, "acc%d" % i, [128, TT], F32) for i in range(NACC)]
            R_acc = [Res() for _ in range(NACC)]
            sg = [sb(ph, "sg%d" % i, [128, TT], F32) for i in range(4)]
            R_sg = [Res() for _ in range(4)]
            hl = sb(ph, "hl", [128, 2, FC], F32)
            R_hl = Res()
            NQ = FC // 4
            wspecs = []
            for it in range(NT):
                for q in range(NQ):
                    wspecs.append((WG[:, q * 512:(q + 1) * 512], 16, 512))
                    wspecs.append((WU[:, q * 512:(q + 1) * 512], 16, 512))
                for n in range(4):
                    for kq in range(4):
                        wspecs.append((WD[kq * 1408:(kq + 1) * 1408, n * 512:(n + 1) * 512], 11, 512))
            WS = WStream(wsl, Rw, s_w, wspecs)
            WPT = 2 * NQ + 16
            fcnt = [0]
            for it in range(NT):
                wb = it * WPT
                if it == 0:
                    load_x7(0)
                    load_x7(1)
                norm_pipe(4, lambda j, it=it: (xb[(it * 4 + j) % NXB][:, :], Rxb[(it * 4 + j) % NXB]), xs, Rxs, junk, hT, R_hT,
                          pre_load=lambda j, it=it: load_x7(it * 4 + j), done_a=(0 if it == 0 else 2))
                if it > 0:
                    ins("dve", "tensor_tensor", out=hl[:, 0, :], in0=cwv[:, 0, :], in1=gbh[:, :, it - 1], op=ALU.mult,
                        reads=[R_const, R_gbh], writes=[R_hl])
                if it < NT - 1:
                    ins("dve", "tensor_tensor", out=hl[:, 1, :], in0=cwv[:, 2, :], in1=gbh[:, :, 16 + it], op=ALU.mult,
                        reads=[R_const, R_gbh], writes=[R_hl])
                for q in range(NQ):
                    wg, wgr = WS.get(wb + 2 * q)
                    for c in range(4):
                        fc = 4 * q + c
                        ai = fcnt[0] % NACC
                        fcnt[0] += 1
                        pg, pgr = next_ps()
                        for kc in range(KC):
                            ins("pe", "matmul", pg[:, :], wg[:, kc, c * 128:(c + 1) * 128], hT[:, kc, :],
                                start=(kc == 0), stop=(kc == KC - 1), reads=[wgr, R_hT], writes=[pgr])
                        a_ = acc[ai]
                        Ra = R_acc[ai]
                        ins("act", "activation", out=a_[:, :], in_=pg[:, :], func=AF.Identity, scale=cwv[:, 1, fc:fc + 1],
                            bias=cwv[:, 3, fc:fc + 1], reads=[pgr, R_const], writes=[Ra])
                        ins("dve", "scalar_tensor_tensor", out=a_[:, 1:TT], in0=pg[:, 0:TT - 1], scalar=cwv[:, 0, fc:fc + 1],
                            in1=a_[:, 1:TT], op0=ALU.mult, op1=ALU.add, reads=[pgr, Ra, R_const], writes=[Ra])
                        ins("dve", "scalar_tensor_tensor", out=a_[:, 0:TT - 1], in0=pg[:, 1:TT], scalar=cwv[:, 2, fc:fc + 1],
                            in1=a_[:, 0:TT - 1], op0=ALU.mult, op1=ALU.add, reads=[pgr, Ra, R_const], writes=[Ra])
                        if it > 0:
                            ins("pool", "tensor_tensor", out=a_[:, 0:1], in0=a_[:, 0:1], in1=hl[:, 0, fc:fc + 1], op=ALU.add,
                                reads=[Ra, R_hl], writes=[Ra])
                        if it < NT - 1:
                            ins("pool", "tensor_tensor", out=a_[:, TT - 1:TT], in0=a_[:, TT - 1:TT], in1=hl[:, 1, fc:fc + 1], op=ALU.add,
                                reads=[Ra, R_hl], writes=[Ra])
                        ins("act", "activation", out=sg[c][:, :], in_=a_[:, :], func=AF.Silu, reads=[Ra], writes=[R_sg[c]])
                    wu, wur = WS.get(wb + 2 * q + 1)
                    for c in range(4):
                        fc = 4 * q + c
                        pu, pur = next_ps()
                        for kc in range(KC):
                            ins("pe", "matmul", pu[:, :], wu[:, kc, c * 128:(c + 1) * 128], hT[:, kc, :],
                                start=(kc == 0), stop=(kc == KC - 1), reads=[wur, R_hT], writes=[pur])
                        ins("dve", "tensor_tensor", out=aT[:, fc, :], in0=pu[:, :], in1=sg[c][:, :], op=ALU.mult,
                            reads=[pur, R_sg[c]], writes=[R_aT])
                if it + 1 < NT:
                    for j in range(2):
                        g = (it + 1) * 4 + j
                        load_x7(g)
                        tok_norm_a(xb[g % NXB][:, :], Rxb[g % NXB], xs[g % 2], Rxs[g % 2], junk)
                for n in range(4):
                    banks = [next_ps() for _ in range(4)]
                    for kq in range(4):
                        wv, wr = WS.get(wb + 2 * NQ + n * 4 + kq)
                        for j in range(4):
                            pt, pr = banks[j]
                            for kc in range(11):
                                ins("pe", "matmul", pt[:, :], aT[:, kq * 11 + kc, j * 128:(j + 1) * 128], wv[:, kc, :],
                                    start=(kq == 0 and kc == 0), stop=(kq == 3 and kc == 10), reads=[wr, R_aT], writes=[pr])
                    for j in range(4):
                        pt, pr = banks[j]
                        bi = (it * 4 + j) % NXB
                        ins("dve", "tensor_tensor", out=xb[bi][:, n * 512:(n + 1) * 512], in0=pt[:, :],
                            in1=xb[bi][:, n * 512:(n + 1) * 512], op=ALU.add, reads=[pr, Rxb[bi]], writes=[Rxb[bi]])
                for j in range(4):
                    bi = (it * 4 + j) % NXB
                    st_ap, Rs = next_stat()
                    ins("act", "activation", out=junk[:, :], in_=xb[bi][:, :], func=AF.Square, scale=float(D) ** -0.5,
                        accum_out=st_ap[:, 0:1], reads=[Rxb[bi]], writes=[Rs])
                    ins("act", "activation", out=st_ap[:, 1:2], in_=st_ap[:, 0:1], func=AF.Sqrt, bias=eps_t[:, 0:1],
                        reads=[Rs, R_const], writes=[Rs])
                    ins("dve", "reciprocal", out=st_ap[:, 2:3], in_=st_ap[:, 1:2], reads=[Rs], writes=[Rs])
                    ins("dve", "scalar_tensor_tensor", out=xb[bi][:, :], in0=xb[bi][:, :], scalar=st_ap[:, 2:3], in1=gfin[:, :],
                        op0=ALU.mult, op1=ALU.mult, reads=[Rxb[bi], Rs, R_const], writes=[Rxb[bi]])
                    ins("pool", "dma_start", out=y_t[it * 4 + j], in_=xb[bi][:, :], reads=[Rxb[bi]], dma=s_xo[bi])
            T.barrier()


        T.barrier()
        with nc.Block() as block:
            @block.tensor
            def _(e):
                T.replay(e, "pe")

            @block.scalar
            def _(e):
                T.replay(e, "act")

            @block.vector
            def _(e):
                T.replay(e, "dve")

            @block.gpsimd
            def _(e):
                T.replay(e, "pool")

            @block.sync
            def _(e):
                T.replay(e, "sp")
    return nc


S_FULL = 8192
_NC_CACHE = {}


def kernel(x_prompt, x_sample, mem_prompt, mem_sample, **weights):
    xs = [np.asarray(x_prompt[i]) for i in range(x_prompt.shape[0])] + \
         [np.asarray(x_sample[i]) for i in range(x_sample.shape[0])]
    ms = [np.asarray(mem_prompt[i]) for i in range(mem_prompt.shape[0])] + \
         [np.asarray(mem_sample[i]) for i in range(mem_sample.shape[0])]
    nseq = len(xs)
    S = xs[0].shape[0]
    if S not in _NC_CACHE:
        _NC_CACHE[S] = build(S)
    nc = _NC_CACHE[S]
    shared = {}
    for name, shp in WEIGHT_SPECS:
        shared[name] = np.ascontiguousarray(np.asarray(weights[name], dtype=np.float32).reshape(shp))
    shared.update(host_constants(S))
    in_maps = []
    core_of_seq = [0, 1, 2, 4, 5, 6]
    seq_of_core = {c: i for i, c in enumerate(core_of_seq)}
    zx = np.zeros_like(np.ascontiguousarray(xs[0], dtype=np.float32))
    zm = np.zeros_like(np.ascontiguousarray(ms[0], dtype=np.float32))
    for c in range(8):
        m = dict(shared)
        if c in seq_of_core:
            m["x"] = np.ascontiguousarray(xs[seq_of_core[c]], dtype=np.float32)
            m["mem"] = np.ascontiguousarray(ms[seq_of_core[c]], dtype=np.float32)
        else:
            m["x"] = zx
            m["mem"] = zm
        in_maps.append(m)
    res = run_bass_kernel_spmd(nc, in_maps, core_ids=list(range(8)))
    ys = [np.asarray(res.results[core_of_seq[i]]["y"], dtype=np.float32) for i in range(nseq)]
    nb = x_prompt.shape[0]
    y_prompt = np.stack(ys[:nb], axis=0)
    y_sample = np.stack(ys[nb:], axis=0)
    return (y_prompt, y_sample)
```
